# Optimizing a Trainium2 kernel written in Bass

```python
import jax, jax.numpy as jnp
from jax import lax
import numpy as np

D_MODEL = 2048
BATCH = 8
SEQ = 4096
DEPTH = 1
DEC_BATCH = 16
DEC_SEQ = 32
PAST_LEN = 2048

CHUNK = 64
Q_BLOCK = 128
PLE_DIM = 256
MLA_HEADS = 8
MLA_Q_RANK = 512
MLA_KV_RANK = 512
MLA_NOPE = 128
MLA_ROPE = 64
MLA_V = 128
MLA_SCALE = (MLA_NOPE + MLA_ROPE) ** -0.5
ROPE_THETA = 10000.0
FOX_HEADS = 8
FOX_HEAD_DIM = 128
FOX_WIDTH = FOX_HEADS * FOX_HEAD_DIM
FOX_SCALE = FOX_HEAD_DIM ** -0.5
D_FF = 5632
CONV_W = 3
EPS = 1e-6
NEG_INF = -1e30

IN_SPLITS = (MLA_Q_RANK, MLA_KV_RANK, MLA_ROPE, FOX_WIDTH, FOX_WIDTH, FOX_WIDTH,
             FOX_HEADS, D_MODEL, D_MODEL)
IN_WIDTH = sum(IN_SPLITS)

kernel_name = "streaming_mla_fox_gated_hybrid"


def rmsnorm(x, g):
    xf = x.astype(jnp.float32)
    y = xf * lax.rsqrt(jnp.mean(xf * xf, axis=-1, keepdims=True) + EPS)
    return (y * g.astype(jnp.float32)).astype(x.dtype)


def rope(x, pos):
    half = x.shape[-1] // 2
    inv = ROPE_THETA ** (-jnp.arange(half, dtype=jnp.float32) / half)
    ang = pos.astype(jnp.float32)[:, None] * inv[None, :]
    ang = ang.reshape((ang.shape[0],) + (1,) * (x.ndim - 3) + (half,))
    cos, sin = jnp.cos(ang), jnp.sin(ang)
    xf = x.astype(jnp.float32)
    x1, x2 = xf[..., :half], xf[..., half:]
    return jnp.concatenate([x1 * cos - x2 * sin, x2 * cos + x1 * sin], axis=-1).astype(x.dtype)


def split_cols(z, widths):
    outs, o = [], 0
    for w in widths:
        outs.append(z[..., o:o + w])
        o += w
    return outs


def sweep_query_blocks(fn, q_args, qpos):
    b, t = q_args[0].shape[:2]
    nb = t // Q_BLOCK
    xs = tuple(jnp.swapaxes(a.reshape((b, nb, Q_BLOCK) + a.shape[2:]), 0, 1) for a in q_args)
    out = lax.map(lambda blk: fn(*blk[0], blk[1]), (xs, qpos.reshape(nb, Q_BLOCK)))
    return jnp.swapaxes(out, 0, 1).reshape((b, t) + out.shape[3:])


def layer(x, pe, past, w):
    b, t, _ = x.shape
    p_len = 0 if past is None else past[0].shape[1]
    qpos = p_len + jnp.arange(t)

    a = rmsnorm(x, w["g_mix"])
    cq, ckv, kr, fq, fk, fv, fl, ga, gb = split_cols(a @ w["w_in"], IN_SPLITS)
    q = (rmsnorm(cq, w["g_q"]) @ w["w_uq"]).reshape(b, t, MLA_HEADS, MLA_NOPE + MLA_ROPE)
    q_nope, q_rope = q[..., :MLA_NOPE], rope(q[..., MLA_NOPE:], qpos)
    ckv = rmsnorm(ckv, w["g_kv"])
    kr = rope(kr, qpos)
    fq = fq.reshape(b, t, FOX_HEADS, FOX_HEAD_DIM)
    fk = fk.reshape(b, t, FOX_HEADS, FOX_HEAD_DIM)
    fv = fv.reshape(b, t, FOX_HEADS, FOX_HEAD_DIM)
    logf = jax.nn.log_sigmoid((fl + w["b_f"]).astype(jnp.float32))

    if past is None:
        ckv_all, kr_all, fk_all, fv_all, logf_all = ckv, kr, fk, fv, logf
    else:
        c_ckv, c_kr, c_fk, c_fv, c_logf, _ = past
        ckv_all = jnp.concatenate([c_ckv.astype(ckv.dtype), ckv], axis=1)
        kr_all = jnp.concatenate([c_kr.astype(kr.dtype), kr], axis=1)
        fk_all = jnp.concatenate([c_fk.astype(fk.dtype), fk], axis=1)
        fv_all = jnp.concatenate([c_fv.astype(fv.dtype), fv], axis=1)
        logf_all = jnp.concatenate([c_logf.astype(jnp.float32), logf], axis=1)
    s_len = p_len + t
    kpos = jnp.arange(s_len)

    kv = (ckv_all @ w["w_ukv"]).reshape(b, s_len, MLA_HEADS, MLA_NOPE + MLA_V)
    k_nope, v_mla = kv[..., :MLA_NOPE], kv[..., MLA_NOPE:]
    kchunk = kpos // CHUNK

    def mla_fn(qn, qr, qp):
        s = (jnp.einsum('bqhd,bkhd->bhqk', qn, k_nope, preferred_element_type=jnp.float32)
             + jnp.einsum('bqhr,bkr->bhqk', qr, kr_all, preferred_element_type=jnp.float32)) * MLA_SCALE
        vis = kchunk[None, :] <= (qp // CHUNK)[:, None]
        s = jnp.where(vis, s, NEG_INF)
        pr = jax.nn.softmax(s, axis=-1).astype(v_mla.dtype)
        return jnp.einsum('bhqk,bkhd->bqhd', pr, v_mla)

    cum = jnp.cumsum(logf_all, axis=1)
    cum_q = cum[:, p_len:]
    cum_k = jnp.swapaxes(cum, 1, 2)[:, :, None, :]

    def fox_fn(qq, cq_, qp):
        s = (jnp.einsum('bqhd,bkhd->bhqk', qq, fk_all, preferred_element_type=jnp.float32) * FOX_SCALE
             + jnp.swapaxes(cq_, 1, 2)[..., None] - cum_k)
        vis = kpos[None, :] <= qp[:, None]
        s = jnp.where(vis, s, NEG_INF)
        pr = jax.nn.softmax(s, axis=-1).astype(fv_all.dtype)
        return jnp.einsum('bhqk,bkhd->bqhd', pr, fv_all)

    if past is None:
        o_a = sweep_query_blocks(mla_fn, (q_nope, q_rope), qpos)
        o_b = sweep_query_blocks(fox_fn, (fq, cum_q), qpos)
    else:
        o_a = mla_fn(q_nope, q_rope, qpos)
        o_b = fox_fn(fq, cum_q, qpos)

    merged = (jax.nn.sigmoid(ga) * (o_a.reshape(b, t, MLA_HEADS * MLA_V) @ w["w_oa"])
              + jax.nn.sigmoid(gb) * (o_b.reshape(b, t, FOX_WIDTH) @ w["w_ob"]))
    h = x + merged @ w["w_o"]

    u = rmsnorm(h, w["g_ffn"]) @ w["w_up"]
    if past is None:
        prev = jnp.zeros((b, CONV_W - 1, 2 * D_FF), u.dtype)
    else:
        prev = past[5].astype(u.dtype)
    u_pad = jnp.concatenate([prev, u], axis=1)
    uc = w["b_conv"] + sum(u_pad[:, k:k + t] * w["w_conv"][k] for k in range(CONV_W))
    val, gate = uc[..., :D_FF], uc[..., D_FF:]
    h = h + (jax.nn.gelu(gate) * val) @ w["w_down"]

    h = h + jax.nn.sigmoid(rmsnorm(h, w["g_ple"]) @ w["w_pg"]) * (pe @ w["w_ple"])

    new_state = (ckv, kr, fk, fv, logf, u_pad[:, -(CONV_W - 1):])
    return h, new_state


def setup_inputs(seed: int = 0) -> dict:
    key = jax.random.key(seed)
    ks = list(jax.random.split(key, 40))

    def nrm(k, shape, scale=1.0):
        return scale * jax.random.normal(k, shape, jnp.float32)

    def gain(k, n):
        return 1.0 + 0.1 * jax.random.normal(k, (DEPTH, n), jnp.float32)

    L = DEPTH
    return {
        "x_prompt": nrm(ks[0], (BATCH, SEQ, D_MODEL)),
        "x_sample": nrm(ks[1], (DEC_BATCH, DEC_SEQ, D_MODEL)),
        "cache_mla_ckv": nrm(ks[2], (L, DEC_BATCH, PAST_LEN, MLA_KV_RANK)),
        "cache_mla_krope": nrm(ks[3], (L, DEC_BATCH, PAST_LEN, MLA_ROPE)),
        "cache_fox_k": nrm(ks[4], (L, DEC_BATCH, PAST_LEN, FOX_HEADS, FOX_HEAD_DIM)),
        "cache_fox_v": nrm(ks[5], (L, DEC_BATCH, PAST_LEN, FOX_HEADS, FOX_HEAD_DIM)),
        "cache_fox_logf": jax.nn.log_sigmoid(2.0 + nrm(ks[6], (L, DEC_BATCH, PAST_LEN, FOX_HEADS), 0.5)),
        "state_ffn_conv": nrm(ks[7], (L, DEC_BATCH, CONV_W - 1, 2 * D_FF)),
        "p_prompt": nrm(ks[8], (L, BATCH, SEQ, PLE_DIM)),
        "p_sample": nrm(ks[9], (L, DEC_BATCH, DEC_SEQ, PLE_DIM)),
        "g_mix": gain(ks[10], D_MODEL),
        "w_in": nrm(ks[11], (L, D_MODEL, IN_WIDTH), D_MODEL ** -0.5),
        "b_f": 2.0 + nrm(ks[12], (L, FOX_HEADS), 0.5),
        "g_q": gain(ks[13], MLA_Q_RANK),
        "w_uq": nrm(ks[14], (L, MLA_Q_RANK, MLA_HEADS * (MLA_NOPE + MLA_ROPE)), MLA_Q_RANK ** -0.5),
        "g_kv": gain(ks[15], MLA_KV_RANK),
        "w_ukv": nrm(ks[16], (L, MLA_KV_RANK, MLA_HEADS * (MLA_NOPE + MLA_V)), MLA_KV_RANK ** -0.5),
        "w_oa": nrm(ks[17], (L, MLA_HEADS * MLA_V, D_MODEL), (MLA_HEADS * MLA_V) ** -0.5),
        "w_ob": nrm(ks[18], (L, FOX_WIDTH, D_MODEL), FOX_WIDTH ** -0.5),
        "w_o": nrm(ks[19], (L, D_MODEL, D_MODEL), D_MODEL ** -0.5),
        "g_ffn": gain(ks[20], D_MODEL),
        "w_up": nrm(ks[21], (L, D_MODEL, 2 * D_FF), D_MODEL ** -0.5),
        "w_conv": nrm(ks[22], (L, CONV_W, 2 * D_FF), CONV_W ** -0.5),
        "b_conv": nrm(ks[23], (L, 2 * D_FF), 0.01),
        "w_down": nrm(ks[24], (L, D_FF, D_MODEL), D_FF ** -0.5),
        "g_ple": gain(ks[25], D_MODEL),
        "w_pg": nrm(ks[26], (L, D_MODEL, D_MODEL), D_MODEL ** -0.5),
        "w_ple": nrm(ks[27], (L, PLE_DIM, D_MODEL), PLE_DIM ** -0.5),
        "g_final": 1.0 + 0.1 * jax.random.normal(ks[28], (D_MODEL,), jnp.float32),
    }


def reference(x_prompt, x_sample, cache_mla_ckv, cache_mla_krope, cache_fox_k, cache_fox_v,
              cache_fox_logf, state_ffn_conv, p_prompt, p_sample, g_mix, w_in, b_f, g_q, w_uq,
              g_kv, w_ukv, w_oa, w_ob, w_o, g_ffn, w_up, w_conv, b_conv, w_down, g_ple, w_pg,
              w_ple, g_final):
    def layer_weights(i):
        return {"g_mix": g_mix[i], "w_in": w_in[i], "b_f": b_f[i], "g_q": g_q[i], "w_uq": w_uq[i],
                "g_kv": g_kv[i], "w_ukv": w_ukv[i], "w_oa": w_oa[i], "w_ob": w_ob[i], "w_o": w_o[i],
                "g_ffn": g_ffn[i], "w_up": w_up[i], "w_conv": w_conv[i], "b_conv": b_conv[i],
                "w_down": w_down[i], "g_ple": g_ple[i], "w_pg": w_pg[i], "w_ple": w_ple[i]}

    h_p, st_p = x_prompt, []
    for i in range(DEPTH):
        h_p, st = layer(h_p, p_prompt[i], None, layer_weights(i))
        st_p.append(st)
    y_prompt = rmsnorm(h_p, g_final)

    h_s, st_s = x_sample, []
    for i in range(DEPTH):
        past = (cache_mla_ckv[i], cache_mla_krope[i], cache_fox_k[i], cache_fox_v[i],
                cache_fox_logf[i], state_ffn_conv[i])
        h_s, st = layer(h_s, p_sample[i], past, layer_weights(i))
        st_s.append(st)
    y_sample = rmsnorm(h_s, g_final)

    def stk(states, j):
        return jnp.stack([s[j] for s in states], axis=0)

    return (y_prompt, y_sample,
            stk(st_p, 0), stk(st_s, 0),
            stk(st_p, 1), stk(st_s, 1),
            stk(st_p, 2), stk(st_s, 2),
            stk(st_p, 3), stk(st_s, 3),
            stk(st_p, 4), stk(st_s, 4),
            stk(st_p, 5), stk(st_s, 5))
```

```python
import numpy as np
from contextlib import ExitStack
import concourse.bass as bass
import concourse.mybir as mybir
from concourse.bass_utils import run_bass_kernel_spmd

F32 = mybir.dt.float32
BF16 = mybir.dt.bfloat16
AF = mybir.ActivationFunctionType
ALU = mybir.AluOpType

D = 2048
FF = 5632
NFT = 88
EPS = 1e-6
MLA_SCALE = 192.0 ** -0.5
FOX_SCALE = 128.0 ** -0.5
TS = 32
ENGS = ("pe", "act", "dve", "pool", "sp")
SAME_ENGINE_SYNC = True


class Buf:
    def __init__(self, t, name, acc=False):
        self.t = t
        self.name = name
        self.lw = {}
        self.rd = {}
        self.al = []
        self.acc = acc
        self.ld = None
        self.st = None
        self.psum = False

    def __getitem__(self, idx):
        return self.t[idx]


class SemOwner:
    def __init__(self, sem):
        self.sem = sem
        self.cnt = 0


class Prog:
    def __init__(self, nc, es):
        self.nc = nc
        self.es = es
        self.q = {e: [] for e in ENGS}
        self.cnt = {e: 0 for e in ENGS}
        self.sem = {e: es.enter_context(nc.semaphore("sem_" + e)) for e in ENGS}
        self.seen = {e: {} for e in ENGS}
        self.nsem = len(ENGS)
        self.owners = []
        self.nbuf = 0

    def owner(self):
        self.nsem += 1
        o = SemOwner(self.es.enter_context(self.nc.semaphore("dsem%d" % self.nsem)))
        self.owners.append(o)
        return o

    def sb(self, name, shape, dt):
        t = self.es.enter_context(self.nc.sbuf_tensor(name, list(shape), dt))
        return Buf(t, name)

    def psum(self, name, shape, dt):
        t = self.es.enter_context(self.nc.psum_tensor(name, list(shape), dt))
        return Buf(t, name)

    def view(self, base, name):
        return Buf(base.t, name)

    def _collect(self, rd, wr, eng=None):
        toks = {}

        def add(d):
            for k, (s, v) in d.items():
                if k not in toks or toks[k][1] < v:
                    toks[k] = (s, v)
        for b in rd:
            add(b.lw)
            if b.psum:
                add({k: v for k, v in b.rd.items() if k != eng})
        for b in wr:
            if b.acc:
                continue
            add(b.lw)
            add(b.rd)
            for a in b.al:
                add(a.lw)
                add(a.rd)
        return toks

    def _waits(self, eng, toks):
        waits = []
        seen = self.seen[eng]
        for k, (s, v) in toks.items():
            if k == eng and (eng == "pe" or not SAME_ENGINE_SYNC):
                continue
            if seen.get(k, 0) < v:
                seen[k] = v
                waits.append((s, v))
        return waits

    def _commit(self, tok_key, tok, rd, wr):
        for b in wr:
            if b.acc:
                b.lw[tok_key] = tok
            else:
                b.lw = {tok_key: tok}
                b.rd = {}
        for b in rd:
            b.rd[tok_key] = tok

    def op(self, eng, name, rd, wr, *args, **kw):
        waits = self._waits(eng, self._collect(rd, wr, eng))
        self.cnt[eng] += 1
        v = self.cnt[eng]
        sem = self.sem[eng]

        def emit(e, name=name, args=args, kw=kw, waits=waits, sem=sem):
            for (s, val) in waits:
                e.wait_ge(s, val)
            getattr(e, name)(*args, **kw).then_inc(sem, 1)
        self.q[eng].append(emit)
        self._commit(eng, (sem, v), rd, wr)

    def dma(self, eng, out, in_, owner, rd=(), wr=(), **kw):
        waits = self._waits(eng, self._collect(rd, wr, eng))
        owner.cnt += 16
        v = owner.cnt
        sem = owner.sem

        def emit(e, out=out, in_=in_, waits=waits, sem=sem, kw=kw):
            for (s, val) in waits:
                e.wait_ge(s, val)
            e.dma_start(out=out, in_=in_, **kw).then_inc(sem, 16)
        self.q[eng].append(emit)
        self._commit(id(owner), (sem, v), rd, wr)

    def finish(self):
        finals = [(o.sem, o.cnt) for o in self.owners if o.cnt > 0]

        def emit(e, finals=finals):
            for (s, v) in finals:
                e.wait_ge(s, v)
        self.q["sp"].append(emit)
        ecnt = [(self.sem[k], self.cnt[k]) for k in ("pe", "act", "dve", "pool") if self.cnt[k] > 0]

        def emit2(e, ecnt=ecnt):
            for (s, v) in ecnt:
                e.wait_ge(s, v)
        self.q["sp"].append(emit2)
        nc = self.nc
        with nc.Block() as block:
            @block.tensor
            def _(e):
                for c in self.q["pe"]:
                    c(e)

            @block.scalar
            def _(e):
                for c in self.q["act"]:
                    c(e)

            @block.vector
            def _(e):
                for c in self.q["dve"]:
                    c(e)

            @block.gpsimd
            def _(e):
                for c in self.q["pool"]:
                    c(e)

            @block.sync
            def _(e):
                for c in self.q["sp"]:
                    c(e)


def slab_table():
    S = []
    idx = {}

    def add(key, w, nkc, r0, c0, width, grp):
        idx[key] = len(S)
        S.append((w, nkc, r0, c0, width, grp))
    for i in range(10):
        add(("in_fm", i), "w_in_p", 16, 0, i * 256, 256, 0)
    add(("in_kr", 0), "w_in_p", 16, 0, 2560, 128, 0)
    for i in range(16):
        add(("in_g", i), "w_in_p", 16, 0, 2688 + i * 256, 256, 1)
    for i in range(10):
        add(("in_tm", i), "w_in_p", 16, 0, 6784 + i * 256, 256, 2)
    add(("in_fl", 0), "w_in_p", 16, 0, 9344, 8, 2)
    for i in range(2):
        add(("uq", i), "w_uq_p", 4, 0, i * 1024, 1024, 3)
    for i in range(2):
        add(("ukv", i), "w_ukv_p", 4, 0, i * 1024, 1024, 3)
    for i in range(4):
        add(("oa", i), "w_oa", 8, 0, i * 512, 512, 4)
        add(("ob", i), "w_ob", 8, 0, i * 512, 512, 4)
    for i in range(8):
        add(("o", i), "w_o", 16, 0, i * 256, 256, 5)
    for g in range(22):
        add(("upv", g), "w_up", 16, 0, g * 256, 256, 6 + g // 6)
        add(("upg", g), "w_up", 16, 0, FF + g * 256, 256, 6 + g // 6)
        add(("dn", g), "w_down", 2, g * 256, 0, 2048, 6 + g // 6)
    for i in range(8):
        add(("pg", i), "w_pg", 16, 0, i * 256, 256, 10)
    add(("ple", 0), "w_ple", 2, 0, 0, 2048, 10)
    return S, idx


NCP = 9352


def win_perm_cols():
    cq = np.arange(0, 512)
    ckv = np.arange(512, 1024)
    kr = np.arange(1024, 1088)
    fq = np.arange(1088, 2112)
    fk = np.arange(2112, 3136)
    fv = np.arange(3136, 4160)
    fl = np.arange(4160, 4168)
    ga = np.arange(4168, 6216)
    gb = np.arange(6216, 8264)
    kr_sw = np.concatenate([kr[32:], kr[:32]])
    cols = np.concatenate([cq, fq, fk, kr, kr_sw, ga, gb, ckv, fk, fv, fl])
    assert cols.shape[0] == NCP
    return cols


def build(TP, PAST, NSS):
    nc = bass.Bass("TRN2", target_bir_lowering=False)
    CAPP = max(TP, 128)
    CAPS = ((PAST + TS + 127) // 128) * 128
    NKTMAX = max(TP // 128, NSS * (CAPS // 128))

    def din(name, shape, dt=F32):
        return nc.dram_tensor(name, list(shape), dt, kind="ExternalInput").ap()

    def dout(name, shape, dt=F32):
        return nc.dram_tensor(name, list(shape), dt, kind="ExternalOutput").ap()

    def dint(name, shape, dt=BF16):
        return nc.dram_tensor(name, list(shape), dt, kind="Internal").ap()

    I = {}
    I["xp"] = din("xp", [TP, D])
    I["pp"] = din("pp", [TP, 256])
    I["xs"] = din("xs", [NSS, TS, D])
    I["ps"] = din("ps", [NSS, TS, 256])
    I["c_ckv"] = din("c_ckv", [NSS, PAST, 512])
    I["c_kr"] = din("c_kr", [NSS, PAST, 64])
    I["c_fk"] = din("c_fk", [NSS, PAST, 1024])
    I["c_fv"] = din("c_fv", [NSS, PAST, 1024])
    I["c_lf"] = din("c_lf", [NSS, PAST, 8])
    I["c_conv"] = din("c_conv", [NSS, 2, 2 * FF])
    WSRC = {
        "w_in_p": din("w_in_p", [D, NCP]), "w_uq_p": din("w_uq_p", [512, 2048]),
        "w_ukv_p": din("w_ukv_p", [512, 2048]), "w_oa": din("w_oa", [1024, D]),
        "w_ob": din("w_ob", [1024, D]), "w_o": din("w_o", [D, D]), "w_up": din("w_up", [D, 2 * FF]),
        "w_down": din("w_down", [FF, D]), "w_pg": din("w_pg", [D, D]), "w_ple": din("w_ple", [256, D]),
    }
    for nm, shp in (("gT_mix", [128, 16]), ("gT_ffn", [128, 16]), ("gT_ple", [128, 16]), ("gqT", [128, 4]),
                    ("gkv_bc", [128, 512]), ("gfin_bc", [128, D]), ("bf_bc", [128, 8]),
                    ("wcT", [128, NFT * 3]), ("bcT", [128, NFT]), ("ident", [128, 128]), ("tri", [128, 128]),
                    ("ropeC_p", [64, TP]), ("ropeS_p", [64, TP]), ("ropeC_s", [64, NSS * TS]), ("ropeS_s", [64, NSS * TS])):
        I[nm] = din(nm, shp)
    O = {}
    O["y_p"] = dout("y_p", [TP, D])
    O["y_s"] = dout("y_s", [NSS, TS, D])
    O["ckv_p"] = dout("ckv_p", [TP, 512])
    O["ckv_s"] = dout("ckv_s", [NSS, TS, 512])
    O["kr_p"] = dout("kr_p", [TP, 64])
    O["kr_s"] = dout("kr_s", [NSS, TS, 64])
    O["fk_p"] = dout("fk_p", [TP, 1024])
    O["fk_s"] = dout("fk_s", [NSS, TS, 1024])
    O["fv_p"] = dout("fv_p", [TP, 1024])
    O["fv_s"] = dout("fv_s", [NSS, TS, 1024])
    O["lf_p"] = dout("lf_p", [TP, 8])
    O["lf_s"] = dout("lf_s", [NSS, TS, 8])
    O["conv_p"] = dout("conv_p", [2, 2 * FF])
    O["conv_s"] = dout("conv_s", [NSS, 2, 2 * FF])

    SLABS, SIDX = slab_table()
    WS = dint("ws", [len(SLABS), 128, 4096])
    SGD = dint("sgd", [2, 16, 128, 512])

    es = ExitStack()
    with es:
        P = Prog(nc, es)
        op, dma = P.op, P.dma

        def mk(t, name, ld=False, st=False):
            b = Buf(t, name)
            if ld:
                b.ld = P.owner()
            if st:
                b.st = P.owner()
            return b

        def sbd(name, shape, dt, ld=False, st=False):
            t = es.enter_context(nc.sbuf_tensor("sb_" + name, list(shape), dt))
            return mk(t, name, ld, st)

        ident32 = sbd("ident32", [128, 128], F32, ld=True)
        identb = sbd("identb", [128, 128], BF16)
        ones32 = sbd("ones32", [128, 128], F32)
        onesb = sbd("onesb", [128, 128], BF16)
        tri32 = sbd("tri32", [128, 128], F32, ld=True)
        trib = sbd("trib", [128, 128], BF16)
        gT = {k: sbd(k, [128, 16], F32, ld=True) for k in ("gT_mix", "gT_ffn", "gT_ple")}
        gqT = sbd("gqT", [128, 4], F32, ld=True)
        gkv_bc = sbd("gkv_bc", [128, 512], F32, ld=True)
        bf_bc = sbd("bf_bc", [128, 8], F32, ld=True)
        wcT = sbd("wcT", [128, NFT, 3], F32, ld=True)
        bcT = sbd("bcT", [128, NFT], F32, ld=True)
        ropeC = sbd("ropeC", [64, 512], F32, ld=True)
        ropeS = sbd("ropeS", [64, 512], F32, ld=True)
        arena = es.enter_context(nc.sbuf_tensor("arena", [128, 8192], F32))
        H = [mk(arena[:, i * 2048:(i + 1) * 2048], "H%d" % i, ld=True, st=True) for i in range(4)]
        qviews = [mk(arena[:, i * 2048:(i + 1) * 2048].bitcast(BF16).rearrange("p (a b) -> p a b", a=8), nm)
                  for i, nm in enumerate(("QN", "QF", "OA", "OB"))]
        QN, QF, OA, OB = qviews
        for i in range(4):
            H[i].al = [qviews[i]]
            qviews[i].al = [H[i]]
        QR = sbd("QR", [64, 8, 512], BF16)
        G = [sbd("G%d" % i, [128, 2, 512], BF16) for i in range(2)]
        AB = [sbd("Abuf%d" % i, [128, 16, 512], BF16) for i in range(2)]
        XS = sbd("XS", [128, D], F32, ld=True)
        CB = mk(XS.t[:, :].bitcast(BF16), "CB", ld=True)
        CB.al = [XS]
        XS.al = [CB]
        XNS = [sbd("XN%d" % i, [128, D], BF16) for i in range(2)]
        XN = XNS[0]
        NWB = 5
        WB = [sbd("WB%d" % i, [128, 4096], BF16, ld=True) for i in range(NWB)]
        NT32 = 8
        T32 = [sbd("T32_%d" % i, [128, 512], F32, ld=True, st=True) for i in range(NT32)]
        NTB = 8
        TB16 = [sbd("TB16_%d" % i, [128, 512], BF16, ld=True, st=True) for i in range(NTB)]
        CQH = [sbd("CQH%d" % i, [128, 512], F32) for i in range(2)]
        CQN = sbd("CQN", [128, 4, 512], BF16)
        CKVT = sbd("CKVT", [128, 4, 512], BF16)
        KB = [sbd("KB%d" % i, [128, 1024], BF16, ld=True) for i in range(2)]
        VB = [sbd("VB%d" % i, [128, 8, 256], BF16, ld=True) for i in range(2)]
        CQ32 = []
        for i in range(4):
            v = VB[i // 2]
            c = mk(v.t[:, :, :].rearrange("p a b -> p (a b)")[:, (i % 2) * 1024:(i % 2 + 1) * 1024].bitcast(F32),
                   "CQ32_%d" % i)
            c.al = [v]
            v.al = v.al + [c]
            CQ32.append(c)
        KRB = [sbd("KRB%d" % i, [64, 1024], BF16, ld=True) for i in range(2)]
        U = [sbd("U%d" % i, [128, 516], F32) for i in range(2)]
        LOGF = sbd("LOGF", [128, NKTMAX + 1, 8], F32, ld=True, st=True)
        CUM = sbd("CUM", [128, NKTMAX + 1, 8], F32)
        NEGCUM = sbd("NEGCUM", [128, NKTMAX + 1, 8], F32)
        PREFS = [sbd("PREF%d" % i, [128, 8], F32) for i in range(2)]
        CONVSTS = [sbd("CONVST%d" % i, [128, NFT, 2], F32) for i in range(2)]
        ST2 = sbd("ST2", [2, 512], F32, ld=True, st=True)
        SMALL = sbd("SMALL", [128, 16], F32)
        PET = sbd("PET", [128, 2, 512], BF16)
        SGB = [[Buf(None, "sgb%d_%d" % (g, m)) for m in range(16)] for g in range(2)]
        NPS = 6
        PS = [mk(es.enter_context(nc.psum_tensor("PS%d" % i, [128, 512], F32)), "PS%d" % i) for i in range(NPS)]
        PT = [mk(es.enter_context(nc.psum_tensor("PTb%d" % i, [128, 1024], BF16)), "PT%d" % i) for i in range(2)]
        PSX = []
        for i in range(2):
            v = mk(PT[i].t[:, :].bitcast(F32), "PSX%d" % i)
            v.al = [PT[i]]
            PT[i].al = [v]
            PSX.append(v)
        for b_ in PS + PT + PSX:
            b_.psum = True
        rr = {"ps": 0, "psn": NPS, "t32": 0, "tb": 0, "pt": 0, "u": 0, "ev": 0, "w": 0}

        def ps():
            rr["ps"] = (rr["ps"] + 1) % rr["psn"]
            return PS[rr["ps"]]

        def t32():
            rr["t32"] = (rr["t32"] + 1) % NT32
            return T32[rr["t32"]]

        def tb16():
            rr["tb"] = (rr["tb"] + 1) % NTB
            return TB16[rr["tb"]]

        def ptb():
            rr["pt"] = (rr["pt"] + 1) % 2
            return PT[rr["pt"]]

        for buf, src in ((ident32, I["ident"]), (tri32, I["tri"]), (gT["gT_mix"], I["gT_mix"]),
                         (gT["gT_ffn"], I["gT_ffn"]), (gT["gT_ple"], I["gT_ple"]), (gqT, I["gqT"]),
                         (gkv_bc, I["gkv_bc"]), (bf_bc, I["bf_bc"]), (bcT, I["bcT"])):
            dma("sp", buf.t[:, :], src, buf.ld, wr=[buf])
        dma("sp", wcT.t[:, :, :].rearrange("p a b -> p (a b)"), I["wcT"], wcT.ld, wr=[wcT])
        op("dve", "tensor_copy", [ident32], [identb], out=identb.t[:, :], in_=ident32.t[:, :])
        op("dve", "tensor_copy", [tri32], [trib], out=trib.t[:, :], in_=tri32.t[:, :])
        op("dve", "memset", [], [ones32], ones32.t[:, :], 1.0)
        op("dve", "memset", [], [onesb], onesb.t[:, :], 1.0)

        NGRP = 11
        gowner = [P.owner() for _ in range(NGRP)]
        WSB = [Buf(None, "wsgrp%d" % g, acc=True) for g in range(NGRP)]
        converted = set()

        import os as _os
        STOP = _os.environ.get("K_STOP", "")

        def wslab(key):
            si = SIDX[key]
            wn, nkc, r0, c0, width, grp = SLABS[si]
            b = WB[rr["w"] % NWB]
            rr["w"] += 1
            if si not in converted:
                converted.add(si)
                wsrc = WSRC[wn][r0:r0 + nkc * 128, c0:c0 + width].rearrange("(k p) c -> p k c", p=128)
                dma("pool", b.t[:, 0:nkc * width].rearrange("p (k c) -> p k c", k=nkc), wsrc, b.ld, wr=[b])
                dma("sp", WS[si, :, 0:nkc * width], b.t[:, 0:nkc * width], gowner[grp], rd=[b], wr=[WSB[grp]])
            else:
                dma("pool", b.t[:, 0:nkc * width], WS[si, :, 0:nkc * width], b.ld, rd=[WSB[grp]], wr=[b])
            return b, b.t[:, 0:nkc * width].rearrange("p (k c) -> p k c", k=nkc)

        def evac_copy(out_ap, in_ap, rd, wr, eng=None):
            if eng is None:
                rr["ev"] += 1
                eng = "act" if rr["ev"] % 2 else "dve"
            if eng == "act":
                op("act", "activation", rd, wr, out=out_ap, in_=in_ap, func=AF.Copy)
            else:
                op("dve", "tensor_copy", rd, wr, out=out_ap, in_=in_ap)

        def rstd_from_ss(ss_ap, out_ap, n):
            op("act", "activation", [SMALL], [SMALL], out=out_ap, in_=ss_ap, func=AF.Sqrt, bias=EPS, scale=1.0 / n)
            op("dve", "reciprocal", [SMALL], [SMALL], out=out_ap, in_=out_ap)

        def nt_phase1(src, sb, st, sz):
            rr["xn"] = rr.get("xn", 0) + 1
            XN = XNS[rr["xn"] % 2]
            ss = SMALL.t[:sz, st:st + 1]
            rs = SMALL.t[:sz, 8 + st:9 + st]
            op("act", "activation", [sb], [XN, SMALL], out=XN.t[:sz, :], in_=src, func=AF.Square, accum_out=ss)
            rstd_from_ss(ss, rs, D)
            op("act", "activation", [sb, SMALL], [XN], out=XN.t[:sz, :], in_=src, func=AF.Copy, scale=rs)
            return XN

        def nt_phase2(Abuf, XN, off, sz, gTb):
            for q4 in range(4):
                pt = ptb()
                for j in range(4):
                    kc = q4 * 4 + j
                    op("pe", "transpose", [XN, identb], [pt], out=pt.t[:, j * 128:j * 128 + sz],
                       in_=XN.t[:sz, kc * 128:(kc + 1) * 128], identity=identb.t[:sz, :sz])
                for j in range(4):
                    kc = q4 * 4 + j
                    op("dve", "tensor_scalar", [pt, gTb], [Abuf], out=Abuf.t[:, kc, off:off + sz],
                       in0=pt.t[:, j * 128:j * 128 + sz], scalar1=gTb.t[:, kc:kc + 1], scalar2=None, op0=ALU.mult)

        def norm_transpose(Abuf, src_fn, srcbufs, subt, gTb, load=None):
            for st, (off, sz) in enumerate(subt):
                if load is not None:
                    load(st, off, sz)
                XN = nt_phase1(src_fn(st, sz), srcbufs[st], st, sz)
                nt_phase2(Abuf, XN, off, sz, gTb)

        def norm_transpose_il(Abuf, Hs, subt, gTb, filler):
            if len(subt) != 4 or filler is None:
                if filler is not None:
                    filler()
                norm_transpose(Abuf, lambda st, sz: Hs[st].t[:sz, :], Hs, subt, gTb)
                return
            ph1 = lambda st: nt_phase1(Hs[st].t[:subt[st][1], :], Hs[st], st, subt[st][1])
            ph2 = lambda st, xn: nt_phase2(Abuf, xn, subt[st][0], subt[st][1], gTb)
            x0 = ph1(0)
            x1 = ph1(1)
            filler()
            ph2(0, x0)
            x2 = ph1(2)
            ph2(1, x1)
            x3 = ph1(3)
            ph2(2, x2)
            ph2(3, x3)

        def mm_fm(psb, W, wbuf, c0, m, rhs_fn, rhsbufs, nkc, n):
            for kc in range(nkc):
                op("pe", "matmul", [wbuf] + rhsbufs, [psb], psb.t[:m, 0:n], lhsT=W[:, kc, c0:c0 + m], rhs=rhs_fn(kc),
                   start=(kc == 0), stop=(kc == nkc - 1))

        def mm_tm(psb, pc0, W, wbuf, c0, width, lhs_fn, lhsbufs, nkc, sz):
            for kc in range(nkc):
                op("pe", "matmul", [wbuf] + lhsbufs, [psb], psb.t[:sz, pc0:pc0 + width], lhsT=lhs_fn(kc),
                   rhs=W[:, kc, c0:c0 + width], start=(kc == 0), stop=(kc == nkc - 1))

        def rope_fm(ps1, ps2, n, out_ap, outbuf):
            a = t32()
            b = t32()
            op("dve", "tensor_tensor", [ps1, ropeC], [a], out=a.t[:64, :n], in0=ps1.t[:64, :n], in1=ropeC.t[:64, :n],
               op=ALU.mult)
            op("dve", "tensor_tensor", [ps2, ropeS], [b], out=b.t[:64, :n], in0=ps2.t[:64, :n], in1=ropeS.t[:64, :n],
               op=ALU.mult)
            op("dve", "tensor_tensor", [a, b], [outbuf], out=out_ap, in0=a.t[:64, :n], in1=b.t[:64, :n], op=ALU.add)

        class Seq:
            pass

        def make_seq(name, cap):
            s = Seq()
            s.cap = cap
            s.KnT = dint(name + "_knt", [8, 128, cap])
            s.KrT = dint(name + "_krt", [64, cap])
            s.KfT = dint(name + "_kft", [8, 128, cap])
            s.Vm = dint(name + "_vm", [cap, 1024])
            s.Vf = dint(name + "_vf", [cap, 1024])
            s.kv = Buf(None, name + "_kv", acc=True)
            return s

        def cum_tile(kt, sz, first, PREF):
            p1 = ps()
            op("pe", "matmul", [tri32, LOGF], [p1], p1.t[:sz, 0:8], lhsT=tri32.t[:sz, :sz], rhs=LOGF.t[:sz, kt, :],
               start=True, stop=True)
            if first:
                op("dve", "tensor_copy", [p1], [CUM], out=CUM.t[:sz, kt, :], in_=p1.t[:sz, 0:8])
            else:
                op("dve", "tensor_tensor", [p1, PREF], [CUM], out=CUM.t[:sz, kt, :], in0=p1.t[:sz, 0:8],
                   in1=PREF.t[:sz, :], op=ALU.add)
            op("dve", "tensor_scalar", [CUM], [NEGCUM], out=NEGCUM.t[:sz, kt, :], in0=CUM.t[:sz, kt, :],
               scalar1=-1.0, scalar2=None, op0=ALU.mult)
            p2 = ps()
            op("pe", "matmul", [ones32, LOGF], [p2], p2.t[:, 0:8], lhsT=ones32.t[:sz, :], rhs=LOGF.t[:sz, kt, :],
               start=True, stop=True)
            if first:
                op("dve", "tensor_copy", [p2], [PREF], out=PREF.t[:, :], in_=p2.t[:, 0:8])
            else:
                op("dve", "tensor_tensor", [p2, PREF], [PREF], out=PREF.t[:, :], in0=p2.t[:, 0:8], in1=PREF.t[:, :],
                   op=ALU.add)

        def store_groups(s, groups, dst_fn, part=128):
            for G_ in groups:
                dma("sp", dst_fn(G_), s.t[:part, G_["col0"]:G_["col0"] + G_["n"]], s.st, rd=[s], wr=[G_["seq"].kv])
            for G_ in groups:
                G_["seq"].kv.lw[id(s.st)] = (s.st.sem, s.st.cnt)

        def kv_upproj(groups, segs, n):
            wb0, W0 = wslab(("ukv", 0))
            for h in range(8):
                p = ps()
                mm_fm(p, W0, wb0, h * 128, 128, lambda kc: CKVT.t[:, kc, 0:n], [CKVT], 4, n)
                s = tb16()
                evac_copy(s.t[:, 0:n], p.t[:, 0:n], [p], [s])
                store_groups(s, groups, lambda G_: G_["seq"].KnT[h, :, G_["pos0"]:G_["pos0"] + G_["n"]])
            wb1, W1 = wslab(("ukv", 1))
            for S_ in segs:
                off, sz = S_["off"], S_["sz"]
                G_ = groups[S_["g"]]
                r0 = G_["pos0"] + off - G_["col0"]
                for cg in range(2):
                    p = ps()
                    mm_tm(p, 0, W1, wb1, cg * 512, 512, lambda kc: CKVT.t[:, kc, off:off + sz], [CKVT], 4, sz)
                    s = tb16()
                    evac_copy(s.t[:sz, :], p.t[:sz, :], [p], [s])
                    dma("sp", G_["seq"].Vm[r0:r0 + sz, cg * 512:(cg + 1) * 512], s.t[:sz, :], s.st,
                        rd=[s], wr=[G_["seq"].kv])

        def attention(seq, q0, nq, mla, col0=0):
            ktb = seq.ktbase
            nkeys = q0 + nq
            nkt = (nkeys + 127) // 128
            nch = (nkt + 7) // 8
            units = [(h, c) for h in range(8) for c in range(nch)]
            KT = seq.KnT if mla else seq.KfT
            Vd = seq.Vm if mla else seq.Vf
            Ob = OA if mla else OB
            loaded = {}
            SBK = [PS[0], PS[1], PSX[0], PSX[1]]
            sst = {"i": 0}

            def ps():
                sst["i"] = (sst["i"] + 1) % 4
                return SBK[sst["i"]]

            def issue(ui):
                if ui >= len(units):
                    return
                h, c = units[ui]
                k0 = c * 1024
                kn = min(1024, nkeys - k0)
                kb = KB[ui % 2]
                dma("sp", kb.t[:, 0:kn], KT[h, :, k0:k0 + kn], kb.ld, rd=[seq.kv], wr=[kb])
                vb = VB[ui % 2]
                nfull = kn // 128
                hp = h // 2
                if nfull > 0:
                    dma("sp", vb.t[:, 0:nfull, :],
                        Vd[k0:k0 + nfull * 128, hp * 256:(hp + 1) * 256].rearrange("(kt p) d -> p kt d", p=128),
                        vb.ld, rd=[seq.kv], wr=[vb])
                rem = kn - nfull * 128
                if rem > 0:
                    dma("sp", vb.t[0:rem, nfull, :], Vd[k0 + nfull * 128:k0 + kn, hp * 256:(hp + 1) * 256],
                        vb.ld, rd=[seq.kv], wr=[vb])
                krb = None
                if mla:
                    krb = KRB[ui % 2]
                    dma("sp", krb.t[:, 0:kn], seq.KrT[:, k0:k0 + kn], krb.ld, rd=[seq.kv], wr=[krb])
                loaded[ui] = (kb, vb, krb)

            issue(0)
            issue(1)
            jobs = []
            for ui, (h, c) in enumerate(units):
                ktn = min(8, nkt - c * 8)
                tl = []
                for kl in range(ktn):
                    kt = c * 8 + kl
                    ks = kt * 128
                    ksz = min(128, nkeys - ks)
                    j0 = max(0, ks - q0)
                    if j0 >= nq:
                        continue
                    tl.append((kl, kt, ks, ksz, j0))
                for i, (kl, kt, ks, ksz, j0) in enumerate(tl):
                    jobs.append(dict(ui=ui, h=h, c=c, kl=kl, kt=kt, ks=ks, ksz=ksz, j0=j0,
                                     first_of_unit=(i == 0), last_of_unit=(i == len(tl) - 1)))
            state = {}

            def stage1(J):
                ui, h, c, kl, kt, ks, ksz, j0 = (J[k] for k in ("ui", "h", "c", "kl", "kt", "ks", "ksz", "j0"))
                kb, vb, krb = loaded[ui]
                if c == 0 and J["first_of_unit"]:
                    state[h] = dict(po=PS[2 + 2 * (h % 2)], pd=PS[3 + 2 * (h % 2)], cq=None)
                    if not mla:
                        cq = CQH[h % 2]
                        pq = ps()
                        for off in range(0, nq, 128):
                            sz = min(128, nq - off)
                            kt_q = (q0 + off) // 128
                            dg = t32()
                            op("dve", "tensor_scalar", [ident32, CUM], [dg], out=dg.t[:sz, :sz],
                               in0=ident32.t[:sz, :sz], scalar1=CUM.t[:sz, ktb + kt_q, h:h + 1], scalar2=None,
                               op0=ALU.mult)
                            op("pe", "matmul", [ones32, dg], [pq], pq.t[:, off:off + sz], lhsT=ones32.t[:sz, :],
                               rhs=dg.t[:sz, :sz], start=True, stop=True)
                        evac_copy(cq.t[:, 0:nq], pq.t[:, 0:nq], [pq], [cq], eng="act")
                        state[h]["cq"] = cq
                cq = state[h]["cq"]
                diag = (ks + ksz - 1) > (q0 + j0)
                sp_ = ps()
                if mla:
                    op("pe", "matmul", [kb, QN], [sp_], sp_.t[:ksz, j0:nq], lhsT=kb.t[:, kl * 128:kl * 128 + ksz],
                       rhs=QN.t[:, h, col0 + j0:col0 + nq], start=True, stop=False)
                    op("pe", "matmul", [krb, QR], [sp_], sp_.t[:ksz, j0:nq],
                       lhsT=krb.t[:, kl * 128:kl * 128 + ksz], rhs=QR.t[:, h, col0 + j0:col0 + nq], start=False, stop=True)
                else:
                    op("pe", "matmul", [kb, QF], [sp_], sp_.t[:ksz, j0:nq], lhsT=kb.t[:, kl * 128:kl * 128 + ksz],
                       rhs=QF.t[:, h, col0 + j0:col0 + nq], start=True, stop=True)
                pt_ = tb16()
                if mla:
                    op("act", "activation", [sp_], [pt_], out=pt_.t[:ksz, j0:nq], in_=sp_.t[:ksz, j0:nq],
                       func=AF.Exp, scale=MLA_SCALE)
                    if diag and ksz > 64:
                        op("dve", "memset", [], [pt_], pt_.t[64:128, j0:j0 + 64], 0.0)
                else:
                    tmp = t32()
                    op("dve", "scalar_tensor_tensor", [sp_, cq], [tmp], out=tmp.t[:ksz, j0:nq],
                       in0=sp_.t[:ksz, j0:nq], scalar=FOX_SCALE, in1=cq.t[:ksz, j0:nq], op0=ALU.mult, op1=ALU.add)
                    op("act", "activation", [tmp, NEGCUM], [pt_], out=pt_.t[:ksz, j0:nq], in_=tmp.t[:ksz, j0:nq],
                       func=AF.Exp, bias=NEGCUM.t[:ksz, ktb + kt, h:h + 1])
                    if diag:
                        msz = min(ksz, nq - j0)
                        op("dve", "tensor_tensor", [pt_, trib], [pt_], out=pt_.t[:ksz, j0:j0 + msz],
                           in0=pt_.t[:ksz, j0:j0 + msz], in1=trib.t[:ksz, :msz], op=ALU.mult)
                J["pt"] = pt_

            def stage2(J):
                ui, h, c, kl, kt, ksz, j0 = (J[k] for k in ("ui", "h", "c", "kl", "kt", "ksz", "j0"))
                kb, vb, krb = loaded[ui]
                po, pd = state[h]["po"], state[h]["pd"]
                pt_ = J["pt"]
                first = (kt == 0)
                last = (kt == nkt - 1)
                vcol = (h % 2) * 128
                op("pe", "matmul", [vb, pt_], [po], po.t[:, j0:nq], lhsT=vb.t[:ksz, kl, vcol:vcol + 128],
                   rhs=pt_.t[:ksz, j0:nq], start=first, stop=last)
                op("pe", "matmul", [onesb, pt_], [pd], pd.t[:, j0:nq], lhsT=onesb.t[:ksz, :],
                   rhs=pt_.t[:ksz, j0:nq], start=first, stop=last)
                if J["last_of_unit"]:
                    issue(ui + 2)
                    if c == nch - 1:
                        rd_ = t32()
                        op("act", "activation", [pd], [rd_], out=rd_.t[:, 0:nq], in_=pd.t[:, 0:nq], func=AF.Ln)
                        op("act", "activation", [rd_], [rd_], out=rd_.t[:, 0:nq], in_=rd_.t[:, 0:nq], func=AF.Exp,
                           scale=-1.0)
                        op("dve", "tensor_tensor", [po, rd_], [Ob], out=Ob.t[:, h, col0:col0 + nq], in0=po.t[:, 0:nq],
                           in1=rd_.t[:, 0:nq], op=ALU.mult)

            pending = []
            for i, J in enumerate(jobs):
                if J["first_of_unit"]:
                    while pending and pending[0]["ui"] <= J["ui"] - 2:
                        stage2(pending.pop(0))
                stage1(J)
                pending.append(J)
                if len(pending) > 2:
                    stage2(pending.pop(0))
            while pending:
                stage2(pending.pop(0))

        def stage_a(Adst, segs):
            subt = [(S_["off"], S_["sz"]) for S_ in segs]

            def load_x(st, off, sz):
                dma("sp", XS.t[:sz, :], segs[st]["x"], XS.ld, wr=[XS])
            norm_transpose(Adst, lambda st, sz: XS.t[:sz, :], [XS] * len(subt), subt, gT["gT_mix"], load=load_x)

        def fk_fm(Asrc, groups, n, slabs):
            for i in slabs:
                wb, W = wslab(("in_fm", i))
                for tt in range(2):
                    h = i * 2 + tt - 12
                    p = ps()
                    mm_fm(p, W, wb, tt * 128, 128, lambda kc: Asrc.t[:, kc, 0:n], [Asrc], 16, n)
                    s = tb16()
                    evac_copy(s.t[:, 0:n], p.t[:, 0:n], [p], [s])
                    store_groups(s, groups, lambda G_: G_["seq"].KfT[h, :, G_["pos0"]:G_["pos0"] + G_["n"]])

        def gates_fm(Asrc, n, gi):
            for i in range(8):
                wb, W = wslab(("in_g", gi * 8 + i))
                for tt in range(2):
                    m = i * 2 + tt
                    p = ps()
                    mm_fm(p, W, wb, tt * 128, 128, lambda kc: Asrc.t[:, kc, 0:n], [Asrc], 16, n)
                    s = tb16()
                    op("act", "activation", [p], [s], out=s.t[:, 0:n], in_=p.t[:, 0:n], func=AF.Sigmoid)
                    dma("sp", SGD[gi, m, :, 0:n], s.t[:, 0:n], s.st, rd=[s], wr=[SGB[gi][m]])

        def block(bi, groups, segs, n, ropeC_src, ropeS_src, a_done=False, next_a=None, skip_fk=False, next_b=None,
                  next_c=None):
            subt = [(S_["off"], S_["sz"]) for S_ in segs]
            nst = len(subt)
            Abuf = AB[bi % 2]
            Abuf2 = AB[(bi + 1) % 2]
            dma("sp", ropeC.t[:, 0:n], ropeC_src, ropeC.ld, wr=[ropeC])
            dma("sp", ropeS.t[:, 0:n], ropeS_src, ropeS.ld, wr=[ropeS])

            if not a_done:
                stage_a(Abuf, segs)

            def A(kc):
                return Abuf.t[:, kc, 0:n]

            def At(off, sz):
                return lambda kc: Abuf.t[:, kc, off:off + sz]
            if STOP == "A":
                return
            for i in range(6):
                wb, W = wslab(("in_fm", i))
                for tt in range(2):
                    t = i * 2 + tt
                    p = ps()
                    mm_fm(p, W, wb, tt * 128, 128, A, [Abuf], 16, n)
                    if t < 4:
                        evac_copy(CQ32[t].t[:, 0:n], p.t[:, 0:n], [p], [CQ32[t]], eng="act")
                    else:
                        evac_copy(QF.t[:, t - 4, 0:n], p.t[:, 0:n], [p], [QF])
            if not skip_fk:
                fk_fm(Abuf, groups, n, range(6, 10))
            if STOP == "B1":
                return
            wb, W = wslab(("in_kr", 0))
            p1, p2 = ps(), ps()
            mm_fm(p1, W, wb, 0, 64, A, [Abuf], 16, n)
            mm_fm(p2, W, wb, 64, 64, A, [Abuf], 16, n)
            kr32 = t32()
            rope_fm(p1, p2, n, kr32.t[:64, 0:n], kr32)
            s = tb16()
            evac_copy(s.t[:64, 0:n], kr32.t[:64, 0:n], [kr32], [s])
            store_groups(s, groups, lambda G_: G_["seq"].KrT[:, G_["pos0"]:G_["pos0"] + G_["n"]], part=64)
            for st, (off, sz) in enumerate(subt):
                p = ps()
                op("pe", "transpose", [kr32, ident32], [p], out=p.t[:sz, 0:64], in_=kr32.t[:64, off:off + sz],
                   identity=ident32.t[:64, :64])
                o32 = t32()
                evac_copy(o32.t[:sz, 0:64], p.t[:sz, 0:64], [p], [o32])
                dma("sp", segs[st]["out"]["kr"], o32.t[:sz, 0:64], o32.st, rd=[o32])
            if STOP == "B2":
                return
            if not skip_fk:
                gates_fm(Abuf, n, 0)
                gates_fm(Abuf, n, 1)
            if STOP == "B3":
                return
            wbs = [wslab(("in_tm", half)) for half in range(2)]
            for st, (off, sz) in enumerate(subt):
                p = ps()
                for half in range(2):
                    mm_tm(p, half * 256, wbs[half][1], wbs[half][0], 0, 256, At(off, sz), [Abuf], 16, sz)
                junk = t32()
                ss = SMALL.t[:sz, st:st + 1]
                rs = SMALL.t[:sz, 8 + st:9 + st]
                op("act", "activation", [p], [junk, SMALL], out=junk.t[:sz, :], in_=p.t[:sz, :], func=AF.Square,
                   accum_out=ss)
                rstd_from_ss(ss, rs, 512)
                o32 = t32()
                op("dve", "scalar_tensor_tensor", [p, SMALL, gkv_bc], [o32], out=o32.t[:sz, :], in0=p.t[:sz, :],
                   scalar=rs, in1=gkv_bc.t[:sz, :], op0=ALU.mult, op1=ALU.mult)
                dma("sp", segs[st]["out"]["ckv"], o32.t[:sz, :], o32.st, rd=[o32])
                cb = tb16()
                evac_copy(cb.t[:sz, :], o32.t[:sz, :], [o32], [cb], eng="act")
                pt = ptb()
                for kc in range(4):
                    op("pe", "transpose", [cb, identb], [pt], out=pt.t[:, kc * 128:kc * 128 + sz],
                       in_=cb.t[:sz, kc * 128:(kc + 1) * 128], identity=identb.t[:sz, :sz])
                for kc in range(4):
                    evac_copy(CKVT.t[:, kc, off:off + sz], pt.t[:, kc * 128:kc * 128 + sz], [pt], [CKVT], eng="dve")
            if STOP == "B4":
                return
            def fkv_tm(which):
                for cg in range(2):
                    wbs = [wslab(("in_tm", 2 + which * 4 + cg * 2 + half)) for half in range(2)]
                    for st, (off, sz) in enumerate(subt):
                        p = ps()
                        for half in range(2):
                            mm_tm(p, half * 256, wbs[half][1], wbs[half][0], 0, 256, At(off, sz), [Abuf], 16, sz)
                        o32 = t32()
                        rr["ev"] += 1
                        eng_ = "act" if rr["ev"] % 2 else "dve"
                        evac_copy(o32.t[:sz, :], p.t[:sz, :], [p], [o32], eng=eng_)
                        dst = segs[st]["out"]["fk"] if which == 0 else segs[st]["out"]["fv"]
                        dma("sp", dst[:, cg * 512:(cg + 1) * 512], o32.t[:sz, :], o32.st, rd=[o32])
                        if which == 1:
                            vb_ = tb16()
                            evac_copy(vb_.t[:sz, :], p.t[:sz, :], [p], [vb_], eng=eng_)
                            G_ = groups[segs[st]["g"]]
                            r0 = G_["pos0"] + off - G_["col0"]
                            dma("sp", G_["seq"].Vf[r0:r0 + sz, cg * 512:(cg + 1) * 512], vb_.t[:sz, :],
                                vb_.st, rd=[vb_], wr=[G_["seq"].kv])
            if STOP == "B5":
                return
            fkv_tm(1)
            wb, W = wslab(("in_fl", 0))
            for st, (off, sz) in enumerate(subt):
                p = ps()
                mm_tm(p, 0, W, wb, 0, 8, At(off, sz), [Abuf], 16, sz)
                kt = segs[st]["kt"]
                a = t32()
                op("dve", "tensor_tensor", [p, bf_bc], [a], out=a.t[:sz, 0:8], in0=p.t[:sz, 0:8], in1=bf_bc.t[:sz, :],
                   op=ALU.add)
                op("act", "activation", [a], [a], out=a.t[:sz, 8:16], in_=a.t[:sz, 0:8], func=AF.Exp, scale=-1.0)
                op("act", "activation", [a], [a], out=a.t[:sz, 16:24], in_=a.t[:sz, 8:16], func=AF.Ln, bias=1.0)
                op("dve", "tensor_scalar", [a], [LOGF], out=LOGF.t[:sz, kt, :], in0=a.t[:sz, 16:24], scalar1=-1.0,
                   scalar2=None, op0=ALU.mult)
                dma("sp", segs[st]["out"]["lf"], LOGF.t[:sz, kt, :], LOGF.st, rd=[LOGF])
                cum_tile(kt, sz, segs[st]["first_cum"], groups[segs[st]["g"]]["seq"].PREF)
            if STOP == "B6":
                return
            sq = [t32() for _ in range(4)]
            for kc in range(4):
                op("act", "activation", [CQ32[kc]], [sq[kc]], out=sq[kc].t[:, 0:n], in_=CQ32[kc].t[:, 0:n],
                   func=AF.Square)
            pq = ps()
            for kc in range(4):
                op("pe", "matmul", [ones32, sq[kc]], [pq], pq.t[:, 0:n], lhsT=ones32.t[:, :], rhs=sq[kc].t[:, 0:n],
                   start=(kc == 0), stop=(kc == 3))
            rb = t32()
            op("act", "activation", [pq], [rb], out=rb.t[:, 0:n], in_=pq.t[:, 0:n], func=AF.Sqrt, bias=EPS,
               scale=1.0 / 512)
            op("dve", "reciprocal", [rb], [rb], out=rb.t[:, 0:n], in_=rb.t[:, 0:n])
            for kc in range(4):
                op("dve", "scalar_tensor_tensor", [CQ32[kc], gqT, rb], [CQN], out=CQN.t[:, kc, 0:n],
                   in0=CQ32[kc].t[:, 0:n], scalar=gqT.t[:, kc:kc + 1], in1=rb.t[:, 0:n], op0=ALU.mult, op1=ALU.mult)

            def cqn(kc):
                return CQN.t[:, kc, 0:n]
            wb, W = wslab(("uq", 0))
            for h in range(8):
                p = ps()
                mm_fm(p, W, wb, h * 128, 128, cqn, [CQN], 4, n)
                evac_copy(QN.t[:, h, 0:n], p.t[:, 0:n], [p], [QN])
            wb, W = wslab(("uq", 1))
            for h in range(8):
                p1, p2 = ps(), ps()
                mm_fm(p1, W, wb, h * 64, 64, cqn, [CQN], 4, n)
                mm_fm(p2, W, wb, 512 + h * 64, 64, cqn, [CQN], 4, n)
                rope_fm(p1, p2, n, QR.t[:64, h, 0:n], QR)
            if STOP == "C":
                return
            kv_upproj(groups, segs, n)
            if STOP == "D":
                return
            for G_ in groups:
                attention(G_["seq"], G_["pos0"], G_["n"], True, G_["col0"])
            for G_ in groups:
                attention(G_["seq"], G_["pos0"], G_["n"], False, G_["col0"])
            if STOP == "E":
                return
            for q4 in range(4):
                wba, Wa = wslab(("oa", q4))
                wbb, Wb = wslab(("ob", q4))
                for tt in range(4):
                    m = q4 * 4 + tt
                    sga, sgb = tb16(), tb16()
                    dma("sp", sga.t[:, 0:n], SGD[0, m, :, 0:n], sga.ld, rd=[SGB[0][m]], wr=[sga])
                    dma("sp", sgb.t[:, 0:n], SGD[1, m, :, 0:n], sgb.ld, rd=[SGB[1][m]], wr=[sgb])
                    pa, pb = ps(), ps()
                    mm_fm(pa, Wa, wba, tt * 128, 128, lambda kc: OA.t[:, kc, 0:n], [OA], 8, n)
                    mm_fm(pb, Wb, wbb, tt * 128, 128, lambda kc: OB.t[:, kc, 0:n], [OB], 8, n)
                    t1, t2 = t32(), t32()
                    op("dve", "tensor_tensor", [pa, sga], [t1], out=t1.t[:, 0:n], in0=pa.t[:, 0:n], in1=sga.t[:, 0:n],
                       op=ALU.mult)
                    op("dve", "tensor_tensor", [pb, sgb], [t2], out=t2.t[:, 0:n], in0=pb.t[:, 0:n], in1=sgb.t[:, 0:n],
                       op=ALU.mult)
                    op("dve", "tensor_tensor", [t1, t2], [Abuf2], out=Abuf2.t[:, m, 0:n], in0=t1.t[:, 0:n],
                       in1=t2.t[:, 0:n], op=ALU.add)
            if STOP == "F1":
                return
            for st, (off, sz) in enumerate(subt):
                dma("sp", H[st].t[:sz, :], segs[st]["x"], H[st].ld, wr=[H[st]])
            for i in range(4):
                wbs = [wslab(("o", i * 2 + half)) for half in range(2)]
                for st, (off, sz) in enumerate(subt):
                    p = ps()
                    for half in range(2):
                        mm_tm(p, half * 256, wbs[half][1], wbs[half][0], 0, 256,
                              (lambda o_, s_: (lambda kc: Abuf2.t[:, kc, o_:o_ + s_]))(off, sz), [Abuf2], 16, sz)
                    hs = H[st].t[:sz, i * 512:(i + 1) * 512]
                    op("dve", "tensor_tensor", [p, H[st]], [H[st]], out=hs, in0=p.t[:sz, :], in1=hs, op=ALU.add)
            if STOP == "F2":
                return
            norm_transpose_il(Abuf, H, subt, gT["gT_ffn"], lambda: fkv_tm(0))
            for G_ in groups:
                if not G_["first"]:
                    continue
                CONVST = G_["seq"].CONVST
                if G_["seq"].conv_src is None:
                    op("dve", "memset", [], [CONVST], CONVST.t[:, :, :].rearrange("p a b -> p (a b)"), 0.0)
                else:
                    pc = ps()
                    for q in range(22):
                        dma("sp", ST2.t[0:2, :], G_["seq"].conv_src[:, q * 512:(q + 1) * 512], ST2.ld, wr=[ST2])
                        for jj in range(4):
                            j = q * 4 + jj
                            op("pe", "matmul", [ST2, ident32], [pc], pc.t[:, 2 * j:2 * j + 2],
                               lhsT=ST2.t[0:2, jj * 128:(jj + 1) * 128], rhs=ident32.t[0:2, 0:2], start=True, stop=True)
                    op("act", "activation", [pc], [CONVST], out=CONVST.t[:, :, :].rearrange("p a b -> p (a b)"),
                       in_=pc.t[:, 0:2 * NFT], func=AF.Copy)

            def conv_a(p, jj):
                rr["u"] = (rr["u"] + 1) % 2
                u = U[rr["u"]]
                acc = t32()
                for gi, G_ in enumerate(groups):
                    ub, c0, ng = G_["col0"] + 2 * gi, G_["col0"], G_["n"]
                    CONVST = G_["seq"].CONVST
                    op("dve", "tensor_copy", [CONVST], [u], out=u.t[:, ub:ub + 2], in_=CONVST.t[:, jj, :])
                    op("act", "activation", [p], [u], out=u.t[:, ub + 2:ub + 2 + ng], in_=p.t[:, c0:c0 + ng], func=AF.Copy)
                op("act", "activation", [p, wcT, bcT], [acc], out=acc.t[:, 0:n], in_=p.t[:, 0:n], func=AF.Identity,
                   scale=wcT.t[:, jj, 2:3], bias=bcT.t[:, jj:jj + 1])
                for gi, G_ in enumerate(groups):
                    ub, c0, ng = G_["col0"] + 2 * gi, G_["col0"], G_["n"]
                    op("dve", "scalar_tensor_tensor", [u, wcT, acc], [acc], out=acc.t[:, c0:c0 + ng],
                       in0=u.t[:, ub + 1:ub + 1 + ng], scalar=wcT.t[:, jj, 1:2], in1=acc.t[:, c0:c0 + ng],
                       op0=ALU.mult, op1=ALU.add)
                return u, acc

            def conv_b(u, acc, jj):
                for gi, G_ in enumerate(groups):
                    ub, c0, ng = G_["col0"] + 2 * gi, G_["col0"], G_["n"]
                    CONVST = G_["seq"].CONVST
                    op("dve", "scalar_tensor_tensor", [u, wcT, acc], [acc], out=acc.t[:, c0:c0 + ng],
                       in0=u.t[:, ub:ub + ng], scalar=wcT.t[:, jj, 0:1], in1=acc.t[:, c0:c0 + ng],
                       op0=ALU.mult, op1=ALU.add)
                    op("dve", "tensor_copy", [u], [CONVST], out=CONVST.t[:, jj, :], in_=u.t[:, ub + ng:ub + ng + 2])
                return acc

            def down(g):
                wbd, Wd = wslab(("dn", g))
                Gb = G[g % 2]
                for cg in range(4):
                    for st, (off, sz) in enumerate(subt):
                        p = ps()
                        mm_tm(p, 0, Wd, wbd, cg * 512, 512, (lambda o_, s_: (lambda kc: Gb.t[:, kc, o_:o_ + s_]))(off, sz),
                              [Gb], 2, sz)
                        hs = H[st].t[:sz, cg * 512:(cg + 1) * 512]
                        op("dve", "tensor_tensor", [p, H[st]], [H[st]], out=hs, in0=p.t[:sz, :], in1=hs, op=ALU.add)

            pre = {}
            for g in range(22):
                if next_a is not None:
                    na = len(next_a)
                    st2 = g - (22 - na)
                    if 0 <= st2 < na:
                        S2 = next_a[st2]
                        nt_phase2(Abuf2, pre[st2], S2["off"], S2["sz"], gT["gT_mix"])
                    st1 = g - (21 - na)
                    if 0 <= st1 < na:
                        S1 = next_a[st1]
                        dma("sp", XS.t[:S1["sz"], :], S1["x"], XS.ld, wr=[XS])
                        pre[st1] = nt_phase1(XS.t[:S1["sz"], :], XS, st1, S1["sz"])
                wbv, Wv = wslab(("upv", g))
                wbg, Wg = wslab(("upg", g))
                Gb = G[g % 2]
                for tt in range(2):
                    j = g * 2 + tt
                    pv, pg = ps(), ps()
                    mm_fm(pv, Wv, wbv, tt * 128, 128, A, [Abuf], 16, n)
                    mm_fm(pg, Wg, wbg, tt * 128, 128, A, [Abuf], 16, n)
                    uv, vc = conv_a(pv, j)
                    ug, gc = conv_a(pg, 44 + j)
                    conv_b(uv, vc, j)
                    conv_b(ug, gc, 44 + j)
                    op("act", "activation", [gc], [gc], out=gc.t[:, 0:n], in_=gc.t[:, 0:n], func=AF.Gelu_apprx_tanh)
                    op("dve", "tensor_tensor", [gc, vc], [Gb], out=Gb.t[:, tt, 0:n], in0=gc.t[:, 0:n],
                       in1=vc.t[:, 0:n], op=ALU.mult)
                if g > 0:
                    down(g - 1)
            down(21)
            for G_ in groups:
                conv_out = G_["conv_out"]
                if conv_out is None:
                    continue
                CONVST = G_["seq"].CONVST
                for q in range(22):
                    pc = ps()
                    for jj in range(4):
                        j = q * 4 + jj
                        op("pe", "matmul", [CONVST, ident32], [pc], pc.t[0:2, jj * 128:(jj + 1) * 128],
                           lhsT=CONVST.t[:, j, :], rhs=ident32.t[:, :], start=True, stop=True)
                    op("act", "activation", [pc], [ST2], out=ST2.t[0:2, :], in_=pc.t[0:2, :], func=AF.Copy)
                    dma("sp", conv_out[:, q * 512:(q + 1) * 512], ST2.t[0:2, :], ST2.st, rd=[ST2])
            if STOP == "F3":
                return
            norm_transpose_il(Abuf, H, subt, gT["gT_ple"], (lambda: next_b(Abuf2)) if next_b is not None else None)
            for st, (off, sz) in enumerate(subt):
                pe32 = t32()
                dma("sp", pe32.t[:sz, 0:256], segs[st]["pe"], pe32.ld, wr=[pe32])
                peb = tb16()
                evac_copy(peb.t[:sz, 0:256], pe32.t[:sz, 0:256], [pe32], [peb])
                pt = ptb()
                for kc in range(2):
                    op("pe", "transpose", [peb, identb], [pt], out=pt.t[:, kc * 128:kc * 128 + sz],
                       in_=peb.t[:sz, kc * 128:(kc + 1) * 128], identity=identb.t[:sz, :sz])
                for kc in range(2):
                    evac_copy(PET.t[:, kc, off:off + sz], pt.t[:, kc * 128:kc * 128 + sz], [pt], [PET], eng="dve")
            for i in range(4):
                wbp, Wp = wslab(("ple", 0))
                wbs = [wslab(("pg", i * 2 + half)) for half in range(2)]
                for st, (off, sz) in enumerate(subt):
                    p1 = ps()
                    for half in range(2):
                        mm_tm(p1, half * 256, wbs[half][1], wbs[half][0], 0, 256, At(off, sz), [Abuf], 16, sz)
                    sg = t32()
                    op("act", "activation", [p1], [sg], out=sg.t[:sz, :], in_=p1.t[:sz, :], func=AF.Sigmoid)
                    p2 = ps()
                    mm_tm(p2, 0, Wp, wbp, i * 512, 512, (lambda o_, s_: (lambda kc: PET.t[:, kc, o_:o_ + s_]))(off, sz),
                          [PET], 2, sz)
                    op("dve", "tensor_tensor", [p2, sg], [sg], out=sg.t[:sz, :], in0=p2.t[:sz, :], in1=sg.t[:sz, :],
                       op=ALU.mult)
                    hs = H[st].t[:sz, i * 512:(i + 1) * 512]
                    op("dve", "tensor_tensor", [sg, H[st]], [H[st]], out=hs, in0=sg.t[:sz, :], in1=hs, op=ALU.add)
            if STOP == "F5":
                return
            if next_c is not None:
                next_c(Abuf2)
            for st, (off, sz) in enumerate(subt):
                ss = SMALL.t[:sz, st:st + 1]
                rs = SMALL.t[:sz, 8 + st:9 + st]
                op("act", "activation", [H[st]], [XN, SMALL], out=XN.t[:sz, :], in_=H[st].t[:sz, :], func=AF.Square,
                   accum_out=ss)
                rstd_from_ss(ss, rs, D)
                for i in range(4):
                    gq = t32()
                    dma("sp", gq.t[:, :], I["gfin_bc"][:, i * 512:(i + 1) * 512], gq.ld, wr=[gq])
                    hs = H[st].t[:sz, i * 512:(i + 1) * 512]
                    op("dve", "scalar_tensor_tensor", [H[st], SMALL, gq], [H[st]], out=hs, in0=hs, scalar=rs,
                       in1=gq.t[:sz, :], op0=ALU.mult, op1=ALU.mult)
                dma("sp", segs[st]["out"]["y"], H[st].t[:sz, :], H[st].st, rd=[H[st]])

        OK_ = ("y", "ckv", "kr", "fk", "fv", "lf")

        def prompt_segs(b):
            return [dict(off=o, sz=128, g=0, x=I["xp"][b * 512 + o:b * 512 + o + 128, :],
                         pe=I["pp"][b * 512 + o:b * 512 + o + 128, :],
                         out={k: O[k + "_p"][b * 512 + o:b * 512 + o + 128, :] for k in OK_},
                         kt=(b * 512 + o) // 128, first_cum=(b == 0 and o == 0)) for o in range(0, 512, 128)]

        if TP > 0 and STOP != "conv":
            sp_ = make_seq("p", CAPP)
            sp_.conv_src = None
            sp_.ktbase = 0
            sp_.PREF = PREFS[0]
            sp_.CONVST = CONVSTS[0]
            nb = TP // 512
            for b in range(nb):
                nxt = nxtb = nxtc = None
                if b + 1 < nb:
                    nxt = prompt_segs(b + 1)

                    def nxtb(Asrc, b1=b + 1):
                        fk_fm(Asrc, [dict(seq=sp_, pos0=b1 * 512, col0=0, n=512)], 512, range(6, 10))
                        gates_fm(Asrc, 512, 0)

                    def nxtc(Asrc):
                        gates_fm(Asrc, 512, 1)
                groups = [dict(seq=sp_, pos0=b * 512, col0=0, n=512, first=(b == 0),
                               conv_out=(O["conv_p"] if b == nb - 1 else None))]
                block(b, groups, prompt_segs(b), 512, I["ropeC_p"][:, b * 512:(b + 1) * 512],
                      I["ropeS_p"][:, b * 512:(b + 1) * 512], a_done=(b > 0), next_a=nxt, skip_fk=(b > 0),
                      next_b=nxtb, next_c=nxtc)

        NKTS = CAPS // 128
        sseqs = []
        for s in range(NSS if STOP not in ("conv", "prompt") else 0):
            sq_ = make_seq("s%d" % s, CAPS)
            sq_.conv_src = I["c_conv"][s]
            sq_.ktbase = s * NKTS
            sq_.PREF = PREFS[s % 2]
            sq_.CONVST = CONVSTS[s % 2]
            sseqs.append(sq_)
            vown = P.owner()
            for c in range(PAST // 512):
                t0 = c * 512
                pf_groups = [dict(seq=sq_, pos0=t0, col0=0, n=512)]
                pf_segs = [dict(off=o, sz=128, g=0) for o in range(0, 512, 128)]
                dma("pool", CB.t[:, 0:2048].rearrange("p (a b) -> p a b", a=4),
                    I["c_ckv"][s, t0:t0 + 512, :].rearrange("(a p) d -> p a d", p=128), CB.ld, wr=[CB])
                for st in range(4):
                    pt = ptb()
                    for kc in range(4):
                        op("pe", "transpose", [CB, identb], [pt], out=pt.t[:, kc * 128:(kc + 1) * 128],
                           in_=CB.t[:, st * 512 + kc * 128:st * 512 + (kc + 1) * 128], identity=identb.t[:, :])
                    for kc in range(4):
                        evac_copy(CKVT.t[:, kc, st * 128:(st + 1) * 128], pt.t[:, kc * 128:(kc + 1) * 128], [pt], [CKVT],
                                  eng=("act" if st % 2 else "dve"))
                kv_upproj(pf_groups, pf_segs, 512)
                dma("pool", CB.t[:, 0:256].rearrange("p (a b) -> p a b", a=4),
                    I["c_kr"][s, t0:t0 + 512, :].rearrange("(a p) d -> p a d", p=128), CB.ld, wr=[CB])
                pt = ptb()
                for st in range(4):
                    op("pe", "transpose", [CB, identb], [pt], out=pt.t[:64, st * 128:(st + 1) * 128],
                       in_=CB.t[:, st * 64:(st + 1) * 64], identity=identb.t[:, :])
                sg_ = tb16()
                evac_copy(sg_.t[:64, :], pt.t[:64, 0:512], [pt], [sg_])
                dma("sp", sq_.KrT[:, t0:t0 + 512], sg_.t[:64, :], sg_.st, rd=[sg_], wr=[sq_.kv])
                dma("pool", CB.t[:, :].rearrange("p (a b) -> p a b", a=4),
                    I["c_fv"][s, t0:t0 + 512, :].rearrange("(a p) d -> p a d", p=128), CB.ld, wr=[CB])
                dma("sp", sq_.Vf[t0:t0 + 512, :].rearrange("(a p) d -> p a d", p=128),
                    CB.t[:, :].rearrange("p (a b) -> p a b", a=4), vown, rd=[CB], wr=[sq_.kv])
                dma("pool", CB.t[:, :].rearrange("p (a b) -> p a b", a=4),
                    I["c_fk"][s, t0:t0 + 512, :].rearrange("(a p) d -> p a d", p=128), CB.ld, wr=[CB])
                for h in range(8):
                    pt = ptb()
                    for st in range(4):
                        op("pe", "transpose", [CB, identb], [pt], out=pt.t[:, st * 128:(st + 1) * 128],
                           in_=CB.t[:, st * 1024 + h * 128:st * 1024 + (h + 1) * 128], identity=identb.t[:, :])
                    sg_ = tb16()
                    evac_copy(sg_.t[:, :], pt.t[:, 0:512], [pt], [sg_])
                    dma("sp", sq_.KfT[h, :, t0:t0 + 512], sg_.t[:, :], sg_.st, rd=[sg_], wr=[sq_.kv])
            kb_ = sq_.ktbase
            dma("sp", LOGF.t[:, kb_:kb_ + PAST // 128, :], I["c_lf"][s].rearrange("(a p) h -> p a h", p=128), LOGF.ld,
                wr=[LOGF])
            for kt in range(PAST // 128):
                cum_tile(kb_ + kt, 128, kt == 0, sq_.PREF)
        for s0 in range(0, len(sseqs), 2):
            grp = sseqs[s0:s0 + 2]
            ng = len(grp)
            groups = [dict(seq=q_, pos0=PAST, col0=j * TS, n=TS, first=True, conv_out=O["conv_s"][s0 + j])
                      for j, q_ in enumerate(grp)]
            segs = [dict(off=j * TS, sz=TS, g=j, x=I["xs"][s0 + j], pe=I["ps"][s0 + j],
                         out={k: O[k + "_s"][s0 + j] for k in OK_},
                         kt=q_.ktbase + PAST // 128, first_cum=False) for j, q_ in enumerate(grp)]
            block(s0 // 2, groups, segs, ng * TS, I["ropeC_s"][:, s0 * TS:(s0 + ng) * TS],
                  I["ropeS_s"][:, s0 * TS:(s0 + ng) * TS])

        P.finish()
    return nc


def _rope_tables(pos):
    half = 32
    inv = (np.float32(10000.0) ** (-np.arange(half, dtype=np.float32) / np.float32(half))).astype(np.float32)
    ang = (pos.astype(np.float32)[:, None] * inv[None, :]).astype(np.float32)
    cos, sin = np.cos(ang).astype(np.float32), np.sin(ang).astype(np.float32)
    C = np.concatenate([cos, cos], axis=1).T
    S = np.concatenate([-sin, sin], axis=1).T
    return np.ascontiguousarray(C), np.ascontiguousarray(S)


_CACHE = {}


def run(inputs, n_cores, TP, PAST, NSS):
    f32 = lambda a: np.ascontiguousarray(np.asarray(a, dtype=np.float32))
    x_prompt, x_sample = f32(inputs["x_prompt"]), f32(inputs["x_sample"])
    key = (TP, PAST, NSS)
    if key not in _CACHE:
        _CACHE[key] = build(TP, PAST, NSS)
    nc = _CACHE[key]
    w_in = f32(inputs["w_in"])[0]
    com = {}
    com["w_in_p"] = np.ascontiguousarray(w_in[:, win_perm_cols()])
    w_uq = f32(inputs["w_uq"])[0].reshape(512, 8, 192)
    nope = w_uq[:, :, :128].reshape(512, 1024)
    rp = w_uq[:, :, 128:]
    rp_sw = np.concatenate([rp[:, :, 32:], rp[:, :, :32]], axis=2)
    com["w_uq_p"] = np.ascontiguousarray(np.concatenate([nope, rp.reshape(512, 512), rp_sw.reshape(512, 512)], axis=1))
    w_ukv = f32(inputs["w_ukv"])[0].reshape(512, 8, 256)
    com["w_ukv_p"] = np.ascontiguousarray(
        np.concatenate([w_ukv[:, :, :128].reshape(512, 1024), w_ukv[:, :, 128:].reshape(512, 1024)], axis=1))
    for k in ("w_oa", "w_ob", "w_o", "w_up", "w_down", "w_pg", "w_ple"):
        com[k] = f32(inputs[k])[0]
    for k, src in (("gT_mix", "g_mix"), ("gT_ffn", "g_ffn"), ("gT_ple", "g_ple")):
        com[k] = np.ascontiguousarray(f32(inputs[src])[0].reshape(16, 128).T)
    com["gqT"] = np.ascontiguousarray(f32(inputs["g_q"])[0].reshape(4, 128).T)
    com["gkv_bc"] = np.ascontiguousarray(np.broadcast_to(f32(inputs["g_kv"])[0][None, :], (128, 512)))
    com["gfin_bc"] = np.ascontiguousarray(np.broadcast_to(f32(inputs["g_final"])[None, :], (128, D)))
    com["bf_bc"] = np.ascontiguousarray(np.broadcast_to(f32(inputs["b_f"])[0][None, :], (128, 8)))
    wc = f32(inputs["w_conv"])[0]
    com["wcT"] = np.ascontiguousarray(wc.reshape(3, NFT, 128).transpose(2, 1, 0).reshape(128, NFT * 3))
    com["bcT"] = np.ascontiguousarray(f32(inputs["b_conv"])[0].reshape(NFT, 128).T)
    com["ident"] = np.eye(128, dtype=np.float32)
    com["tri"] = np.triu(np.ones((128, 128), dtype=np.float32))
    C, S = _rope_tables(np.arange(TP))
    com["ropeC_p"], com["ropeS_p"] = C, S
    C, S = _rope_tables(PAST + np.arange(TS))
    com["ropeC_s"] = np.ascontiguousarray(np.tile(C, (1, NSS)))
    com["ropeS_s"] = np.ascontiguousarray(np.tile(S, (1, NSS)))
    pp = f32(inputs["p_prompt"])[0]
    psm = f32(inputs["p_sample"])[0]
    c_ckv = f32(inputs["cache_mla_ckv"])[0]
    c_kr = f32(inputs["cache_mla_krope"])[0]
    c_fk = f32(inputs["cache_fox_k"])[0].reshape(-1, PAST, 1024)
    c_fv = f32(inputs["cache_fox_v"])[0].reshape(-1, PAST, 1024)
    c_lf = f32(inputs["cache_fox_logf"])[0]
    c_conv = f32(inputs["state_ffn_conv"])[0]
    in_maps = []
    for i in range(n_cores):
        m = dict(com)
        m["xp"] = x_prompt[i]
        m["pp"] = pp[i]
        sl = slice(i * NSS, (i + 1) * NSS)
        m["xs"] = x_sample[sl]
        m["ps"] = psm[sl]
        m["c_ckv"], m["c_kr"], m["c_fk"], m["c_fv"] = c_ckv[sl], c_kr[sl], c_fk[sl], c_fv[sl]
        m["c_lf"], m["c_conv"] = c_lf[sl], c_conv[sl]
        in_maps.append(m)
    res = run_bass_kernel_spmd(nc, in_maps, core_ids=list(range(n_cores)))
    R = res.results
    cat = lambda k: np.concatenate([np.asarray(r[k], dtype=np.float32)[None] for r in R], axis=0)
    cats = lambda k: np.concatenate([np.asarray(r[k], dtype=np.float32) for r in R], axis=0)
    B = n_cores
    return (cat("y_p"), cats("y_s"),
            cat("ckv_p")[None], cats("ckv_s")[None],
            cat("kr_p")[None], cats("kr_s")[None],
            cat("fk_p").reshape(1, B, TP, 8, 128), cats("fk_s").reshape(1, B * NSS, TS, 8, 128),
            cat("fv_p").reshape(1, B, TP, 8, 128), cats("fv_s").reshape(1, B * NSS, TS, 8, 128),
            cat("lf_p")[None], cats("lf_s")[None],
            cat("conv_p")[None], cats("conv_s")[None])


def kernel(**inputs):
    return run(inputs, 8, 4096, 2048, 2)
```

```python
import numpy as np
from contextlib import ExitStack
import concourse.bass as bass
import concourse.mybir as mybir
from concourse.bass_utils import run_bass_kernel_spmd

F32 = mybir.dt.float32
BF16 = mybir.dt.bfloat16
AF = mybir.ActivationFunctionType
ALU = mybir.AluOpType

D = 2048
FF = 5632
NFT = 88
EPS = 1e-6
MLA_SCALE = 192.0 ** -0.5
FOX_SCALE = 128.0 ** -0.5
TS = 32
ENGS = ("pe", "act", "dve", "pool", "sp")
SAME_ENGINE_SYNC = True


class Buf:
    def __init__(self, t, name, acc=False):
        self.t = t
        self.name = name
        self.lw = {}
        self.rd = {}
        self.al = []
        self.acc = acc
        self.ld = None
        self.st = None
        self.psum = False

    def __getitem__(self, idx):
        return self.t[idx]


class SemOwner:
    def __init__(self, sem):
        self.sem = sem
        self.cnt = 0


class Prog:
    def __init__(self, nc, es):
        self.nc = nc
        self.es = es
        self.q = {e: [] for e in ENGS}
        self.cnt = {e: 0 for e in ENGS}
        self.sem = {e: es.enter_context(nc.semaphore("sem_" + e)) for e in ENGS}
        self.seen = {e: {} for e in ENGS}
        self.nsem = len(ENGS)
        self.owners = []
        self.nbuf = 0

    def owner(self):
        self.nsem += 1
        o = SemOwner(self.es.enter_context(self.nc.semaphore("dsem%d" % self.nsem)))
        self.owners.append(o)
        return o

    def sb(self, name, shape, dt):
        t = self.es.enter_context(self.nc.sbuf_tensor(name, list(shape), dt))
        return Buf(t, name)

    def psum(self, name, shape, dt):
        t = self.es.enter_context(self.nc.psum_tensor(name, list(shape), dt))
        return Buf(t, name)

    def view(self, base, name):
        return Buf(base.t, name)

    def _collect(self, rd, wr, eng=None):
        toks = {}

        def add(d):
            for k, (s, v) in d.items():
                if k not in toks or toks[k][1] < v:
                    toks[k] = (s, v)
        for b in rd:
            add(b.lw)
            if b.psum:
                add({k: v for k, v in b.rd.items() if k != eng})
        for b in wr:
            if b.acc:
                continue
            add(b.lw)
            add(b.rd)
            for a in b.al:
                add(a.lw)
                add(a.rd)
        return toks

    def _waits(self, eng, toks):
        waits = []
        seen = self.seen[eng]
        for k, (s, v) in toks.items():
            if k == eng and (eng == "pe" or not SAME_ENGINE_SYNC):
                continue
            if seen.get(k, 0) < v:
                seen[k] = v
                waits.append((s, v))
        return waits

    def _commit(self, tok_key, tok, rd, wr):
        for b in wr:
            if b.acc:
                b.lw[tok_key] = tok
            else:
                b.lw = {tok_key: tok}
                b.rd = {}
        for b in rd:
            b.rd[tok_key] = tok

    def op(self, eng, name, rd, wr, *args, **kw):
        waits = self._waits(eng, self._collect(rd, wr, eng))
        self.cnt[eng] += 1
        v = self.cnt[eng]
        sem = self.sem[eng]

        def emit(e, name=name, args=args, kw=kw, waits=waits, sem=sem):
            for (s, val) in waits:
                e.wait_ge(s, val)
            getattr(e, name)(*args, **kw).then_inc(sem, 1)
        self.q[eng].append(emit)
        self._commit(eng, (sem, v), rd, wr)

    def dma(self, eng, out, in_, owner, rd=(), wr=(), **kw):
        waits = self._waits(eng, self._collect(rd, wr, eng))
        owner.cnt += 16
        v = owner.cnt
        sem = owner.sem

        def emit(e, out=out, in_=in_, waits=waits, sem=sem, kw=kw):
            for (s, val) in waits:
                e.wait_ge(s, val)
            e.dma_start(out=out, in_=in_, **kw).then_inc(sem, 16)
        self.q[eng].append(emit)
        self._commit(id(owner), (sem, v), rd, wr)

    def finish(self):
        finals = [(o.sem, o.cnt) for o in self.owners if o.cnt > 0]

        def emit(e, finals=finals):
            for (s, v) in finals:
                e.wait_ge(s, v)
        self.q["sp"].append(emit)
        ecnt = [(self.sem[k], self.cnt[k]) for k in ("pe", "act", "dve", "pool") if self.cnt[k] > 0]

        def emit2(e, ecnt=ecnt):
            for (s, v) in ecnt:
                e.wait_ge(s, v)
        self.q["sp"].append(emit2)
        nc = self.nc
        with nc.Block() as block:
            @block.tensor
            def _(e):
                for c in self.q["pe"]:
                    c(e)

            @block.scalar
            def _(e):
                for c in self.q["act"]:
                    c(e)

            @block.vector
            def _(e):
                for c in self.q["dve"]:
                    c(e)

            @block.gpsimd
            def _(e):
                for c in self.q["pool"]:
                    c(e)

            @block.sync
            def _(e):
                for c in self.q["sp"]:
                    c(e)


def slab_table():
    S = []
    idx = {}

    def add(key, w, nkc, r0, c0, width, grp):
        idx[key] = len(S)
        S.append((w, nkc, r0, c0, width, grp))
    for i in range(10):
        add(("in_fm", i), "w_in_p", 16, 0, i * 256, 256, 0)
    add(("in_kr", 0), "w_in_p", 16, 0, 2560, 128, 0)
    for i in range(16):
        add(("in_g", i), "w_in_p", 16, 0, 2688 + i * 256, 256, 1)
    for i in range(10):
        add(("in_tm", i), "w_in_p", 16, 0, 6784 + i * 256, 256, 2)
    add(("in_fl", 0), "w_in_p", 16, 0, 9344, 8, 2)
    for i in range(2):
        add(("uq", i), "w_uq_p", 4, 0, i * 1024, 1024, 3)
    for i in range(2):
        add(("ukv", i), "w_ukv_p", 4, 0, i * 1024, 1024, 3)
    for i in range(4):
        add(("oa", i), "w_oa", 8, 0, i * 512, 512, 4)
        add(("ob", i), "w_ob", 8, 0, i * 512, 512, 4)
    for i in range(8):
        add(("o", i), "w_o", 16, 0, i * 256, 256, 5)
    for g in range(22):
        add(("upv", g), "w_up", 16, 0, g * 256, 256, 6 + g // 6)
        add(("upg", g), "w_up", 16, 0, FF + g * 256, 256, 6 + g // 6)
        add(("dn", g), "w_down", 2, g * 256, 0, 2048, 6 + g // 6)
    for i in range(8):
        add(("pg", i), "w_pg", 16, 0, i * 256, 256, 10)
    add(("ple", 0), "w_ple", 2, 0, 0, 2048, 10)
    return S, idx


NCP = 9352


def win_perm_cols():
    cq = np.arange(0, 512)
    ckv = np.arange(512, 1024)
    kr = np.arange(1024, 1088)
    fq = np.arange(1088, 2112)
    fk = np.arange(2112, 3136)
    fv = np.arange(3136, 4160)
    fl = np.arange(4160, 4168)
    ga = np.arange(4168, 6216)
    gb = np.arange(6216, 8264)
    kr_sw = np.concatenate([kr[32:], kr[:32]])
    cols = np.concatenate([cq, fq, fk, kr, kr_sw, ga, gb, ckv, fk, fv, fl])
    assert cols.shape[0] == NCP
    return cols


def build(TP, PAST, NSS):
    nc = bass.Bass("TRN2", target_bir_lowering=False)
    CAPP = max(TP, 128)
    CAPS = ((PAST + TS + 127) // 128) * 128
    NKTMAX = max(TP // 128, NSS * (CAPS // 128))

    def din(name, shape, dt=F32):
        return nc.dram_tensor(name, list(shape), dt, kind="ExternalInput").ap()

    def dout(name, shape, dt=F32):
        return nc.dram_tensor(name, list(shape), dt, kind="ExternalOutput").ap()

    def dint(name, shape, dt=BF16):
        return nc.dram_tensor(name, list(shape), dt, kind="Internal").ap()

    I = {}
    I["xp"] = din("xp", [TP, D])
    I["pp"] = din("pp", [TP, 256])
    I["xs"] = din("xs", [NSS, TS, D])
    I["ps"] = din("ps", [NSS, TS, 256])
    I["c_ckv"] = din("c_ckv", [NSS, PAST, 512])
    I["c_kr"] = din("c_kr", [NSS, PAST, 64])
    I["c_fk"] = din("c_fk", [NSS, PAST, 1024])
    I["c_fv"] = din("c_fv", [NSS, PAST, 1024])
    I["c_lf"] = din("c_lf", [NSS, PAST, 8])
    I["c_conv"] = din("c_conv", [NSS, 2, 2 * FF])
    WSRC = {
        "w_in_p": din("w_in_p", [D, NCP]), "w_uq_p": din("w_uq_p", [512, 2048]),
        "w_ukv_p": din("w_ukv_p", [512, 2048]), "w_oa": din("w_oa", [1024, D]),
        "w_ob": din("w_ob", [1024, D]), "w_o": din("w_o", [D, D]), "w_up": din("w_up", [D, 2 * FF]),
        "w_down": din("w_down", [FF, D]), "w_pg": din("w_pg", [D, D]), "w_ple": din("w_ple", [256, D]),
    }
    for nm, shp in (("gT_mix", [128, 16]), ("gT_ffn", [128, 16]), ("gT_ple", [128, 16]), ("gqT", [128, 4]),
                    ("gkv_bc", [128, 512]), ("gfin_bc", [128, D]), ("bf_bc", [128, 8]),
                    ("wcT", [128, NFT * 3]), ("bcT", [128, NFT]), ("ident", [128, 128]), ("tri", [128, 128]),
                    ("ropeC_p", [64, TP]), ("ropeS_p", [64, TP]), ("ropeC_s", [64, NSS * TS]), ("ropeS_s", [64, NSS * TS])):
        I[nm] = din(nm, shp)
    O = {}
    O["y_p"] = dout("y_p", [TP, D])
    O["y_s"] = dout("y_s", [NSS, TS, D])
    O["ckv_p"] = dout("ckv_p", [TP, 512])
    O["ckv_s"] = dout("ckv_s", [NSS, TS, 512])
    O["kr_p"] = dout("kr_p", [TP, 64])
    O["kr_s"] = dout("kr_s", [NSS, TS, 64])
    O["fk_p"] = dout("fk_p", [TP, 1024])
    O["fk_s"] = dout("fk_s", [NSS, TS, 1024])
    O["fv_p"] = dout("fv_p", [TP, 1024])
    O["fv_s"] = dout("fv_s", [NSS, TS, 1024])
    O["lf_p"] = dout("lf_p", [TP, 8])
    O["lf_s"] = dout("lf_s", [NSS, TS, 8])
    O["conv_p"] = dout("conv_p", [2, 2 * FF])
    O["conv_s"] = dout("conv_s", [NSS, 2, 2 * FF])

    SLABS, SIDX = slab_table()
    WS = dint("ws", [len(SLABS), 128, 4096])
    SGD = dint("sgd", [2, 16, 128, 512])

    es = ExitStack()
    with es:
        P = Prog(nc, es)
        op, dma = P.op, P.dma

        def mk(t, name, ld=False, st=False):
            b = Buf(t, name)
            if ld:
                b.ld = P.owner()
            if st:
                b.st = P.owner()
            return b

        def sbd(name, shape, dt, ld=False, st=False):
            t = es.enter_context(nc.sbuf_tensor("sb_" + name, list(shape), dt))
            return mk(t, name, ld, st)

        ident32 = sbd("ident32", [128, 128], F32, ld=True)
        identb = sbd("identb", [128, 128], BF16)
        ones32 = sbd("ones32", [128, 128], F32)
        onesb = sbd("onesb", [128, 128], BF16)
        tri32 = sbd("tri32", [128, 128], F32, ld=True)
        trib = sbd("trib", [128, 128], BF16)
        gT = {k: sbd(k, [128, 16], F32, ld=True) for k in ("gT_mix", "gT_ffn", "gT_ple")}
        gqT = sbd("gqT", [128, 4], F32, ld=True)
        gkv_bc = sbd("gkv_bc", [128, 512], F32, ld=True)
        bf_bc = sbd("bf_bc", [128, 8], F32, ld=True)
        wcT = sbd("wcT", [128, NFT, 3], F32, ld=True)
        bcT = sbd("bcT", [128, NFT], F32, ld=True)
        ropeC = sbd("ropeC", [64, 512], F32, ld=True)
        ropeS = sbd("ropeS", [64, 512], F32, ld=True)
        arena = es.enter_context(nc.sbuf_tensor("arena", [128, 8192], F32))
        H = [mk(arena[:, i * 2048:(i + 1) * 2048], "H%d" % i, ld=True, st=True) for i in range(4)]
        qviews = [mk(arena[:, i * 2048:(i + 1) * 2048].bitcast(BF16).rearrange("p (a b) -> p a b", a=8), nm)
                  for i, nm in enumerate(("QN", "QF", "OA", "OB"))]
        QN, QF, OA, OB = qviews
        for i in range(4):
            H[i].al = [qviews[i]]
            qviews[i].al = [H[i]]
        QR = sbd("QR", [64, 8, 512], BF16)
        G = [sbd("G%d" % i, [128, 2, 512], BF16) for i in range(2)]
        AB = [sbd("Abuf%d" % i, [128, 16, 512], BF16) for i in range(2)]
        XS = sbd("XS", [128, D], F32, ld=True)
        CB = mk(XS.t[:, :].bitcast(BF16), "CB", ld=True)
        CB.al = [XS]
        XS.al = [CB]
        XNS = [sbd("XN%d" % i, [128, D], BF16) for i in range(2)]
        XN = XNS[0]
        NWB = 5
        WB = [sbd("WB%d" % i, [128, 4096], BF16, ld=True) for i in range(NWB)]
        NT32 = 8
        T32 = [sbd("T32_%d" % i, [128, 512], F32, ld=True, st=True) for i in range(NT32)]
        NTB = 8
        TB16 = [sbd("TB16_%d" % i, [128, 512], BF16, ld=True, st=True) for i in range(NTB)]
        CQH = [sbd("CQH%d" % i, [128, 512], F32) for i in range(2)]
        CQN = sbd("CQN", [128, 4, 512], BF16)
        CKVT = sbd("CKVT", [128, 4, 512], BF16)
        KB = [sbd("KB%d" % i, [128, 1024], BF16, ld=True) for i in range(2)]
        VB = [sbd("VB%d" % i, [128, 8, 256], BF16, ld=True) for i in range(2)]
        CQ32 = []
        for i in range(4):
            v = VB[i // 2]
            c = mk(v.t[:, :, :].rearrange("p a b -> p (a b)")[:, (i % 2) * 1024:(i % 2 + 1) * 1024].bitcast(F32),
                   "CQ32_%d" % i)
            c.al = [v]
            v.al = v.al + [c]
            CQ32.append(c)
        KRB = [sbd("KRB%d" % i, [128, 1024], BF16, ld=True) for i in range(2)]
        for i in range(2):
            g_ = mk(KRB[i].t[:, :].rearrange("p (a b) -> p a b", a=2), "G%d" % (2 + i))
            g_.al = [KRB[i]]
            KRB[i].al = KRB[i].al + [g_]
            G.append(g_)
        U = [sbd("U%d" % i, [128, 516], F32) for i in range(2)]
        LOGF = sbd("LOGF", [128, NKTMAX + 1, 8], F32, ld=True, st=True)
        CUM = sbd("CUM", [128, NKTMAX + 1, 8], F32)
        NEGCUM = sbd("NEGCUM", [128, NKTMAX + 1, 8], F32)
        PREFS = [sbd("PREF%d" % i, [128, 8], F32) for i in range(2)]
        CONVSTS = [sbd("CONVST%d" % i, [128, NFT, 2], F32) for i in range(2)]
        ST2 = sbd("ST2", [2, 512], F32, ld=True, st=True)
        SMALL = sbd("SMALL", [128, 16], F32)
        PET = sbd("PET", [128, 2, 512], BF16)
        SGB = [[Buf(None, "sgb%d_%d" % (g, m)) for m in range(16)] for g in range(2)]
        NPS = 6
        PS = [mk(es.enter_context(nc.psum_tensor("PS%d" % i, [128, 512], F32)), "PS%d" % i) for i in range(NPS)]
        PT = [mk(es.enter_context(nc.psum_tensor("PTb%d" % i, [128, 1024], BF16)), "PT%d" % i) for i in range(2)]
        PSX = []
        for i in range(2):
            v = mk(PT[i].t[:, :].bitcast(F32), "PSX%d" % i)
            v.al = [PT[i]]
            PT[i].al = [v]
            PSX.append(v)
        for b_ in PS + PT + PSX:
            b_.psum = True
        rr = {"ps": 0, "psn": NPS, "t32": 0, "tb": 0, "pt": 0, "u": 0, "ev": 0, "w": 0}

        def ps():
            rr["ps"] = (rr["ps"] + 1) % rr["psn"]
            return PS[rr["ps"]]

        def t32():
            rr["t32"] = (rr["t32"] + 1) % NT32
            return T32[rr["t32"]]

        def tb16():
            rr["tb"] = (rr["tb"] + 1) % NTB
            return TB16[rr["tb"]]

        def ptb():
            rr["pt"] = (rr["pt"] + 1) % 2
            return PT[rr["pt"]]

        for buf, src in ((ident32, I["ident"]), (tri32, I["tri"]), (gT["gT_mix"], I["gT_mix"]),
                         (gT["gT_ffn"], I["gT_ffn"]), (gT["gT_ple"], I["gT_ple"]), (gqT, I["gqT"]),
                         (gkv_bc, I["gkv_bc"]), (bf_bc, I["bf_bc"]), (bcT, I["bcT"])):
            dma("sp", buf.t[:, :], src, buf.ld, wr=[buf])
        dma("sp", wcT.t[:, :, :].rearrange("p a b -> p (a b)"), I["wcT"], wcT.ld, wr=[wcT])
        op("dve", "tensor_copy", [ident32], [identb], out=identb.t[:, :], in_=ident32.t[:, :])
        op("dve", "tensor_copy", [tri32], [trib], out=trib.t[:, :], in_=tri32.t[:, :])
        op("dve", "memset", [], [ones32], ones32.t[:, :], 1.0)
        op("dve", "memset", [], [onesb], onesb.t[:, :], 1.0)

        NGRP = 11
        gowner = [P.owner() for _ in range(NGRP)]
        WSB = [Buf(None, "wsgrp%d" % g, acc=True) for g in range(NGRP)]
        converted = set()

        import os as _os
        STOP = _os.environ.get("K_STOP", "")

        def wslab(key):
            si = SIDX[key]
            wn, nkc, r0, c0, width, grp = SLABS[si]
            b = WB[rr["w"] % NWB]
            rr["w"] += 1
            if si not in converted:
                converted.add(si)
                wsrc = WSRC[wn][r0:r0 + nkc * 128, c0:c0 + width].rearrange("(k p) c -> p k c", p=128)
                dma("pool", b.t[:, 0:nkc * width].rearrange("p (k c) -> p k c", k=nkc), wsrc, b.ld, wr=[b])
                dma("sp", WS[si, :, 0:nkc * width], b.t[:, 0:nkc * width], gowner[grp], rd=[b], wr=[WSB[grp]])
            else:
                dma("pool", b.t[:, 0:nkc * width], WS[si, :, 0:nkc * width], b.ld, rd=[WSB[grp]], wr=[b])
            return b, b.t[:, 0:nkc * width].rearrange("p (k c) -> p k c", k=nkc)

        def evac_copy(out_ap, in_ap, rd, wr, eng=None):
            if eng is None:
                rr["ev"] += 1
                eng = "act" if rr["ev"] % 2 else "dve"
            if eng == "act":
                op("act", "activation", rd, wr, out=out_ap, in_=in_ap, func=AF.Copy)
            else:
                op("dve", "tensor_copy", rd, wr, out=out_ap, in_=in_ap)

        def rstd_from_ss(ss_ap, out_ap, n):
            op("act", "activation", [SMALL], [SMALL], out=out_ap, in_=ss_ap, func=AF.Sqrt, bias=EPS, scale=1.0 / n)
            op("dve", "reciprocal", [SMALL], [SMALL], out=out_ap, in_=out_ap)

        def norm_transpose(Abuf, src_fn, srcbufs, subt, gTb, load=None):
            for st, (off, sz) in enumerate(subt):
                if load is not None:
                    load(st, off, sz)
                src = src_fn(st, sz)
                sb = srcbufs[st]
                rr["xn"] = rr.get("xn", 0) + 1
                XN = XNS[rr["xn"] % 2]
                ss = SMALL.t[:sz, st:st + 1]
                rs = SMALL.t[:sz, 8 + st:9 + st]
                op("act", "activation", [sb], [XN, SMALL], out=XN.t[:sz, :], in_=src, func=AF.Square, accum_out=ss)
                rstd_from_ss(ss, rs, D)
                op("act", "activation", [sb, SMALL], [XN], out=XN.t[:sz, :], in_=src, func=AF.Copy, scale=rs)
                for q4 in range(4):
                    pt = ptb()
                    for j in range(4):
                        kc = q4 * 4 + j
                        op("pe", "transpose", [XN, identb], [pt], out=pt.t[:, j * 128:j * 128 + sz],
                           in_=XN.t[:sz, kc * 128:(kc + 1) * 128], identity=identb.t[:sz, :sz])
                    for j in range(4):
                        kc = q4 * 4 + j
                        if True:
                            op("dve", "tensor_scalar", [pt, gTb], [Abuf], out=Abuf.t[:, kc, off:off + sz],
                               in0=pt.t[:, j * 128:j * 128 + sz], scalar1=gTb.t[:, kc:kc + 1], scalar2=None,
                               op0=ALU.mult)
                        else:
                            op("act", "activation", [pt, gTb], [Abuf], out=Abuf.t[:, kc, off:off + sz],
                               in_=pt.t[:, j * 128:j * 128 + sz], func=AF.Copy, scale=gTb.t[:, kc:kc + 1])

        def mm_fm(psb, W, wbuf, c0, m, rhs_fn, rhsbufs, nkc, n):
            for kc in range(nkc):
                op("pe", "matmul", [wbuf] + rhsbufs, [psb], psb.t[:m, 0:n], lhsT=W[:, kc, c0:c0 + m], rhs=rhs_fn(kc),
                   start=(kc == 0), stop=(kc == nkc - 1))

        def mm_tm(psb, pc0, W, wbuf, c0, width, lhs_fn, lhsbufs, nkc, sz):
            for kc in range(nkc):
                op("pe", "matmul", [wbuf] + lhsbufs, [psb], psb.t[:sz, pc0:pc0 + width], lhsT=lhs_fn(kc),
                   rhs=W[:, kc, c0:c0 + width], start=(kc == 0), stop=(kc == nkc - 1))

        def rope_fm(ps1, ps2, n, out_ap, outbuf):
            a = t32()
            b = t32()
            op("dve", "tensor_tensor", [ps1, ropeC], [a], out=a.t[:64, :n], in0=ps1.t[:64, :n], in1=ropeC.t[:64, :n],
               op=ALU.mult)
            op("dve", "tensor_tensor", [ps2, ropeS], [b], out=b.t[:64, :n], in0=ps2.t[:64, :n], in1=ropeS.t[:64, :n],
               op=ALU.mult)
            op("dve", "tensor_tensor", [a, b], [outbuf], out=out_ap, in0=a.t[:64, :n], in1=b.t[:64, :n], op=ALU.add)

        class Seq:
            pass

        def make_seq(name, cap):
            s = Seq()
            s.cap = cap
            s.KnT = dint(name + "_knt", [8, 128, cap])
            s.KrT = dint(name + "_krt", [64, cap])
            s.KfT = dint(name + "_kft", [8, 128, cap])
            s.Vm = dint(name + "_vm", [cap, 1024])
            s.Vf = dint(name + "_vf", [cap, 1024])
            s.kv = Buf(None, name + "_kv", acc=True)
            return s

        def cum_tile(kt, sz, first, PREF):
            p1 = ps()
            op("pe", "matmul", [tri32, LOGF], [p1], p1.t[:sz, 0:8], lhsT=tri32.t[:sz, :sz], rhs=LOGF.t[:sz, kt, :],
               start=True, stop=True)
            if first:
                op("dve", "tensor_copy", [p1], [CUM], out=CUM.t[:sz, kt, :], in_=p1.t[:sz, 0:8])
            else:
                op("dve", "tensor_tensor", [p1, PREF], [CUM], out=CUM.t[:sz, kt, :], in0=p1.t[:sz, 0:8],
                   in1=PREF.t[:sz, :], op=ALU.add)
            op("dve", "tensor_scalar", [CUM], [NEGCUM], out=NEGCUM.t[:sz, kt, :], in0=CUM.t[:sz, kt, :],
               scalar1=-1.0, scalar2=None, op0=ALU.mult)
            p2 = ps()
            op("pe", "matmul", [ones32, LOGF], [p2], p2.t[:, 0:8], lhsT=ones32.t[:sz, :], rhs=LOGF.t[:sz, kt, :],
               start=True, stop=True)
            if first:
                op("dve", "tensor_copy", [p2], [PREF], out=PREF.t[:, :], in_=p2.t[:, 0:8])
            else:
                op("dve", "tensor_tensor", [p2, PREF], [PREF], out=PREF.t[:, :], in0=p2.t[:, 0:8], in1=PREF.t[:, :],
                   op=ALU.add)

        def store_groups(s, groups, dst_fn, part=128):
            for G_ in groups:
                dma("sp", dst_fn(G_), s.t[:part, G_["col0"]:G_["col0"] + G_["n"]], s.st, rd=[s], wr=[G_["seq"].kv])
            for G_ in groups:
                G_["seq"].kv.lw[id(s.st)] = (s.st.sem, s.st.cnt)

        def kv_upproj(groups, segs, n):
            wb0, W0 = wslab(("ukv", 0))
            for h in range(8):
                p = ps()
                mm_fm(p, W0, wb0, h * 128, 128, lambda kc: CKVT.t[:, kc, 0:n], [CKVT], 4, n)
                s = tb16()
                evac_copy(s.t[:, 0:n], p.t[:, 0:n], [p], [s])
                store_groups(s, groups, lambda G_: G_["seq"].KnT[h, :, G_["pos0"]:G_["pos0"] + G_["n"]])
            wb1, W1 = wslab(("ukv", 1))
            for S_ in segs:
                off, sz = S_["off"], S_["sz"]
                G_ = groups[S_["g"]]
                r0 = G_["pos0"] + off - G_["col0"]
                for cg in range(2):
                    p = ps()
                    mm_tm(p, 0, W1, wb1, cg * 512, 512, lambda kc: CKVT.t[:, kc, off:off + sz], [CKVT], 4, sz)
                    s = tb16()
                    evac_copy(s.t[:sz, :], p.t[:sz, :], [p], [s])
                    dma("sp", G_["seq"].Vm[r0:r0 + sz, cg * 512:(cg + 1) * 512], s.t[:sz, :], s.st,
                        rd=[s], wr=[G_["seq"].kv])

        def attention(seq, q0, nq, mla, col0=0):
            ktb = seq.ktbase
            nkeys = q0 + nq
            nkt = (nkeys + 127) // 128
            nch = (nkt + 7) // 8
            units = [(h, c) for h in range(8) for c in range(nch)]
            KT = seq.KnT if mla else seq.KfT
            Vd = seq.Vm if mla else seq.Vf
            Ob = OA if mla else OB
            loaded = {}
            SBK = [PS[0], PS[1], PSX[0], PSX[1]]
            sst = {"i": 0}

            def ps():
                sst["i"] = (sst["i"] + 1) % 4
                return SBK[sst["i"]]

            def issue(ui):
                if ui >= len(units):
                    return
                h, c = units[ui]
                k0 = c * 1024
                kn = min(1024, nkeys - k0)
                kb = KB[ui % 2]
                dma("sp", kb.t[:, 0:kn], KT[h, :, k0:k0 + kn], kb.ld, rd=[seq.kv], wr=[kb])
                vb = VB[ui % 2]
                nfull = kn // 128
                hp = h // 2
                if nfull > 0:
                    dma("sp", vb.t[:, 0:nfull, :],
                        Vd[k0:k0 + nfull * 128, hp * 256:(hp + 1) * 256].rearrange("(kt p) d -> p kt d", p=128),
                        vb.ld, rd=[seq.kv], wr=[vb])
                rem = kn - nfull * 128
                if rem > 0:
                    dma("sp", vb.t[0:rem, nfull, :], Vd[k0 + nfull * 128:k0 + kn, hp * 256:(hp + 1) * 256],
                        vb.ld, rd=[seq.kv], wr=[vb])
                krb = None
                if mla:
                    krb = KRB[ui % 2]
                    dma("sp", krb.t[:64, 0:kn], seq.KrT[:, k0:k0 + kn], krb.ld, rd=[seq.kv], wr=[krb])
                loaded[ui] = (kb, vb, krb)

            issue(0)
            issue(1)
            jobs = []
            for ui, (h, c) in enumerate(units):
                ktn = min(8, nkt - c * 8)
                tl = []
                for kl in range(ktn):
                    kt = c * 8 + kl
                    ks = kt * 128
                    ksz = min(128, nkeys - ks)
                    j0 = max(0, ks - q0)
                    if j0 >= nq:
                        continue
                    tl.append((kl, kt, ks, ksz, j0))
                for i, (kl, kt, ks, ksz, j0) in enumerate(tl):
                    jobs.append(dict(ui=ui, h=h, c=c, kl=kl, kt=kt, ks=ks, ksz=ksz, j0=j0,
                                     first_of_unit=(i == 0), last_of_unit=(i == len(tl) - 1)))
            state = {}

            def stage1(J):
                ui, h, c, kl, kt, ks, ksz, j0 = (J[k] for k in ("ui", "h", "c", "kl", "kt", "ks", "ksz", "j0"))
                kb, vb, krb = loaded[ui]
                if c == 0 and J["first_of_unit"]:
                    state[h] = dict(po=PS[2 + 2 * (h % 2)], pd=PS[3 + 2 * (h % 2)], cq=None)
                    if not mla:
                        cq = CQH[h % 2]
                        pq = ps()
                        for off in range(0, nq, 128):
                            sz = min(128, nq - off)
                            kt_q = (q0 + off) // 128
                            dg = t32()
                            op("dve", "tensor_scalar", [ident32, CUM], [dg], out=dg.t[:sz, :sz],
                               in0=ident32.t[:sz, :sz], scalar1=CUM.t[:sz, ktb + kt_q, h:h + 1], scalar2=None,
                               op0=ALU.mult)
                            op("pe", "matmul", [ones32, dg], [pq], pq.t[:, off:off + sz], lhsT=ones32.t[:sz, :],
                               rhs=dg.t[:sz, :sz], start=True, stop=True)
                        evac_copy(cq.t[:, 0:nq], pq.t[:, 0:nq], [pq], [cq], eng="act")
                        state[h]["cq"] = cq
                cq = state[h]["cq"]
                diag = (ks + ksz - 1) > (q0 + j0)
                sp_ = ps()
                if mla:
                    op("pe", "matmul", [kb, QN], [sp_], sp_.t[:ksz, j0:nq], lhsT=kb.t[:, kl * 128:kl * 128 + ksz],
                       rhs=QN.t[:, h, col0 + j0:col0 + nq], start=True, stop=False)
                    op("pe", "matmul", [krb, QR], [sp_], sp_.t[:ksz, j0:nq],
                       lhsT=krb.t[:64, kl * 128:kl * 128 + ksz], rhs=QR.t[:, h, col0 + j0:col0 + nq], start=False, stop=True)
                else:
                    op("pe", "matmul", [kb, QF], [sp_], sp_.t[:ksz, j0:nq], lhsT=kb.t[:, kl * 128:kl * 128 + ksz],
                       rhs=QF.t[:, h, col0 + j0:col0 + nq], start=True, stop=True)
                pt_ = tb16()
                if mla:
                    op("act", "activation", [sp_], [pt_], out=pt_.t[:ksz, j0:nq], in_=sp_.t[:ksz, j0:nq],
                       func=AF.Exp, scale=MLA_SCALE)
                    if diag and ksz > 64:
                        op("dve", "memset", [], [pt_], pt_.t[64:128, j0:j0 + 64], 0.0)
                else:
                    tmp = t32()
                    op("dve", "scalar_tensor_tensor", [sp_, cq], [tmp], out=tmp.t[:ksz, j0:nq],
                       in0=sp_.t[:ksz, j0:nq], scalar=FOX_SCALE, in1=cq.t[:ksz, j0:nq], op0=ALU.mult, op1=ALU.add)
                    op("act", "activation", [tmp, NEGCUM], [pt_], out=pt_.t[:ksz, j0:nq], in_=tmp.t[:ksz, j0:nq],
                       func=AF.Exp, bias=NEGCUM.t[:ksz, ktb + kt, h:h + 1])
                    if diag:
                        msz = min(ksz, nq - j0)
                        op("dve", "tensor_tensor", [pt_, trib], [pt_], out=pt_.t[:ksz, j0:j0 + msz],
                           in0=pt_.t[:ksz, j0:j0 + msz], in1=trib.t[:ksz, :msz], op=ALU.mult)
                J["pt"] = pt_

            def stage2(J):
                ui, h, c, kl, kt, ksz, j0 = (J[k] for k in ("ui", "h", "c", "kl", "kt", "ksz", "j0"))
                kb, vb, krb = loaded[ui]
                po, pd = state[h]["po"], state[h]["pd"]
                pt_ = J["pt"]
                first = (kt == 0)
                last = (kt == nkt - 1)
                vcol = (h % 2) * 128
                op("pe", "matmul", [vb, pt_], [po], po.t[:, j0:nq], lhsT=vb.t[:ksz, kl, vcol:vcol + 128],
                   rhs=pt_.t[:ksz, j0:nq], start=first, stop=last)
                op("pe", "matmul", [onesb, pt_], [pd], pd.t[:, j0:nq], lhsT=onesb.t[:ksz, :],
                   rhs=pt_.t[:ksz, j0:nq], start=first, stop=last)
                if J["last_of_unit"]:
                    issue(ui + 2)
                    if c == nch - 1:
                        rd_ = t32()
                        op("act", "activation", [pd], [rd_], out=rd_.t[:, 0:nq], in_=pd.t[:, 0:nq], func=AF.Ln)
                        op("act", "activation", [rd_], [rd_], out=rd_.t[:, 0:nq], in_=rd_.t[:, 0:nq], func=AF.Exp,
                           scale=-1.0)
                        op("dve", "tensor_tensor", [po, rd_], [Ob], out=Ob.t[:, h, col0:col0 + nq], in0=po.t[:, 0:nq],
                           in1=rd_.t[:, 0:nq], op=ALU.mult)

            pending = []
            for i, J in enumerate(jobs):
                if J["first_of_unit"]:
                    while pending and pending[0]["ui"] <= J["ui"] - 2:
                        stage2(pending.pop(0))
                stage1(J)
                pending.append(J)
                if len(pending) > 2:
                    stage2(pending.pop(0))
            while pending:
                stage2(pending.pop(0))

        def stage_a(Adst, segs):
            subt = [(S_["off"], S_["sz"]) for S_ in segs]

            def load_x(st, off, sz):
                dma("sp", XS.t[:sz, :], segs[st]["x"], XS.ld, wr=[XS])
            norm_transpose(Adst, lambda st, sz: XS.t[:sz, :], [XS] * len(subt), subt, gT["gT_mix"], load=load_x)

        def fk_fm(Asrc, groups, n, slabs):
            for i in slabs:
                wb, W = wslab(("in_fm", i))
                for tt in range(2):
                    h = i * 2 + tt - 12
                    p = ps()
                    mm_fm(p, W, wb, tt * 128, 128, lambda kc: Asrc.t[:, kc, 0:n], [Asrc], 16, n)
                    s = tb16()
                    evac_copy(s.t[:, 0:n], p.t[:, 0:n], [p], [s])
                    store_groups(s, groups, lambda G_: G_["seq"].KfT[h, :, G_["pos0"]:G_["pos0"] + G_["n"]])

        def block(bi, groups, segs, n, ropeC_src, ropeS_src, a_done=False, next_a=None, skip_fk=False, next_b=None):
            subt = [(S_["off"], S_["sz"]) for S_ in segs]
            nst = len(subt)
            Abuf = AB[bi % 2]
            Abuf2 = AB[(bi + 1) % 2]
            dma("sp", ropeC.t[:, 0:n], ropeC_src, ropeC.ld, wr=[ropeC])
            dma("sp", ropeS.t[:, 0:n], ropeS_src, ropeS.ld, wr=[ropeS])

            if not a_done:
                stage_a(Abuf, segs)

            def A(kc):
                return Abuf.t[:, kc, 0:n]

            def At(off, sz):
                return lambda kc: Abuf.t[:, kc, off:off + sz]
            if STOP == "A":
                return
            for i in range(6):
                wb, W = wslab(("in_fm", i))
                for tt in range(2):
                    t = i * 2 + tt
                    p = ps()
                    mm_fm(p, W, wb, tt * 128, 128, A, [Abuf], 16, n)
                    if t < 4:
                        evac_copy(CQ32[t].t[:, 0:n], p.t[:, 0:n], [p], [CQ32[t]], eng="act")
                    else:
                        evac_copy(QF.t[:, t - 4, 0:n], p.t[:, 0:n], [p], [QF])
            if not skip_fk:
                fk_fm(Abuf, groups, n, range(6, 10))
            if STOP == "B1":
                return
            wb, W = wslab(("in_kr", 0))
            p1, p2 = ps(), ps()
            mm_fm(p1, W, wb, 0, 64, A, [Abuf], 16, n)
            mm_fm(p2, W, wb, 64, 64, A, [Abuf], 16, n)
            kr32 = t32()
            rope_fm(p1, p2, n, kr32.t[:64, 0:n], kr32)
            s = tb16()
            evac_copy(s.t[:64, 0:n], kr32.t[:64, 0:n], [kr32], [s])
            store_groups(s, groups, lambda G_: G_["seq"].KrT[:, G_["pos0"]:G_["pos0"] + G_["n"]], part=64)
            for st, (off, sz) in enumerate(subt):
                p = ps()
                op("pe", "transpose", [kr32, ident32], [p], out=p.t[:sz, 0:64], in_=kr32.t[:64, off:off + sz],
                   identity=ident32.t[:64, :64])
                o32 = t32()
                evac_copy(o32.t[:sz, 0:64], p.t[:sz, 0:64], [p], [o32])
                dma("sp", segs[st]["out"]["kr"], o32.t[:sz, 0:64], o32.st, rd=[o32])
            if STOP == "B2":
                return
            for gi in range(2):
                for i in range(8):
                    wb, W = wslab(("in_g", gi * 8 + i))
                    for tt in range(2):
                        m = i * 2 + tt
                        p = ps()
                        mm_fm(p, W, wb, tt * 128, 128, A, [Abuf], 16, n)
                        s = tb16()
                        op("act", "activation", [p], [s], out=s.t[:, 0:n], in_=p.t[:, 0:n], func=AF.Sigmoid)
                        dma("sp", SGD[gi, m, :, 0:n], s.t[:, 0:n], s.st, rd=[s], wr=[SGB[gi][m]])
            if STOP == "B3":
                return
            wbs = [wslab(("in_tm", half)) for half in range(2)]
            for st, (off, sz) in enumerate(subt):
                p = ps()
                for half in range(2):
                    mm_tm(p, half * 256, wbs[half][1], wbs[half][0], 0, 256, At(off, sz), [Abuf], 16, sz)
                junk = t32()
                ss = SMALL.t[:sz, st:st + 1]
                rs = SMALL.t[:sz, 8 + st:9 + st]
                op("act", "activation", [p], [junk, SMALL], out=junk.t[:sz, :], in_=p.t[:sz, :], func=AF.Square,
                   accum_out=ss)
                rstd_from_ss(ss, rs, 512)
                o32 = t32()
                op("dve", "scalar_tensor_tensor", [p, SMALL, gkv_bc], [o32], out=o32.t[:sz, :], in0=p.t[:sz, :],
                   scalar=rs, in1=gkv_bc.t[:sz, :], op0=ALU.mult, op1=ALU.mult)
                dma("sp", segs[st]["out"]["ckv"], o32.t[:sz, :], o32.st, rd=[o32])
                cb = tb16()
                evac_copy(cb.t[:sz, :], o32.t[:sz, :], [o32], [cb], eng="act")
                pt = ptb()
                for kc in range(4):
                    op("pe", "transpose", [cb, identb], [pt], out=pt.t[:, kc * 128:kc * 128 + sz],
                       in_=cb.t[:sz, kc * 128:(kc + 1) * 128], identity=identb.t[:sz, :sz])
                for kc in range(4):
                    evac_copy(CKVT.t[:, kc, off:off + sz], pt.t[:, kc * 128:kc * 128 + sz], [pt], [CKVT], eng="dve")
            if STOP == "B4":
                return
            def fkv_tm(which):
                for cg in range(2):
                    wbs = [wslab(("in_tm", 2 + which * 4 + cg * 2 + half)) for half in range(2)]
                    for st, (off, sz) in enumerate(subt):
                        p = ps()
                        for half in range(2):
                            mm_tm(p, half * 256, wbs[half][1], wbs[half][0], 0, 256, At(off, sz), [Abuf], 16, sz)
                        o32 = t32()
                        rr["ev"] += 1
                        eng_ = "act" if rr["ev"] % 2 else "dve"
                        evac_copy(o32.t[:sz, :], p.t[:sz, :], [p], [o32], eng=eng_)
                        dst = segs[st]["out"]["fk"] if which == 0 else segs[st]["out"]["fv"]
                        dma("sp", dst[:, cg * 512:(cg + 1) * 512], o32.t[:sz, :], o32.st, rd=[o32])
                        if which == 1:
                            vb_ = tb16()
                            evac_copy(vb_.t[:sz, :], p.t[:sz, :], [p], [vb_], eng=eng_)
                            G_ = groups[segs[st]["g"]]
                            r0 = G_["pos0"] + off - G_["col0"]
                            dma("sp", G_["seq"].Vf[r0:r0 + sz, cg * 512:(cg + 1) * 512], vb_.t[:sz, :],
                                vb_.st, rd=[vb_], wr=[G_["seq"].kv])
            if STOP == "B5":
                return
            fkv_tm(1)
            wb, W = wslab(("in_fl", 0))
            for st, (off, sz) in enumerate(subt):
                p = ps()
                mm_tm(p, 0, W, wb, 0, 8, At(off, sz), [Abuf], 16, sz)
                kt = segs[st]["kt"]
                a = t32()
                op("dve", "tensor_tensor", [p, bf_bc], [a], out=a.t[:sz, 0:8], in0=p.t[:sz, 0:8], in1=bf_bc.t[:sz, :],
                   op=ALU.add)
                op("act", "activation", [a], [a], out=a.t[:sz, 8:16], in_=a.t[:sz, 0:8], func=AF.Exp, scale=-1.0)
                op("act", "activation", [a], [a], out=a.t[:sz, 16:24], in_=a.t[:sz, 8:16], func=AF.Ln, bias=1.0)
                op("dve", "tensor_scalar", [a], [LOGF], out=LOGF.t[:sz, kt, :], in0=a.t[:sz, 16:24], scalar1=-1.0,
                   scalar2=None, op0=ALU.mult)
                dma("sp", segs[st]["out"]["lf"], LOGF.t[:sz, kt, :], LOGF.st, rd=[LOGF])
                cum_tile(kt, sz, segs[st]["first_cum"], groups[segs[st]["g"]]["seq"].PREF)
            if STOP == "B6":
                return
            sq = [t32() for _ in range(4)]
            for kc in range(4):
                op("act", "activation", [CQ32[kc]], [sq[kc]], out=sq[kc].t[:, 0:n], in_=CQ32[kc].t[:, 0:n],
                   func=AF.Square)
            pq = ps()
            for kc in range(4):
                op("pe", "matmul", [ones32, sq[kc]], [pq], pq.t[:, 0:n], lhsT=ones32.t[:, :], rhs=sq[kc].t[:, 0:n],
                   start=(kc == 0), stop=(kc == 3))
            rb = t32()
            op("act", "activation", [pq], [rb], out=rb.t[:, 0:n], in_=pq.t[:, 0:n], func=AF.Sqrt, bias=EPS,
               scale=1.0 / 512)
            op("dve", "reciprocal", [rb], [rb], out=rb.t[:, 0:n], in_=rb.t[:, 0:n])
            for kc in range(4):
                op("dve", "scalar_tensor_tensor", [CQ32[kc], gqT, rb], [CQN], out=CQN.t[:, kc, 0:n],
                   in0=CQ32[kc].t[:, 0:n], scalar=gqT.t[:, kc:kc + 1], in1=rb.t[:, 0:n], op0=ALU.mult, op1=ALU.mult)

            def cqn(kc):
                return CQN.t[:, kc, 0:n]
            wb, W = wslab(("uq", 0))
            for h in range(8):
                p = ps()
                mm_fm(p, W, wb, h * 128, 128, cqn, [CQN], 4, n)
                evac_copy(QN.t[:, h, 0:n], p.t[:, 0:n], [p], [QN])
            wb, W = wslab(("uq", 1))
            for h in range(8):
                p1, p2 = ps(), ps()
                mm_fm(p1, W, wb, h * 64, 64, cqn, [CQN], 4, n)
                mm_fm(p2, W, wb, 512 + h * 64, 64, cqn, [CQN], 4, n)
                rope_fm(p1, p2, n, QR.t[:64, h, 0:n], QR)
            if STOP == "C":
                return
            kv_upproj(groups, segs, n)
            if STOP == "D":
                return
            for G_ in groups:
                attention(G_["seq"], G_["pos0"], G_["n"], True, G_["col0"])
            for G_ in groups:
                attention(G_["seq"], G_["pos0"], G_["n"], False, G_["col0"])
            if STOP == "E":
                return
            for q4 in range(4):
                wba, Wa = wslab(("oa", q4))
                wbb, Wb = wslab(("ob", q4))
                for tt in range(4):
                    m = q4 * 4 + tt
                    sga, sgb = tb16(), tb16()
                    dma("sp", sga.t[:, 0:n], SGD[0, m, :, 0:n], sga.ld, rd=[SGB[0][m]], wr=[sga])
                    dma("sp", sgb.t[:, 0:n], SGD[1, m, :, 0:n], sgb.ld, rd=[SGB[1][m]], wr=[sgb])
                    pa, pb = ps(), ps()
                    mm_fm(pa, Wa, wba, tt * 128, 128, lambda kc: OA.t[:, kc, 0:n], [OA], 8, n)
                    mm_fm(pb, Wb, wbb, tt * 128, 128, lambda kc: OB.t[:, kc, 0:n], [OB], 8, n)
                    t1, t2 = t32(), t32()
                    op("dve", "tensor_tensor", [pa, sga], [t1], out=t1.t[:, 0:n], in0=pa.t[:, 0:n], in1=sga.t[:, 0:n],
                       op=ALU.mult)
                    op("dve", "tensor_tensor", [pb, sgb], [t2], out=t2.t[:, 0:n], in0=pb.t[:, 0:n], in1=sgb.t[:, 0:n],
                       op=ALU.mult)
                    op("dve", "tensor_tensor", [t1, t2], [Abuf2], out=Abuf2.t[:, m, 0:n], in0=t1.t[:, 0:n],
                       in1=t2.t[:, 0:n], op=ALU.add)
            if STOP == "F1":
                return
            for st, (off, sz) in enumerate(subt):
                dma("sp", H[st].t[:sz, :], segs[st]["x"], H[st].ld, wr=[H[st]])
            for i in range(4):
                wbs = [wslab(("o", i * 2 + half)) for half in range(2)]
                for st, (off, sz) in enumerate(subt):
                    p = ps()
                    for half in range(2):
                        mm_tm(p, half * 256, wbs[half][1], wbs[half][0], 0, 256,
                              (lambda o_, s_: (lambda kc: Abuf2.t[:, kc, o_:o_ + s_]))(off, sz), [Abuf2], 16, sz)
                    hs = H[st].t[:sz, i * 512:(i + 1) * 512]
                    op("dve", "tensor_tensor", [p, H[st]], [H[st]], out=hs, in0=p.t[:sz, :], in1=hs, op=ALU.add)
            if STOP == "F2":
                return
            fkv_tm(0)
            norm_transpose(Abuf, lambda st, sz: H[st].t[:sz, :], H, subt, gT["gT_ffn"])
            for G_ in groups:
                if not G_["first"]:
                    continue
                CONVST = G_["seq"].CONVST
                if G_["seq"].conv_src is None:
                    op("dve", "memset", [], [CONVST], CONVST.t[:, :, :].rearrange("p a b -> p (a b)"), 0.0)
                else:
                    pc = ps()
                    for q in range(22):
                        dma("sp", ST2.t[0:2, :], G_["seq"].conv_src[:, q * 512:(q + 1) * 512], ST2.ld, wr=[ST2])
                        for jj in range(4):
                            j = q * 4 + jj
                            op("pe", "matmul", [ST2, ident32], [pc], pc.t[:, 2 * j:2 * j + 2],
                               lhsT=ST2.t[0:2, jj * 128:(jj + 1) * 128], rhs=ident32.t[0:2, 0:2], start=True, stop=True)
                    op("act", "activation", [pc], [CONVST], out=CONVST.t[:, :, :].rearrange("p a b -> p (a b)"),
                       in_=pc.t[:, 0:2 * NFT], func=AF.Copy)

            def conv_a(p, jj):
                rr["u"] = (rr["u"] + 1) % 2
                u = U[rr["u"]]
                acc = t32()
                for gi, G_ in enumerate(groups):
                    ub, c0, ng = G_["col0"] + 2 * gi, G_["col0"], G_["n"]
                    CONVST = G_["seq"].CONVST
                    op("dve", "tensor_copy", [CONVST], [u], out=u.t[:, ub:ub + 2], in_=CONVST.t[:, jj, :])
                    op("act", "activation", [p], [u], out=u.t[:, ub + 2:ub + 2 + ng], in_=p.t[:, c0:c0 + ng], func=AF.Copy)
                op("act", "activation", [p, wcT, bcT], [acc], out=acc.t[:, 0:n], in_=p.t[:, 0:n], func=AF.Identity,
                   scale=wcT.t[:, jj, 2:3], bias=bcT.t[:, jj:jj + 1])
                for gi, G_ in enumerate(groups):
                    ub, c0, ng = G_["col0"] + 2 * gi, G_["col0"], G_["n"]
                    op("dve", "scalar_tensor_tensor", [u, wcT, acc], [acc], out=acc.t[:, c0:c0 + ng],
                       in0=u.t[:, ub + 1:ub + 1 + ng], scalar=wcT.t[:, jj, 1:2], in1=acc.t[:, c0:c0 + ng],
                       op0=ALU.mult, op1=ALU.add)
                return u, acc

            def conv_b(u, acc, jj):
                for gi, G_ in enumerate(groups):
                    ub, c0, ng = G_["col0"] + 2 * gi, G_["col0"], G_["n"]
                    CONVST = G_["seq"].CONVST
                    op("dve", "scalar_tensor_tensor", [u, wcT, acc], [acc], out=acc.t[:, c0:c0 + ng],
                       in0=u.t[:, ub:ub + ng], scalar=wcT.t[:, jj, 0:1], in1=acc.t[:, c0:c0 + ng],
                       op0=ALU.mult, op1=ALU.add)
                    op("dve", "tensor_copy", [u], [CONVST], out=CONVST.t[:, jj, :], in_=u.t[:, ub + ng:ub + ng + 2])
                return acc

            def down2(g0):
                sl = [wslab(("dn", g0 + i)) for i in range(2)]
                for cg in range(4):
                    for st, (off, sz) in enumerate(subt):
                        p = ps()
                        for i in range(2):
                            Gb = G[(g0 + i) % 4]
                            for kc in range(2):
                                op("pe", "matmul", [sl[i][0], Gb], [p], p.t[:sz, :], lhsT=Gb.t[:, kc, off:off + sz],
                                   rhs=sl[i][1][:, kc, cg * 512:(cg + 1) * 512], start=(i == 0 and kc == 0),
                                   stop=(i == 1 and kc == 1))
                        hs = H[st].t[:sz, cg * 512:(cg + 1) * 512]
                        op("dve", "tensor_tensor", [p, H[st]], [H[st]], out=hs, in0=p.t[:sz, :], in1=hs, op=ALU.add)

            for g in range(22):
                wbv, Wv = wslab(("upv", g))
                wbg, Wg = wslab(("upg", g))
                Gb = G[g % 4]
                for tt in range(2):
                    j = g * 2 + tt
                    pv, pg = ps(), ps()
                    mm_fm(pv, Wv, wbv, tt * 128, 128, A, [Abuf], 16, n)
                    mm_fm(pg, Wg, wbg, tt * 128, 128, A, [Abuf], 16, n)
                    uv, vc = conv_a(pv, j)
                    ug, gc = conv_a(pg, 44 + j)
                    conv_b(uv, vc, j)
                    conv_b(ug, gc, 44 + j)
                    op("act", "activation", [gc], [gc], out=gc.t[:, 0:n], in_=gc.t[:, 0:n], func=AF.Gelu_apprx_tanh)
                    op("dve", "tensor_tensor", [gc, vc], [Gb], out=Gb.t[:, tt, 0:n], in0=gc.t[:, 0:n],
                       in1=vc.t[:, 0:n], op=ALU.mult)
                if g >= 3 and g % 2 == 1:
                    down2(g - 3)
            if next_a is not None:
                next_a(Abuf2)
            down2(20)
            for G_ in groups:
                conv_out = G_["conv_out"]
                if conv_out is None:
                    continue
                CONVST = G_["seq"].CONVST
                for q in range(22):
                    pc = ps()
                    for jj in range(4):
                        j = q * 4 + jj
                        op("pe", "matmul", [CONVST, ident32], [pc], pc.t[0:2, jj * 128:(jj + 1) * 128],
                           lhsT=CONVST.t[:, j, :], rhs=ident32.t[:, :], start=True, stop=True)
                    op("act", "activation", [pc], [ST2], out=ST2.t[0:2, :], in_=pc.t[0:2, :], func=AF.Copy)
                    dma("sp", conv_out[:, q * 512:(q + 1) * 512], ST2.t[0:2, :], ST2.st, rd=[ST2])
            if STOP == "F3":
                return
            if next_b is not None:
                next_b(Abuf2)
            norm_transpose(Abuf, lambda st, sz: H[st].t[:sz, :], H, subt, gT["gT_ple"])
            for st, (off, sz) in enumerate(subt):
                pe32 = t32()
                dma("sp", pe32.t[:sz, 0:256], segs[st]["pe"], pe32.ld, wr=[pe32])
                peb = tb16()
                evac_copy(peb.t[:sz, 0:256], pe32.t[:sz, 0:256], [pe32], [peb])
                pt = ptb()
                for kc in range(2):
                    op("pe", "transpose", [peb, identb], [pt], out=pt.t[:, kc * 128:kc * 128 + sz],
                       in_=peb.t[:sz, kc * 128:(kc + 1) * 128], identity=identb.t[:sz, :sz])
                for kc in range(2):
                    evac_copy(PET.t[:, kc, off:off + sz], pt.t[:, kc * 128:kc * 128 + sz], [pt], [PET], eng="dve")
            for i in range(4):
                wbp, Wp = wslab(("ple", 0))
                wbs = [wslab(("pg", i * 2 + half)) for half in range(2)]
                for st, (off, sz) in enumerate(subt):
                    p1 = ps()
                    for half in range(2):
                        mm_tm(p1, half * 256, wbs[half][1], wbs[half][0], 0, 256, At(off, sz), [Abuf], 16, sz)
                    sg = t32()
                    op("act", "activation", [p1], [sg], out=sg.t[:sz, :], in_=p1.t[:sz, :], func=AF.Sigmoid)
                    p2 = ps()
                    mm_tm(p2, 0, Wp, wbp, i * 512, 512, (lambda o_, s_: (lambda kc: PET.t[:, kc, o_:o_ + s_]))(off, sz),
                          [PET], 2, sz)
                    op("dve", "tensor_tensor", [p2, sg], [sg], out=sg.t[:sz, :], in0=p2.t[:sz, :], in1=sg.t[:sz, :],
                       op=ALU.mult)
                    hs = H[st].t[:sz, i * 512:(i + 1) * 512]
                    op("dve", "tensor_tensor", [sg, H[st]], [H[st]], out=hs, in0=sg.t[:sz, :], in1=hs, op=ALU.add)
            if STOP == "F5":
                return
            for st, (off, sz) in enumerate(subt):
                ss = SMALL.t[:sz, st:st + 1]
                rs = SMALL.t[:sz, 8 + st:9 + st]
                op("act", "activation", [H[st]], [XN, SMALL], out=XN.t[:sz, :], in_=H[st].t[:sz, :], func=AF.Square,
                   accum_out=ss)
                rstd_from_ss(ss, rs, D)
                for i in range(4):
                    gq = t32()
                    dma("sp", gq.t[:, :], I["gfin_bc"][:, i * 512:(i + 1) * 512], gq.ld, wr=[gq])
                    hs = H[st].t[:sz, i * 512:(i + 1) * 512]
                    op("dve", "scalar_tensor_tensor", [H[st], SMALL, gq], [H[st]], out=hs, in0=hs, scalar=rs,
                       in1=gq.t[:sz, :], op0=ALU.mult, op1=ALU.mult)
                dma("sp", segs[st]["out"]["y"], H[st].t[:sz, :], H[st].st, rd=[H[st]])

        OK_ = ("y", "ckv", "kr", "fk", "fv", "lf")

        def prompt_segs(b):
            return [dict(off=o, sz=128, g=0, x=I["xp"][b * 512 + o:b * 512 + o + 128, :],
                         pe=I["pp"][b * 512 + o:b * 512 + o + 128, :],
                         out={k: O[k + "_p"][b * 512 + o:b * 512 + o + 128, :] for k in OK_},
                         kt=(b * 512 + o) // 128, first_cum=(b == 0 and o == 0)) for o in range(0, 512, 128)]

        if TP > 0 and STOP != "conv":
            sp_ = make_seq("p", CAPP)
            sp_.conv_src = None
            sp_.ktbase = 0
            sp_.PREF = PREFS[0]
            sp_.CONVST = CONVSTS[0]
            nb = TP // 512
            for b in range(nb):
                nxt = nxtb = None
                if b + 1 < nb:
                    nxt = (lambda b1: (lambda Adst: stage_a(Adst, prompt_segs(b1))))(b + 1)
                    nxtb = (lambda b1: (lambda Asrc: fk_fm(
                        Asrc, [dict(seq=sp_, pos0=b1 * 512, col0=0, n=512)], 512, range(6, 10))))(b + 1)
                groups = [dict(seq=sp_, pos0=b * 512, col0=0, n=512, first=(b == 0),
                               conv_out=(O["conv_p"] if b == nb - 1 else None))]
                block(b, groups, prompt_segs(b), 512, I["ropeC_p"][:, b * 512:(b + 1) * 512],
                      I["ropeS_p"][:, b * 512:(b + 1) * 512], a_done=(b > 0), next_a=nxt, skip_fk=(b > 0),
                      next_b=nxtb)

        NKTS = CAPS // 128
        sseqs = []
        for s in range(NSS if STOP not in ("conv", "prompt") else 0):
            sq_ = make_seq("s%d" % s, CAPS)
            sq_.conv_src = I["c_conv"][s]
            sq_.ktbase = s * NKTS
            sq_.PREF = PREFS[s % 2]
            sq_.CONVST = CONVSTS[s % 2]
            sseqs.append(sq_)
            vown = P.owner()
            for c in range(PAST // 512):
                t0 = c * 512
                pf_groups = [dict(seq=sq_, pos0=t0, col0=0, n=512)]
                pf_segs = [dict(off=o, sz=128, g=0) for o in range(0, 512, 128)]
                dma("pool", CB.t[:, 0:2048].rearrange("p (a b) -> p a b", a=4),
                    I["c_ckv"][s, t0:t0 + 512, :].rearrange("(a p) d -> p a d", p=128), CB.ld, wr=[CB])
                for st in range(4):
                    pt = ptb()
                    for kc in range(4):
                        op("pe", "transpose", [CB, identb], [pt], out=pt.t[:, kc * 128:(kc + 1) * 128],
                           in_=CB.t[:, st * 512 + kc * 128:st * 512 + (kc + 1) * 128], identity=identb.t[:, :])
                    for kc in range(4):
                        evac_copy(CKVT.t[:, kc, st * 128:(st + 1) * 128], pt.t[:, kc * 128:(kc + 1) * 128], [pt], [CKVT],
                                  eng=("act" if st % 2 else "dve"))
                kv_upproj(pf_groups, pf_segs, 512)
                dma("pool", CB.t[:, 0:256].rearrange("p (a b) -> p a b", a=4),
                    I["c_kr"][s, t0:t0 + 512, :].rearrange("(a p) d -> p a d", p=128), CB.ld, wr=[CB])
                pt = ptb()
                for st in range(4):
                    op("pe", "transpose", [CB, identb], [pt], out=pt.t[:64, st * 128:(st + 1) * 128],
                       in_=CB.t[:, st * 64:(st + 1) * 64], identity=identb.t[:, :])
                sg_ = tb16()
                evac_copy(sg_.t[:64, :], pt.t[:64, 0:512], [pt], [sg_])
                dma("sp", sq_.KrT[:, t0:t0 + 512], sg_.t[:64, :], sg_.st, rd=[sg_], wr=[sq_.kv])
                dma("pool", CB.t[:, :].rearrange("p (a b) -> p a b", a=4),
                    I["c_fv"][s, t0:t0 + 512, :].rearrange("(a p) d -> p a d", p=128), CB.ld, wr=[CB])
                dma("sp", sq_.Vf[t0:t0 + 512, :].rearrange("(a p) d -> p a d", p=128),
                    CB.t[:, :].rearrange("p (a b) -> p a b", a=4), vown, rd=[CB], wr=[sq_.kv])
                dma("pool", CB.t[:, :].rearrange("p (a b) -> p a b", a=4),
                    I["c_fk"][s, t0:t0 + 512, :].rearrange("(a p) d -> p a d", p=128), CB.ld, wr=[CB])
                for h in range(8):
                    pt = ptb()
                    for st in range(4):
                        op("pe", "transpose", [CB, identb], [pt], out=pt.t[:, st * 128:(st + 1) * 128],
                           in_=CB.t[:, st * 1024 + h * 128:st * 1024 + (h + 1) * 128], identity=identb.t[:, :])
                    sg_ = tb16()
                    evac_copy(sg_.t[:, :], pt.t[:, 0:512], [pt], [sg_])
                    dma("sp", sq_.KfT[h, :, t0:t0 + 512], sg_.t[:, :], sg_.st, rd=[sg_], wr=[sq_.kv])
            kb_ = sq_.ktbase
            dma("sp", LOGF.t[:, kb_:kb_ + PAST // 128, :], I["c_lf"][s].rearrange("(a p) h -> p a h", p=128), LOGF.ld,
                wr=[LOGF])
            for kt in range(PAST // 128):
                cum_tile(kb_ + kt, 128, kt == 0, sq_.PREF)
        for s0 in range(0, len(sseqs), 2):
            grp = sseqs[s0:s0 + 2]
            ng = len(grp)
            groups = [dict(seq=q_, pos0=PAST, col0=j * TS, n=TS, first=True, conv_out=O["conv_s"][s0 + j])
                      for j, q_ in enumerate(grp)]
            segs = [dict(off=j * TS, sz=TS, g=j, x=I["xs"][s0 + j], pe=I["ps"][s0 + j],
                         out={k: O[k + "_s"][s0 + j] for k in OK_},
                         kt=q_.ktbase + PAST // 128, first_cum=False) for j, q_ in enumerate(grp)]
            block(s0 // 2, groups, segs, ng * TS, I["ropeC_s"][:, s0 * TS:(s0 + ng) * TS],
                  I["ropeS_s"][:, s0 * TS:(s0 + ng) * TS])

        P.finish()
    return nc


def _rope_tables(pos):
    half = 32
    inv = (np.float32(10000.0) ** (-np.arange(half, dtype=np.float32) / np.float32(half))).astype(np.float32)
    ang = (pos.astype(np.float32)[:, None] * inv[None, :]).astype(np.float32)
    cos, sin = np.cos(ang).astype(np.float32), np.sin(ang).astype(np.float32)
    C = np.concatenate([cos, cos], axis=1).T
    S = np.concatenate([-sin, sin], axis=1).T
    return np.ascontiguousarray(C), np.ascontiguousarray(S)


_CACHE = {}


def run(inputs, n_cores, TP, PAST, NSS):
    f32 = lambda a: np.ascontiguousarray(np.asarray(a, dtype=np.float32))
    x_prompt, x_sample = f32(inputs["x_prompt"]), f32(inputs["x_sample"])
    key = (TP, PAST, NSS)
    if key not in _CACHE:
        _CACHE[key] = build(TP, PAST, NSS)
    nc = _CACHE[key]
    w_in = f32(inputs["w_in"])[0]
    com = {}
    com["w_in_p"] = np.ascontiguousarray(w_in[:, win_perm_cols()])
    w_uq = f32(inputs["w_uq"])[0].reshape(512, 8, 192)
    nope = w_uq[:, :, :128].reshape(512, 1024)
    rp = w_uq[:, :, 128:]
    rp_sw = np.concatenate([rp[:, :, 32:], rp[:, :, :32]], axis=2)
    com["w_uq_p"] = np.ascontiguousarray(np.concatenate([nope, rp.reshape(512, 512), rp_sw.reshape(512, 512)], axis=1))
    w_ukv = f32(inputs["w_ukv"])[0].reshape(512, 8, 256)
    com["w_ukv_p"] = np.ascontiguousarray(
        np.concatenate([w_ukv[:, :, :128].reshape(512, 1024), w_ukv[:, :, 128:].reshape(512, 1024)], axis=1))
    for k in ("w_oa", "w_ob", "w_o", "w_up", "w_down", "w_pg", "w_ple"):
        com[k] = f32(inputs[k])[0]
    for k, src in (("gT_mix", "g_mix"), ("gT_ffn", "g_ffn"), ("gT_ple", "g_ple")):
        com[k] = np.ascontiguousarray(f32(inputs[src])[0].reshape(16, 128).T)
    com["gqT"] = np.ascontiguousarray(f32(inputs["g_q"])[0].reshape(4, 128).T)
    com["gkv_bc"] = np.ascontiguousarray(np.broadcast_to(f32(inputs["g_kv"])[0][None, :], (128, 512)))
    com["gfin_bc"] = np.ascontiguousarray(np.broadcast_to(f32(inputs["g_final"])[None, :], (128, D)))
    com["bf_bc"] = np.ascontiguousarray(np.broadcast_to(f32(inputs["b_f"])[0][None, :], (128, 8)))
    wc = f32(inputs["w_conv"])[0]
    com["wcT"] = np.ascontiguousarray(wc.reshape(3, NFT, 128).transpose(2, 1, 0).reshape(128, NFT * 3))
    com["bcT"] = np.ascontiguousarray(f32(inputs["b_conv"])[0].reshape(NFT, 128).T)
    com["ident"] = np.eye(128, dtype=np.float32)
    com["tri"] = np.triu(np.ones((128, 128), dtype=np.float32))
    C, S = _rope_tables(np.arange(TP))
    com["ropeC_p"], com["ropeS_p"] = C, S
    C, S = _rope_tables(PAST + np.arange(TS))
    com["ropeC_s"] = np.ascontiguousarray(np.tile(C, (1, NSS)))
    com["ropeS_s"] = np.ascontiguousarray(np.tile(S, (1, NSS)))
    pp = f32(inputs["p_prompt"])[0]
    psm = f32(inputs["p_sample"])[0]
    c_ckv = f32(inputs["cache_mla_ckv"])[0]
    c_kr = f32(inputs["cache_mla_krope"])[0]
    c_fk = f32(inputs["cache_fox_k"])[0].reshape(-1, PAST, 1024)
    c_fv = f32(inputs["cache_fox_v"])[0].reshape(-1, PAST, 1024)
    c_lf = f32(inputs["cache_fox_logf"])[0]
    c_conv = f32(inputs["state_ffn_conv"])[0]
    in_maps = []
    for i in range(n_cores):
        m = dict(com)
        m["xp"] = x_prompt[i]
        m["pp"] = pp[i]
        sl = slice(i * NSS, (i + 1) * NSS)
        m["xs"] = x_sample[sl]
        m["ps"] = psm[sl]
        m["c_ckv"], m["c_kr"], m["c_fk"], m["c_fv"] = c_ckv[sl], c_kr[sl], c_fk[sl], c_fv[sl]
        m["c_lf"], m["c_conv"] = c_lf[sl], c_conv[sl]
        in_maps.append(m)
    res = run_bass_kernel_spmd(nc, in_maps, core_ids=list(range(n_cores)))
    R = res.results
    cat = lambda k: np.concatenate([np.asarray(r[k], dtype=np.float32)[None] for r in R], axis=0)
    cats = lambda k: np.concatenate([np.asarray(r[k], dtype=np.float32) for r in R], axis=0)
    B = n_cores
    return (cat("y_p"), cats("y_s"),
            cat("ckv_p")[None], cats("ckv_s")[None],
            cat("kr_p")[None], cats("kr_s")[None],
            cat("fk_p").reshape(1, B, TP, 8, 128), cats("fk_s").reshape(1, B * NSS, TS, 8, 128),
            cat("fv_p").reshape(1, B, TP, 8, 128), cats("fv_s").reshape(1, B * NSS, TS, 8, 128),
            cat("lf_p")[None], cats("lf_s")[None],
            cat("conv_p")[None], cats("conv_s")[None])


def kernel(**inputs):
    return run(inputs, 8, 4096, 2048, 2)
```

```python
import numpy as np
from contextlib import ExitStack
import concourse.bass as bass
import concourse.mybir as mybir
from concourse.bass_utils import run_bass_kernel_spmd

F32 = mybir.dt.float32
BF16 = mybir.dt.bfloat16
AF = mybir.ActivationFunctionType
ALU = mybir.AluOpType

D = 2048
FF = 5632
NFT = 88
EPS = 1e-6
MLA_SCALE = 192.0 ** -0.5
FOX_SCALE = 128.0 ** -0.5
TS = 32
ENGS = ("pe", "act", "dve", "pool", "sp")
SAME_ENGINE_SYNC = True


class Buf:
    def __init__(self, t, name, acc=False):
        self.t = t
        self.name = name
        self.lw = {}
        self.rd = {}
        self.al = []
        self.acc = acc
        self.ld = None
        self.st = None
        self.psum = False

    def __getitem__(self, idx):
        return self.t[idx]


class SemOwner:
    def __init__(self, sem):
        self.sem = sem
        self.cnt = 0


class Prog:
    def __init__(self, nc, es):
        self.nc = nc
        self.es = es
        self.q = {e: [] for e in ENGS}
        self.cnt = {e: 0 for e in ENGS}
        self.sem = {e: es.enter_context(nc.semaphore("sem_" + e)) for e in ENGS}
        self.seen = {e: {} for e in ENGS}
        self.nsem = len(ENGS)
        self.owners = []
        self.nbuf = 0

    def owner(self):
        self.nsem += 1
        o = SemOwner(self.es.enter_context(self.nc.semaphore("dsem%d" % self.nsem)))
        self.owners.append(o)
        return o

    def sb(self, name, shape, dt):
        t = self.es.enter_context(self.nc.sbuf_tensor(name, list(shape), dt))
        return Buf(t, name)

    def psum(self, name, shape, dt):
        t = self.es.enter_context(self.nc.psum_tensor(name, list(shape), dt))
        return Buf(t, name)

    def view(self, base, name):
        return Buf(base.t, name)

    def _collect(self, rd, wr, eng=None):
        toks = {}

        def add(d):
            for k, (s, v) in d.items():
                if k not in toks or toks[k][1] < v:
                    toks[k] = (s, v)
        for b in rd:
            add(b.lw)
            if b.psum:
                add({k: v for k, v in b.rd.items() if k != eng})
        for b in wr:
            if b.acc:
                continue
            add(b.lw)
            add(b.rd)
            for a in b.al:
                add(a.lw)
                add(a.rd)
        return toks

    def _waits(self, eng, toks):
        waits = []
        seen = self.seen[eng]
        for k, (s, v) in toks.items():
            if k == eng and (eng == "pe" or not SAME_ENGINE_SYNC):
                continue
            if seen.get(k, 0) < v:
                seen[k] = v
                waits.append((s, v))
        return waits

    def _commit(self, tok_key, tok, rd, wr):
        for b in wr:
            if b.acc:
                b.lw[tok_key] = tok
            else:
                b.lw = {tok_key: tok}
                b.rd = {}
        for b in rd:
            b.rd[tok_key] = tok

    def op(self, eng, name, rd, wr, *args, **kw):
        waits = self._waits(eng, self._collect(rd, wr, eng))
        self.cnt[eng] += 1
        v = self.cnt[eng]
        sem = self.sem[eng]

        def emit(e, name=name, args=args, kw=kw, waits=waits, sem=sem):
            for (s, val) in waits:
                e.wait_ge(s, val)
            getattr(e, name)(*args, **kw).then_inc(sem, 1)
        self.q[eng].append(emit)
        self._commit(eng, (sem, v), rd, wr)

    def dma(self, eng, out, in_, owner, rd=(), wr=(), **kw):
        waits = self._waits(eng, self._collect(rd, wr, eng))
        owner.cnt += 16
        v = owner.cnt
        sem = owner.sem

        def emit(e, out=out, in_=in_, waits=waits, sem=sem, kw=kw):
            for (s, val) in waits:
                e.wait_ge(s, val)
            e.dma_start(out=out, in_=in_, **kw).then_inc(sem, 16)
        self.q[eng].append(emit)
        self._commit(id(owner), (sem, v), rd, wr)

    def finish(self):
        finals = [(o.sem, o.cnt) for o in self.owners if o.cnt > 0]

        def emit(e, finals=finals):
            for (s, v) in finals:
                e.wait_ge(s, v)
        self.q["sp"].append(emit)
        ecnt = [(self.sem[k], self.cnt[k]) for k in ("pe", "act", "dve", "pool") if self.cnt[k] > 0]

        def emit2(e, ecnt=ecnt):
            for (s, v) in ecnt:
                e.wait_ge(s, v)
        self.q["sp"].append(emit2)
        nc = self.nc
        with nc.Block() as block:
            @block.tensor
            def _(e):
                for c in self.q["pe"]:
                    c(e)

            @block.scalar
            def _(e):
                for c in self.q["act"]:
                    c(e)

            @block.vector
            def _(e):
                for c in self.q["dve"]:
                    c(e)

            @block.gpsimd
            def _(e):
                for c in self.q["pool"]:
                    c(e)

            @block.sync
            def _(e):
                for c in self.q["sp"]:
                    c(e)


def slab_table():
    S = []
    idx = {}

    def add(key, w, nkc, r0, c0, width, grp):
        idx[key] = len(S)
        S.append((w, nkc, r0, c0, width, grp))
    for i in range(10):
        add(("in_fm", i), "w_in_p", 16, 0, i * 256, 256, 0)
    add(("in_kr", 0), "w_in_p", 16, 0, 2560, 128, 0)
    for i in range(16):
        add(("in_g", i), "w_in_p", 16, 0, 2688 + i * 256, 256, 1)
    for i in range(10):
        add(("in_tm", i), "w_in_p", 16, 0, 6784 + i * 256, 256, 2)
    add(("in_fl", 0), "w_in_p", 16, 0, 9344, 8, 2)
    for i in range(2):
        add(("uq", i), "w_uq_p", 4, 0, i * 1024, 1024, 3)
    for i in range(2):
        add(("ukv", i), "w_ukv_p", 4, 0, i * 1024, 1024, 3)
    for i in range(4):
        add(("oa", i), "w_oa", 8, 0, i * 512, 512, 4)
        add(("ob", i), "w_ob", 8, 0, i * 512, 512, 4)
    for i in range(8):
        add(("o", i), "w_o", 16, 0, i * 256, 256, 5)
    for g in range(22):
        add(("upv", g), "w_up", 16, 0, g * 256, 256, 6 + g // 6)
        add(("upg", g), "w_up", 16, 0, FF + g * 256, 256, 6 + g // 6)
        add(("dn", g), "w_down", 2, g * 256, 0, 2048, 6 + g // 6)
    for i in range(8):
        add(("pg", i), "w_pg", 16, 0, i * 256, 256, 10)
    add(("ple", 0), "w_ple", 2, 0, 0, 2048, 10)
    return S, idx


NCP = 9352


def win_perm_cols():
    cq = np.arange(0, 512)
    ckv = np.arange(512, 1024)
    kr = np.arange(1024, 1088)
    fq = np.arange(1088, 2112)
    fk = np.arange(2112, 3136)
    fv = np.arange(3136, 4160)
    fl = np.arange(4160, 4168)
    ga = np.arange(4168, 6216)
    gb = np.arange(6216, 8264)
    kr_sw = np.concatenate([kr[32:], kr[:32]])
    cols = np.concatenate([cq, fq, fk, kr, kr_sw, ga, gb, ckv, fk, fv, fl])
    assert cols.shape[0] == NCP
    return cols


def build(TP, PAST, NSS):
    nc = bass.Bass("TRN2", target_bir_lowering=False)
    CAPP = max(TP, 128)
    CAPS = ((PAST + TS + 127) // 128) * 128
    NKTMAX = max(TP // 128, NSS * (CAPS // 128))

    def din(name, shape, dt=F32):
        return nc.dram_tensor(name, list(shape), dt, kind="ExternalInput").ap()

    def dout(name, shape, dt=F32):
        return nc.dram_tensor(name, list(shape), dt, kind="ExternalOutput").ap()

    def dint(name, shape, dt=BF16):
        return nc.dram_tensor(name, list(shape), dt, kind="Internal").ap()

    I = {}
    I["xp"] = din("xp", [TP, D])
    I["pp"] = din("pp", [TP, 256])
    I["xs"] = din("xs", [NSS, TS, D])
    I["ps"] = din("ps", [NSS, TS, 256])
    I["c_ckv"] = din("c_ckv", [NSS, PAST, 512])
    I["c_kr"] = din("c_kr", [NSS, PAST, 64])
    I["c_fk"] = din("c_fk", [NSS, PAST, 1024])
    I["c_fv"] = din("c_fv", [NSS, PAST, 1024])
    I["c_lf"] = din("c_lf", [NSS, PAST, 8])
    I["c_conv"] = din("c_conv", [NSS, 2, 2 * FF])
    WSRC = {
        "w_in_p": din("w_in_p", [D, NCP]), "w_uq_p": din("w_uq_p", [512, 2048]),
        "w_ukv_p": din("w_ukv_p", [512, 2048]), "w_oa": din("w_oa", [1024, D]),
        "w_ob": din("w_ob", [1024, D]), "w_o": din("w_o", [D, D]), "w_up": din("w_up", [D, 2 * FF]),
        "w_down": din("w_down", [FF, D]), "w_pg": din("w_pg", [D, D]), "w_ple": din("w_ple", [256, D]),
    }
    for nm, shp in (("gT_mix", [128, 16]), ("gT_ffn", [128, 16]), ("gT_ple", [128, 16]), ("gqT", [128, 4]),
                    ("gkv_bc", [128, 512]), ("gfin_bc", [128, D]), ("bf_bc", [128, 8]),
                    ("wcT", [128, NFT * 3]), ("bcT", [128, NFT]), ("ident", [128, 128]), ("tri", [128, 128]),
                    ("ropeC_p", [64, TP]), ("ropeS_p", [64, TP]), ("ropeC_s", [64, NSS * TS]), ("ropeS_s", [64, NSS * TS])):
        I[nm] = din(nm, shp)
    O = {}
    O["y_p"] = dout("y_p", [TP, D])
    O["y_s"] = dout("y_s", [NSS, TS, D])
    O["ckv_p"] = dout("ckv_p", [TP, 512])
    O["ckv_s"] = dout("ckv_s", [NSS, TS, 512])
    O["kr_p"] = dout("kr_p", [TP, 64])
    O["kr_s"] = dout("kr_s", [NSS, TS, 64])
    O["fk_p"] = dout("fk_p", [TP, 1024])
    O["fk_s"] = dout("fk_s", [NSS, TS, 1024])
    O["fv_p"] = dout("fv_p", [TP, 1024])
    O["fv_s"] = dout("fv_s", [NSS, TS, 1024])
    O["lf_p"] = dout("lf_p", [TP, 8])
    O["lf_s"] = dout("lf_s", [NSS, TS, 8])
    O["conv_p"] = dout("conv_p", [2, 2 * FF])
    O["conv_s"] = dout("conv_s", [NSS, 2, 2 * FF])

    SLABS, SIDX = slab_table()
    WS = dint("ws", [len(SLABS), 128, 4096])
    SGD = dint("sgd", [2, 16, 128, 512])

    es = ExitStack()
    with es:
        P = Prog(nc, es)
        op, dma = P.op, P.dma

        def mk(t, name, ld=False, st=False):
            b = Buf(t, name)
            if ld:
                b.ld = P.owner()
            if st:
                b.st = P.owner()
            return b

        def sbd(name, shape, dt, ld=False, st=False):
            t = es.enter_context(nc.sbuf_tensor("sb_" + name, list(shape), dt))
            return mk(t, name, ld, st)

        ident32 = sbd("ident32", [128, 128], F32, ld=True)
        identb = sbd("identb", [128, 128], BF16)
        ones32 = sbd("ones32", [128, 128], F32)
        onesb = sbd("onesb", [128, 128], BF16)
        tri32 = sbd("tri32", [128, 128], F32, ld=True)
        trib = sbd("trib", [128, 128], BF16)
        gT = {k: sbd(k, [128, 16], F32, ld=True) for k in ("gT_mix", "gT_ffn", "gT_ple")}
        gqT = sbd("gqT", [128, 4], F32, ld=True)
        gkv_bc = sbd("gkv_bc", [128, 512], F32, ld=True)
        bf_bc = sbd("bf_bc", [128, 8], F32, ld=True)
        wcT = sbd("wcT", [128, NFT, 3], F32, ld=True)
        bcT = sbd("bcT", [128, NFT], F32, ld=True)
        ropeC = sbd("ropeC", [64, 512], F32, ld=True)
        ropeS = sbd("ropeS", [64, 512], F32, ld=True)
        arena = es.enter_context(nc.sbuf_tensor("arena", [128, 8192], F32))
        H = [mk(arena[:, i * 2048:(i + 1) * 2048], "H%d" % i, ld=True, st=True) for i in range(4)]
        qviews = [mk(arena[:, i * 2048:(i + 1) * 2048].bitcast(BF16).rearrange("p (a b) -> p a b", a=8), nm)
                  for i, nm in enumerate(("QN", "QF", "OA", "OB"))]
        QN, QF, OA, OB = qviews
        for i in range(4):
            H[i].al = [qviews[i]]
            qviews[i].al = [H[i]]
        QR = sbd("QR", [64, 8, 512], BF16)
        G = [sbd("G%d" % i, [128, 2, 512], BF16) for i in range(2)]
        AB = [sbd("Abuf%d" % i, [128, 16, 512], BF16) for i in range(2)]
        XS = sbd("XS", [128, D], F32, ld=True)
        CB = mk(XS.t[:, :].bitcast(BF16), "CB", ld=True)
        CB.al = [XS]
        XS.al = [CB]
        XNS = [sbd("XN%d" % i, [128, D], BF16) for i in range(2)]
        XN = XNS[0]
        NWB = 5
        WB = [sbd("WB%d" % i, [128, 4096], BF16, ld=True) for i in range(NWB)]
        NT32 = 8
        T32 = [sbd("T32_%d" % i, [128, 512], F32, ld=True, st=True) for i in range(NT32)]
        NTB = 8
        TB16 = [sbd("TB16_%d" % i, [128, 512], BF16, ld=True, st=True) for i in range(NTB)]
        CQH = [sbd("CQH%d" % i, [128, 512], F32) for i in range(2)]
        CQN = sbd("CQN", [128, 4, 512], BF16)
        CKVT = sbd("CKVT", [128, 4, 512], BF16)
        KB = [sbd("KB%d" % i, [128, 1024], BF16, ld=True) for i in range(2)]
        VB = [sbd("VB%d" % i, [128, 8, 256], BF16, ld=True) for i in range(2)]
        CQ32 = []
        for i in range(4):
            v = VB[i // 2]
            c = mk(v.t[:, :, :].rearrange("p a b -> p (a b)")[:, (i % 2) * 1024:(i % 2 + 1) * 1024].bitcast(F32),
                   "CQ32_%d" % i)
            c.al = [v]
            v.al = v.al + [c]
            CQ32.append(c)
        KRB = [sbd("KRB%d" % i, [128, 1024], BF16, ld=True) for i in range(2)]
        for i in range(2):
            g_ = mk(KRB[i].t[:, :].rearrange("p (a b) -> p a b", a=2), "G%d" % (2 + i))
            g_.al = [KRB[i]]
            KRB[i].al = KRB[i].al + [g_]
            G.append(g_)
        U = [sbd("U%d" % i, [128, 516], F32) for i in range(2)]
        LOGF = sbd("LOGF", [128, NKTMAX + 1, 8], F32, ld=True, st=True)
        CUM = sbd("CUM", [128, NKTMAX + 1, 8], F32)
        NEGCUM = sbd("NEGCUM", [128, NKTMAX + 1, 8], F32)
        PREFS = [sbd("PREF%d" % i, [128, 8], F32) for i in range(2)]
        CONVSTS = [sbd("CONVST%d" % i, [128, NFT, 2], F32) for i in range(2)]
        ST2 = sbd("ST2", [2, 512], F32, ld=True, st=True)
        SMALL = sbd("SMALL", [128, 16], F32)
        PET = sbd("PET", [128, 2, 512], BF16)
        SGB = [[Buf(None, "sgb%d_%d" % (g, m)) for m in range(16)] for g in range(2)]
        NPS = 6
        PS = [mk(es.enter_context(nc.psum_tensor("PS%d" % i, [128, 512], F32)), "PS%d" % i) for i in range(NPS)]
        PT = [mk(es.enter_context(nc.psum_tensor("PTb%d" % i, [128, 1024], BF16)), "PT%d" % i) for i in range(2)]
        PSX = []
        for i in range(2):
            v = mk(PT[i].t[:, :].bitcast(F32), "PSX%d" % i)
            v.al = [PT[i]]
            PT[i].al = [v]
            PSX.append(v)
        for b_ in PS + PT + PSX:
            b_.psum = True
        rr = {"ps": 0, "psn": NPS, "t32": 0, "tb": 0, "pt": 0, "u": 0, "ev": 0, "w": 0}

        def ps():
            rr["ps"] = (rr["ps"] + 1) % rr["psn"]
            return PS[rr["ps"]]

        def t32():
            rr["t32"] = (rr["t32"] + 1) % NT32
            return T32[rr["t32"]]

        def tb16():
            rr["tb"] = (rr["tb"] + 1) % NTB
            return TB16[rr["tb"]]

        def ptb():
            rr["pt"] = (rr["pt"] + 1) % 2
            return PT[rr["pt"]]

        for buf, src in ((ident32, I["ident"]), (tri32, I["tri"]), (gT["gT_mix"], I["gT_mix"]),
                         (gT["gT_ffn"], I["gT_ffn"]), (gT["gT_ple"], I["gT_ple"]), (gqT, I["gqT"]),
                         (gkv_bc, I["gkv_bc"]), (bf_bc, I["bf_bc"]), (bcT, I["bcT"])):
            dma("sp", buf.t[:, :], src, buf.ld, wr=[buf])
        dma("sp", wcT.t[:, :, :].rearrange("p a b -> p (a b)"), I["wcT"], wcT.ld, wr=[wcT])
        op("dve", "tensor_copy", [ident32], [identb], out=identb.t[:, :], in_=ident32.t[:, :])
        op("dve", "tensor_copy", [tri32], [trib], out=trib.t[:, :], in_=tri32.t[:, :])
        op("dve", "memset", [], [ones32], ones32.t[:, :], 1.0)
        op("dve", "memset", [], [onesb], onesb.t[:, :], 1.0)

        NGRP = 11
        gowner = [P.owner() for _ in range(NGRP)]
        WSB = [Buf(None, "wsgrp%d" % g, acc=True) for g in range(NGRP)]
        converted = set()

        import os as _os
        STOP = _os.environ.get("K_STOP", "")

        def wslab(key):
            si = SIDX[key]
            wn, nkc, r0, c0, width, grp = SLABS[si]
            b = WB[rr["w"] % NWB]
            rr["w"] += 1
            if si not in converted:
                converted.add(si)
                wsrc = WSRC[wn][r0:r0 + nkc * 128, c0:c0 + width].rearrange("(k p) c -> p k c", p=128)
                dma("pool", b.t[:, 0:nkc * width].rearrange("p (k c) -> p k c", k=nkc), wsrc, b.ld, wr=[b])
                dma("sp", WS[si, :, 0:nkc * width], b.t[:, 0:nkc * width], gowner[grp], rd=[b], wr=[WSB[grp]])
            else:
                dma("pool", b.t[:, 0:nkc * width], WS[si, :, 0:nkc * width], b.ld, rd=[WSB[grp]], wr=[b])
            return b, b.t[:, 0:nkc * width].rearrange("p (k c) -> p k c", k=nkc)

        def evac_copy(out_ap, in_ap, rd, wr, eng=None):
            if eng is None:
                rr["ev"] += 1
                eng = "act" if rr["ev"] % 2 else "dve"
            if eng == "act":
                op("act", "activation", rd, wr, out=out_ap, in_=in_ap, func=AF.Copy)
            else:
                op("dve", "tensor_copy", rd, wr, out=out_ap, in_=in_ap)

        def rstd_from_ss(ss_ap, out_ap, n):
            op("act", "activation", [SMALL], [SMALL], out=out_ap, in_=ss_ap, func=AF.Sqrt, bias=EPS, scale=1.0 / n)
            op("dve", "reciprocal", [SMALL], [SMALL], out=out_ap, in_=out_ap)

        def norm_transpose(Abuf, src_fn, srcbufs, subt, gTb, load=None):
            for st, (off, sz) in enumerate(subt):
                if load is not None:
                    load(st, off, sz)
                src = src_fn(st, sz)
                sb = srcbufs[st]
                rr["xn"] = rr.get("xn", 0) + 1
                XN = XNS[rr["xn"] % 2]
                ss = SMALL.t[:sz, st:st + 1]
                rs = SMALL.t[:sz, 8 + st:9 + st]
                op("act", "activation", [sb], [XN, SMALL], out=XN.t[:sz, :], in_=src, func=AF.Square, accum_out=ss)
                rstd_from_ss(ss, rs, D)
                op("act", "activation", [sb, SMALL], [XN], out=XN.t[:sz, :], in_=src, func=AF.Copy, scale=rs)
                for q4 in range(4):
                    pt = ptb()
                    for j in range(4):
                        kc = q4 * 4 + j
                        op("pe", "transpose", [XN, identb], [pt], out=pt.t[:, j * 128:j * 128 + sz],
                           in_=XN.t[:sz, kc * 128:(kc + 1) * 128], identity=identb.t[:sz, :sz])
                    for j in range(4):
                        kc = q4 * 4 + j
                        if True:
                            op("dve", "tensor_scalar", [pt, gTb], [Abuf], out=Abuf.t[:, kc, off:off + sz],
                               in0=pt.t[:, j * 128:j * 128 + sz], scalar1=gTb.t[:, kc:kc + 1], scalar2=None,
                               op0=ALU.mult)
                        else:
                            op("act", "activation", [pt, gTb], [Abuf], out=Abuf.t[:, kc, off:off + sz],
                               in_=pt.t[:, j * 128:j * 128 + sz], func=AF.Copy, scale=gTb.t[:, kc:kc + 1])

        def mm_fm(psb, W, wbuf, c0, m, rhs_fn, rhsbufs, nkc, n):
            for kc in range(nkc):
                op("pe", "matmul", [wbuf] + rhsbufs, [psb], psb.t[:m, 0:n], lhsT=W[:, kc, c0:c0 + m], rhs=rhs_fn(kc),
                   start=(kc == 0), stop=(kc == nkc - 1))

        def mm_tm(psb, pc0, W, wbuf, c0, width, lhs_fn, lhsbufs, nkc, sz):
            for kc in range(nkc):
                op("pe", "matmul", [wbuf] + lhsbufs, [psb], psb.t[:sz, pc0:pc0 + width], lhsT=lhs_fn(kc),
                   rhs=W[:, kc, c0:c0 + width], start=(kc == 0), stop=(kc == nkc - 1))

        def rope_fm(ps1, ps2, n, out_ap, outbuf):
            a = t32()
            b = t32()
            op("dve", "tensor_tensor", [ps1, ropeC], [a], out=a.t[:64, :n], in0=ps1.t[:64, :n], in1=ropeC.t[:64, :n],
               op=ALU.mult)
            op("dve", "tensor_tensor", [ps2, ropeS], [b], out=b.t[:64, :n], in0=ps2.t[:64, :n], in1=ropeS.t[:64, :n],
               op=ALU.mult)
            op("dve", "tensor_tensor", [a, b], [outbuf], out=out_ap, in0=a.t[:64, :n], in1=b.t[:64, :n], op=ALU.add)

        class Seq:
            pass

        def make_seq(name, cap):
            s = Seq()
            s.cap = cap
            s.KnT = dint(name + "_knt", [8, 128, cap])
            s.KrT = dint(name + "_krt", [64, cap])
            s.KfT = dint(name + "_kft", [8, 128, cap])
            s.Vm = dint(name + "_vm", [cap, 1024])
            s.Vf = dint(name + "_vf", [cap, 1024])
            s.kv = Buf(None, name + "_kv", acc=True)
            return s

        def cum_tile(kt, sz, first, PREF):
            p1 = ps()
            op("pe", "matmul", [tri32, LOGF], [p1], p1.t[:sz, 0:8], lhsT=tri32.t[:sz, :sz], rhs=LOGF.t[:sz, kt, :],
               start=True, stop=True)
            if first:
                op("dve", "tensor_copy", [p1], [CUM], out=CUM.t[:sz, kt, :], in_=p1.t[:sz, 0:8])
            else:
                op("dve", "tensor_tensor", [p1, PREF], [CUM], out=CUM.t[:sz, kt, :], in0=p1.t[:sz, 0:8],
                   in1=PREF.t[:sz, :], op=ALU.add)
            op("dve", "tensor_scalar", [CUM], [NEGCUM], out=NEGCUM.t[:sz, kt, :], in0=CUM.t[:sz, kt, :],
               scalar1=-1.0, scalar2=None, op0=ALU.mult)
            p2 = ps()
            op("pe", "matmul", [ones32, LOGF], [p2], p2.t[:, 0:8], lhsT=ones32.t[:sz, :], rhs=LOGF.t[:sz, kt, :],
               start=True, stop=True)
            if first:
                op("dve", "tensor_copy", [p2], [PREF], out=PREF.t[:, :], in_=p2.t[:, 0:8])
            else:
                op("dve", "tensor_tensor", [p2, PREF], [PREF], out=PREF.t[:, :], in0=p2.t[:, 0:8], in1=PREF.t[:, :],
                   op=ALU.add)

        def store_groups(s, groups, dst_fn, part=128):
            for G_ in groups:
                dma("sp", dst_fn(G_), s.t[:part, G_["col0"]:G_["col0"] + G_["n"]], s.st, rd=[s], wr=[G_["seq"].kv])
            for G_ in groups:
                G_["seq"].kv.lw[id(s.st)] = (s.st.sem, s.st.cnt)

        def kv_upproj(groups, segs, n):
            wb0, W0 = wslab(("ukv", 0))
            for h in range(8):
                p = ps()
                mm_fm(p, W0, wb0, h * 128, 128, lambda kc: CKVT.t[:, kc, 0:n], [CKVT], 4, n)
                s = tb16()
                evac_copy(s.t[:, 0:n], p.t[:, 0:n], [p], [s])
                store_groups(s, groups, lambda G_: G_["seq"].KnT[h, :, G_["pos0"]:G_["pos0"] + G_["n"]])
            wb1, W1 = wslab(("ukv", 1))
            for S_ in segs:
                off, sz = S_["off"], S_["sz"]
                G_ = groups[S_["g"]]
                r0 = G_["pos0"] + off - G_["col0"]
                for cg in range(2):
                    p = ps()
                    mm_tm(p, 0, W1, wb1, cg * 512, 512, lambda kc: CKVT.t[:, kc, off:off + sz], [CKVT], 4, sz)
                    s = tb16()
                    evac_copy(s.t[:sz, :], p.t[:sz, :], [p], [s])
                    dma("sp", G_["seq"].Vm[r0:r0 + sz, cg * 512:(cg + 1) * 512], s.t[:sz, :], s.st,
                        rd=[s], wr=[G_["seq"].kv])

        def attention(seq, q0, nq, mla, col0=0):
            ktb = seq.ktbase
            nkeys = q0 + nq
            nkt = (nkeys + 127) // 128
            nch = (nkt + 7) // 8
            units = [(h, c) for h in range(8) for c in range(nch)]
            KT = seq.KnT if mla else seq.KfT
            Vd = seq.Vm if mla else seq.Vf
            Ob = OA if mla else OB
            loaded = {}
            SBK = [PS[0], PS[1], PSX[0], PSX[1]]
            sst = {"i": 0}

            def ps():
                sst["i"] = (sst["i"] + 1) % 4
                return SBK[sst["i"]]

            def issue(ui):
                if ui >= len(units):
                    return
                h, c = units[ui]
                k0 = c * 1024
                kn = min(1024, nkeys - k0)
                kb = KB[ui % 2]
                dma("sp", kb.t[:, 0:kn], KT[h, :, k0:k0 + kn], kb.ld, rd=[seq.kv], wr=[kb])
                vb = VB[ui % 2]
                nfull = kn // 128
                hp = h // 2
                if nfull > 0:
                    dma("sp", vb.t[:, 0:nfull, :],
                        Vd[k0:k0 + nfull * 128, hp * 256:(hp + 1) * 256].rearrange("(kt p) d -> p kt d", p=128),
                        vb.ld, rd=[seq.kv], wr=[vb])
                rem = kn - nfull * 128
                if rem > 0:
                    dma("sp", vb.t[0:rem, nfull, :], Vd[k0 + nfull * 128:k0 + kn, hp * 256:(hp + 1) * 256],
                        vb.ld, rd=[seq.kv], wr=[vb])
                krb = None
                if mla:
                    krb = KRB[ui % 2]
                    dma("sp", krb.t[:64, 0:kn], seq.KrT[:, k0:k0 + kn], krb.ld, rd=[seq.kv], wr=[krb])
                loaded[ui] = (kb, vb, krb)

            issue(0)
            issue(1)
            jobs = []
            for ui, (h, c) in enumerate(units):
                ktn = min(8, nkt - c * 8)
                tl = []
                for kl in range(ktn):
                    kt = c * 8 + kl
                    ks = kt * 128
                    ksz = min(128, nkeys - ks)
                    j0 = max(0, ks - q0)
                    if j0 >= nq:
                        continue
                    tl.append((kl, kt, ks, ksz, j0))
                for i, (kl, kt, ks, ksz, j0) in enumerate(tl):
                    jobs.append(dict(ui=ui, h=h, c=c, kl=kl, kt=kt, ks=ks, ksz=ksz, j0=j0,
                                     first_of_unit=(i == 0), last_of_unit=(i == len(tl) - 1)))
            state = {}

            def stage1(J):
                ui, h, c, kl, kt, ks, ksz, j0 = (J[k] for k in ("ui", "h", "c", "kl", "kt", "ks", "ksz", "j0"))
                kb, vb, krb = loaded[ui]
                if c == 0 and J["first_of_unit"]:
                    state[h] = dict(po=PS[2 + 2 * (h % 2)], pd=PS[3 + 2 * (h % 2)], cq=None)
                    if not mla:
                        cq = CQH[h % 2]
                        pq = ps()
                        for off in range(0, nq, 128):
                            sz = min(128, nq - off)
                            kt_q = (q0 + off) // 128
                            dg = t32()
                            op("dve", "tensor_scalar", [ident32, CUM], [dg], out=dg.t[:sz, :sz],
                               in0=ident32.t[:sz, :sz], scalar1=CUM.t[:sz, ktb + kt_q, h:h + 1], scalar2=None,
                               op0=ALU.mult)
                            op("pe", "matmul", [ones32, dg], [pq], pq.t[:, off:off + sz], lhsT=ones32.t[:sz, :],
                               rhs=dg.t[:sz, :sz], start=True, stop=True)
                        evac_copy(cq.t[:, 0:nq], pq.t[:, 0:nq], [pq], [cq], eng="act")
                        state[h]["cq"] = cq
                cq = state[h]["cq"]
                diag = (ks + ksz - 1) > (q0 + j0)
                sp_ = ps()
                if mla:
                    op("pe", "matmul", [kb, QN], [sp_], sp_.t[:ksz, j0:nq], lhsT=kb.t[:, kl * 128:kl * 128 + ksz],
                       rhs=QN.t[:, h, col0 + j0:col0 + nq], start=True, stop=False)
                    op("pe", "matmul", [krb, QR], [sp_], sp_.t[:ksz, j0:nq],
                       lhsT=krb.t[:64, kl * 128:kl * 128 + ksz], rhs=QR.t[:, h, col0 + j0:col0 + nq], start=False, stop=True)
                else:
                    op("pe", "matmul", [kb, QF], [sp_], sp_.t[:ksz, j0:nq], lhsT=kb.t[:, kl * 128:kl * 128 + ksz],
                       rhs=QF.t[:, h, col0 + j0:col0 + nq], start=True, stop=True)
                pt_ = tb16()
                if mla:
                    op("act", "activation", [sp_], [pt_], out=pt_.t[:ksz, j0:nq], in_=sp_.t[:ksz, j0:nq],
                       func=AF.Exp, scale=MLA_SCALE)
                    if diag and ksz > 64:
                        op("dve", "memset", [], [pt_], pt_.t[64:128, j0:j0 + 64], 0.0)
                else:
                    tmp = t32()
                    op("dve", "scalar_tensor_tensor", [sp_, cq], [tmp], out=tmp.t[:ksz, j0:nq],
                       in0=sp_.t[:ksz, j0:nq], scalar=FOX_SCALE, in1=cq.t[:ksz, j0:nq], op0=ALU.mult, op1=ALU.add)
                    op("act", "activation", [tmp, NEGCUM], [pt_], out=pt_.t[:ksz, j0:nq], in_=tmp.t[:ksz, j0:nq],
                       func=AF.Exp, bias=NEGCUM.t[:ksz, ktb + kt, h:h + 1])
                    if diag:
                        msz = min(ksz, nq - j0)
                        op("dve", "tensor_tensor", [pt_, trib], [pt_], out=pt_.t[:ksz, j0:j0 + msz],
                           in0=pt_.t[:ksz, j0:j0 + msz], in1=trib.t[:ksz, :msz], op=ALU.mult)
                J["pt"] = pt_

            def stage2(J):
                ui, h, c, kl, kt, ksz, j0 = (J[k] for k in ("ui", "h", "c", "kl", "kt", "ksz", "j0"))
                kb, vb, krb = loaded[ui]
                po, pd = state[h]["po"], state[h]["pd"]
                pt_ = J["pt"]
                first = (kt == 0)
                last = (kt == nkt - 1)
                vcol = (h % 2) * 128
                op("pe", "matmul", [vb, pt_], [po], po.t[:, j0:nq], lhsT=vb.t[:ksz, kl, vcol:vcol + 128],
                   rhs=pt_.t[:ksz, j0:nq], start=first, stop=last)
                op("pe", "matmul", [onesb, pt_], [pd], pd.t[:, j0:nq], lhsT=onesb.t[:ksz, :],
                   rhs=pt_.t[:ksz, j0:nq], start=first, stop=last)
                if J["last_of_unit"]:
                    issue(ui + 2)
                    if c == nch - 1:
                        rd_ = t32()
                        op("act", "activation", [pd], [rd_], out=rd_.t[:, 0:nq], in_=pd.t[:, 0:nq], func=AF.Ln)
                        op("act", "activation", [rd_], [rd_], out=rd_.t[:, 0:nq], in_=rd_.t[:, 0:nq], func=AF.Exp,
                           scale=-1.0)
                        op("dve", "tensor_tensor", [po, rd_], [Ob], out=Ob.t[:, h, col0:col0 + nq], in0=po.t[:, 0:nq],
                           in1=rd_.t[:, 0:nq], op=ALU.mult)

            pending = []
            for i, J in enumerate(jobs):
                if J["first_of_unit"]:
                    while pending and pending[0]["ui"] <= J["ui"] - 2:
                        stage2(pending.pop(0))
                stage1(J)
                pending.append(J)
                if len(pending) > 3:
                    stage2(pending.pop(0))
            while pending:
                stage2(pending.pop(0))

        def stage_a(Adst, segs):
            subt = [(S_["off"], S_["sz"]) for S_ in segs]

            def load_x(st, off, sz):
                dma("sp", XS.t[:sz, :], segs[st]["x"], XS.ld, wr=[XS])
            norm_transpose(Adst, lambda st, sz: XS.t[:sz, :], [XS] * len(subt), subt, gT["gT_mix"], load=load_x)

        def fk_fm(Asrc, groups, n, slabs):
            for i in slabs:
                wb, W = wslab(("in_fm", i))
                for tt in range(2):
                    h = i * 2 + tt - 12
                    p = ps()
                    mm_fm(p, W, wb, tt * 128, 128, lambda kc: Asrc.t[:, kc, 0:n], [Asrc], 16, n)
                    s = tb16()
                    evac_copy(s.t[:, 0:n], p.t[:, 0:n], [p], [s])
                    store_groups(s, groups, lambda G_: G_["seq"].KfT[h, :, G_["pos0"]:G_["pos0"] + G_["n"]])

        def block(bi, groups, segs, n, ropeC_src, ropeS_src, a_done=False, next_a=None, skip_fk=False, next_b=None):
            subt = [(S_["off"], S_["sz"]) for S_ in segs]
            nst = len(subt)
            Abuf = AB[bi % 2]
            Abuf2 = AB[(bi + 1) % 2]
            dma("sp", ropeC.t[:, 0:n], ropeC_src, ropeC.ld, wr=[ropeC])
            dma("sp", ropeS.t[:, 0:n], ropeS_src, ropeS.ld, wr=[ropeS])

            if not a_done:
                stage_a(Abuf, segs)

            def A(kc):
                return Abuf.t[:, kc, 0:n]

            def At(off, sz):
                return lambda kc: Abuf.t[:, kc, off:off + sz]
            if STOP == "A":
                return
            for i in range(6):
                wb, W = wslab(("in_fm", i))
                for tt in range(2):
                    t = i * 2 + tt
                    p = ps()
                    mm_fm(p, W, wb, tt * 128, 128, A, [Abuf], 16, n)
                    if t < 4:
                        evac_copy(CQ32[t].t[:, 0:n], p.t[:, 0:n], [p], [CQ32[t]], eng="act")
                    else:
                        evac_copy(QF.t[:, t - 4, 0:n], p.t[:, 0:n], [p], [QF])
            if not skip_fk:
                fk_fm(Abuf, groups, n, range(6, 10))
            if STOP == "B1":
                return
            wb, W = wslab(("in_kr", 0))
            p1, p2 = ps(), ps()
            mm_fm(p1, W, wb, 0, 64, A, [Abuf], 16, n)
            mm_fm(p2, W, wb, 64, 64, A, [Abuf], 16, n)
            kr32 = t32()
            rope_fm(p1, p2, n, kr32.t[:64, 0:n], kr32)
            s = tb16()
            evac_copy(s.t[:64, 0:n], kr32.t[:64, 0:n], [kr32], [s])
            store_groups(s, groups, lambda G_: G_["seq"].KrT[:, G_["pos0"]:G_["pos0"] + G_["n"]], part=64)
            for st, (off, sz) in enumerate(subt):
                p = ps()
                op("pe", "transpose", [kr32, ident32], [p], out=p.t[:sz, 0:64], in_=kr32.t[:64, off:off + sz],
                   identity=ident32.t[:64, :64])
                o32 = t32()
                evac_copy(o32.t[:sz, 0:64], p.t[:sz, 0:64], [p], [o32])
                dma("sp", segs[st]["out"]["kr"], o32.t[:sz, 0:64], o32.st, rd=[o32])
            if STOP == "B2":
                return
            for gi in range(2):
                for i in range(8):
                    wb, W = wslab(("in_g", gi * 8 + i))
                    for tt in range(2):
                        m = i * 2 + tt
                        p = ps()
                        mm_fm(p, W, wb, tt * 128, 128, A, [Abuf], 16, n)
                        s = tb16()
                        op("act", "activation", [p], [s], out=s.t[:, 0:n], in_=p.t[:, 0:n], func=AF.Sigmoid)
                        dma("sp", SGD[gi, m, :, 0:n], s.t[:, 0:n], s.st, rd=[s], wr=[SGB[gi][m]])
            if STOP == "B3":
                return
            wbs = [wslab(("in_tm", half)) for half in range(2)]
            for st, (off, sz) in enumerate(subt):
                p = ps()
                for half in range(2):
                    mm_tm(p, half * 256, wbs[half][1], wbs[half][0], 0, 256, At(off, sz), [Abuf], 16, sz)
                junk = t32()
                ss = SMALL.t[:sz, st:st + 1]
                rs = SMALL.t[:sz, 8 + st:9 + st]
                op("act", "activation", [p], [junk, SMALL], out=junk.t[:sz, :], in_=p.t[:sz, :], func=AF.Square,
                   accum_out=ss)
                rstd_from_ss(ss, rs, 512)
                o32 = t32()
                op("dve", "scalar_tensor_tensor", [p, SMALL, gkv_bc], [o32], out=o32.t[:sz, :], in0=p.t[:sz, :],
                   scalar=rs, in1=gkv_bc.t[:sz, :], op0=ALU.mult, op1=ALU.mult)
                dma("sp", segs[st]["out"]["ckv"], o32.t[:sz, :], o32.st, rd=[o32])
                cb = tb16()
                evac_copy(cb.t[:sz, :], o32.t[:sz, :], [o32], [cb], eng="act")
                pt = ptb()
                for kc in range(4):
                    op("pe", "transpose", [cb, identb], [pt], out=pt.t[:, kc * 128:kc * 128 + sz],
                       in_=cb.t[:sz, kc * 128:(kc + 1) * 128], identity=identb.t[:sz, :sz])
                for kc in range(4):
                    evac_copy(CKVT.t[:, kc, off:off + sz], pt.t[:, kc * 128:kc * 128 + sz], [pt], [CKVT], eng="dve")
            if STOP == "B4":
                return
            def fkv_tm(which):
                for cg in range(2):
                    wbs = [wslab(("in_tm", 2 + which * 4 + cg * 2 + half)) for half in range(2)]
                    for st, (off, sz) in enumerate(subt):
                        p = ps()
                        for half in range(2):
                            mm_tm(p, half * 256, wbs[half][1], wbs[half][0], 0, 256, At(off, sz), [Abuf], 16, sz)
                        o32 = t32()
                        rr["ev"] += 1
                        eng_ = "act" if rr["ev"] % 2 else "dve"
                        evac_copy(o32.t[:sz, :], p.t[:sz, :], [p], [o32], eng=eng_)
                        dst = segs[st]["out"]["fk"] if which == 0 else segs[st]["out"]["fv"]
                        dma("sp", dst[:, cg * 512:(cg + 1) * 512], o32.t[:sz, :], o32.st, rd=[o32])
                        if which == 1:
                            vb_ = tb16()
                            evac_copy(vb_.t[:sz, :], p.t[:sz, :], [p], [vb_], eng=eng_)
                            G_ = groups[segs[st]["g"]]
                            r0 = G_["pos0"] + off - G_["col0"]
                            dma("sp", G_["seq"].Vf[r0:r0 + sz, cg * 512:(cg + 1) * 512], vb_.t[:sz, :],
                                vb_.st, rd=[vb_], wr=[G_["seq"].kv])
            if STOP == "B5":
                return
            fkv_tm(1)
            wb, W = wslab(("in_fl", 0))
            for st, (off, sz) in enumerate(subt):
                p = ps()
                mm_tm(p, 0, W, wb, 0, 8, At(off, sz), [Abuf], 16, sz)
                kt = segs[st]["kt"]
                a = t32()
                op("dve", "tensor_tensor", [p, bf_bc], [a], out=a.t[:sz, 0:8], in0=p.t[:sz, 0:8], in1=bf_bc.t[:sz, :],
                   op=ALU.add)
                op("act", "activation", [a], [a], out=a.t[:sz, 8:16], in_=a.t[:sz, 0:8], func=AF.Exp, scale=-1.0)
                op("act", "activation", [a], [a], out=a.t[:sz, 16:24], in_=a.t[:sz, 8:16], func=AF.Ln, bias=1.0)
                op("dve", "tensor_scalar", [a], [LOGF], out=LOGF.t[:sz, kt, :], in0=a.t[:sz, 16:24], scalar1=-1.0,
                   scalar2=None, op0=ALU.mult)
                dma("sp", segs[st]["out"]["lf"], LOGF.t[:sz, kt, :], LOGF.st, rd=[LOGF])
                cum_tile(kt, sz, segs[st]["first_cum"], groups[segs[st]["g"]]["seq"].PREF)
            if STOP == "B6":
                return
            sq = [t32() for _ in range(4)]
            for kc in range(4):
                op("act", "activation", [CQ32[kc]], [sq[kc]], out=sq[kc].t[:, 0:n], in_=CQ32[kc].t[:, 0:n],
                   func=AF.Square)
            pq = ps()
            for kc in range(4):
                op("pe", "matmul", [ones32, sq[kc]], [pq], pq.t[:, 0:n], lhsT=ones32.t[:, :], rhs=sq[kc].t[:, 0:n],
                   start=(kc == 0), stop=(kc == 3))
            rb = t32()
            op("act", "activation", [pq], [rb], out=rb.t[:, 0:n], in_=pq.t[:, 0:n], func=AF.Sqrt, bias=EPS,
               scale=1.0 / 512)
            op("dve", "reciprocal", [rb], [rb], out=rb.t[:, 0:n], in_=rb.t[:, 0:n])
            for kc in range(4):
                op("dve", "scalar_tensor_tensor", [CQ32[kc], gqT, rb], [CQN], out=CQN.t[:, kc, 0:n],
                   in0=CQ32[kc].t[:, 0:n], scalar=gqT.t[:, kc:kc + 1], in1=rb.t[:, 0:n], op0=ALU.mult, op1=ALU.mult)

            def cqn(kc):
                return CQN.t[:, kc, 0:n]
            wb, W = wslab(("uq", 0))
            for h in range(8):
                p = ps()
                mm_fm(p, W, wb, h * 128, 128, cqn, [CQN], 4, n)
                evac_copy(QN.t[:, h, 0:n], p.t[:, 0:n], [p], [QN])
            wb, W = wslab(("uq", 1))
            for h in range(8):
                p1, p2 = ps(), ps()
                mm_fm(p1, W, wb, h * 64, 64, cqn, [CQN], 4, n)
                mm_fm(p2, W, wb, 512 + h * 64, 64, cqn, [CQN], 4, n)
                rope_fm(p1, p2, n, QR.t[:64, h, 0:n], QR)
            if STOP == "C":
                return
            kv_upproj(groups, segs, n)
            if STOP == "D":
                return
            for G_ in groups:
                attention(G_["seq"], G_["pos0"], G_["n"], True, G_["col0"])
            for G_ in groups:
                attention(G_["seq"], G_["pos0"], G_["n"], False, G_["col0"])
            if STOP == "E":
                return
            for q4 in range(4):
                wba, Wa = wslab(("oa", q4))
                wbb, Wb = wslab(("ob", q4))
                for tt in range(4):
                    m = q4 * 4 + tt
                    sga, sgb = tb16(), tb16()
                    dma("sp", sga.t[:, 0:n], SGD[0, m, :, 0:n], sga.ld, rd=[SGB[0][m]], wr=[sga])
                    dma("sp", sgb.t[:, 0:n], SGD[1, m, :, 0:n], sgb.ld, rd=[SGB[1][m]], wr=[sgb])
                    pa, pb = ps(), ps()
                    mm_fm(pa, Wa, wba, tt * 128, 128, lambda kc: OA.t[:, kc, 0:n], [OA], 8, n)
                    mm_fm(pb, Wb, wbb, tt * 128, 128, lambda kc: OB.t[:, kc, 0:n], [OB], 8, n)
                    t1, t2 = t32(), t32()
                    op("dve", "tensor_tensor", [pa, sga], [t1], out=t1.t[:, 0:n], in0=pa.t[:, 0:n], in1=sga.t[:, 0:n],
                       op=ALU.mult)
                    op("dve", "tensor_tensor", [pb, sgb], [t2], out=t2.t[:, 0:n], in0=pb.t[:, 0:n], in1=sgb.t[:, 0:n],
                       op=ALU.mult)
                    op("dve", "tensor_tensor", [t1, t2], [Abuf2], out=Abuf2.t[:, m, 0:n], in0=t1.t[:, 0:n],
                       in1=t2.t[:, 0:n], op=ALU.add)
            if STOP == "F1":
                return
            for st, (off, sz) in enumerate(subt):
                dma("sp", H[st].t[:sz, :], segs[st]["x"], H[st].ld, wr=[H[st]])
            for i in range(4):
                wbs = [wslab(("o", i * 2 + half)) for half in range(2)]
                for st, (off, sz) in enumerate(subt):
                    p = ps()
                    for half in range(2):
                        mm_tm(p, half * 256, wbs[half][1], wbs[half][0], 0, 256,
                              (lambda o_, s_: (lambda kc: Abuf2.t[:, kc, o_:o_ + s_]))(off, sz), [Abuf2], 16, sz)
                    hs = H[st].t[:sz, i * 512:(i + 1) * 512]
                    op("dve", "tensor_tensor", [p, H[st]], [H[st]], out=hs, in0=p.t[:sz, :], in1=hs, op=ALU.add)
            if STOP == "F2":
                return
            fkv_tm(0)
            norm_transpose(Abuf, lambda st, sz: H[st].t[:sz, :], H, subt, gT["gT_ffn"])
            for G_ in groups:
                if not G_["first"]:
                    continue
                CONVST = G_["seq"].CONVST
                if G_["seq"].conv_src is None:
                    op("dve", "memset", [], [CONVST], CONVST.t[:, :, :].rearrange("p a b -> p (a b)"), 0.0)
                else:
                    pc = ps()
                    for q in range(22):
                        dma("sp", ST2.t[0:2, :], G_["seq"].conv_src[:, q * 512:(q + 1) * 512], ST2.ld, wr=[ST2])
                        for jj in range(4):
                            j = q * 4 + jj
                            op("pe", "matmul", [ST2, ident32], [pc], pc.t[:, 2 * j:2 * j + 2],
                               lhsT=ST2.t[0:2, jj * 128:(jj + 1) * 128], rhs=ident32.t[0:2, 0:2], start=True, stop=True)
                    op("act", "activation", [pc], [CONVST], out=CONVST.t[:, :, :].rearrange("p a b -> p (a b)"),
                       in_=pc.t[:, 0:2 * NFT], func=AF.Copy)

            def conv_a(p, jj):
                rr["u"] = (rr["u"] + 1) % 2
                u = U[rr["u"]]
                acc = t32()
                for gi, G_ in enumerate(groups):
                    ub, c0, ng = G_["col0"] + 2 * gi, G_["col0"], G_["n"]
                    CONVST = G_["seq"].CONVST
                    op("dve", "tensor_copy", [CONVST], [u], out=u.t[:, ub:ub + 2], in_=CONVST.t[:, jj, :])
                    op("act", "activation", [p], [u], out=u.t[:, ub + 2:ub + 2 + ng], in_=p.t[:, c0:c0 + ng], func=AF.Copy)
                op("act", "activation", [p, wcT, bcT], [acc], out=acc.t[:, 0:n], in_=p.t[:, 0:n], func=AF.Identity,
                   scale=wcT.t[:, jj, 2:3], bias=bcT.t[:, jj:jj + 1])
                for gi, G_ in enumerate(groups):
                    ub, c0, ng = G_["col0"] + 2 * gi, G_["col0"], G_["n"]
                    op("dve", "scalar_tensor_tensor", [u, wcT, acc], [acc], out=acc.t[:, c0:c0 + ng],
                       in0=u.t[:, ub + 1:ub + 1 + ng], scalar=wcT.t[:, jj, 1:2], in1=acc.t[:, c0:c0 + ng],
                       op0=ALU.mult, op1=ALU.add)
                return u, acc

            def conv_b(u, acc, jj):
                for gi, G_ in enumerate(groups):
                    ub, c0, ng = G_["col0"] + 2 * gi, G_["col0"], G_["n"]
                    CONVST = G_["seq"].CONVST
                    op("dve", "scalar_tensor_tensor", [u, wcT, acc], [acc], out=acc.t[:, c0:c0 + ng],
                       in0=u.t[:, ub:ub + ng], scalar=wcT.t[:, jj, 0:1], in1=acc.t[:, c0:c0 + ng],
                       op0=ALU.mult, op1=ALU.add)
                    op("dve", "tensor_copy", [u], [CONVST], out=CONVST.t[:, jj, :], in_=u.t[:, ub + ng:ub + ng + 2])
                return acc

            def down2(g0):
                sl = [wslab(("dn", g0 + i)) for i in range(2)]
                for cg in range(4):
                    for st, (off, sz) in enumerate(subt):
                        p = ps()
                        for i in range(2):
                            Gb = G[(g0 + i) % 4]
                            for kc in range(2):
                                op("pe", "matmul", [sl[i][0], Gb], [p], p.t[:sz, :], lhsT=Gb.t[:, kc, off:off + sz],
                                   rhs=sl[i][1][:, kc, cg * 512:(cg + 1) * 512], start=(i == 0 and kc == 0),
                                   stop=(i == 1 and kc == 1))
                        hs = H[st].t[:sz, cg * 512:(cg + 1) * 512]
                        op("dve", "tensor_tensor", [p, H[st]], [H[st]], out=hs, in0=p.t[:sz, :], in1=hs, op=ALU.add)

            for g in range(22):
                wbv, Wv = wslab(("upv", g))
                wbg, Wg = wslab(("upg", g))
                Gb = G[g % 4]
                for tt in range(2):
                    j = g * 2 + tt
                    pv, pg = ps(), ps()
                    mm_fm(pv, Wv, wbv, tt * 128, 128, A, [Abuf], 16, n)
                    mm_fm(pg, Wg, wbg, tt * 128, 128, A, [Abuf], 16, n)
                    uv, vc = conv_a(pv, j)
                    ug, gc = conv_a(pg, 44 + j)
                    conv_b(uv, vc, j)
                    conv_b(ug, gc, 44 + j)
                    op("act", "activation", [gc], [gc], out=gc.t[:, 0:n], in_=gc.t[:, 0:n], func=AF.Gelu_apprx_tanh)
                    op("dve", "tensor_tensor", [gc, vc], [Gb], out=Gb.t[:, tt, 0:n], in0=gc.t[:, 0:n],
                       in1=vc.t[:, 0:n], op=ALU.mult)
                if g >= 3 and g % 2 == 1:
                    down2(g - 3)
            if next_a is not None:
                next_a(Abuf2)
            down2(20)
            for G_ in groups:
                conv_out = G_["conv_out"]
                if conv_out is None:
                    continue
                CONVST = G_["seq"].CONVST
                for q in range(22):
                    pc = ps()
                    for jj in range(4):
                        j = q * 4 + jj
                        op("pe", "matmul", [CONVST, ident32], [pc], pc.t[0:2, jj * 128:(jj + 1) * 128],
                           lhsT=CONVST.t[:, j, :], rhs=ident32.t[:, :], start=True, stop=True)
                    op("act", "activation", [pc], [ST2], out=ST2.t[0:2, :], in_=pc.t[0:2, :], func=AF.Copy)
                    dma("sp", conv_out[:, q * 512:(q + 1) * 512], ST2.t[0:2, :], ST2.st, rd=[ST2])
            if STOP == "F3":
                return
            if next_b is not None:
                next_b(Abuf2)
            norm_transpose(Abuf, lambda st, sz: H[st].t[:sz, :], H, subt, gT["gT_ple"])
            for st, (off, sz) in enumerate(subt):
                pe32 = t32()
                dma("sp", pe32.t[:sz, 0:256], segs[st]["pe"], pe32.ld, wr=[pe32])
                peb = tb16()
                evac_copy(peb.t[:sz, 0:256], pe32.t[:sz, 0:256], [pe32], [peb])
                pt = ptb()
                for kc in range(2):
                    op("pe", "transpose", [peb, identb], [pt], out=pt.t[:, kc * 128:kc * 128 + sz],
                       in_=peb.t[:sz, kc * 128:(kc + 1) * 128], identity=identb.t[:sz, :sz])
                for kc in range(2):
                    evac_copy(PET.t[:, kc, off:off + sz], pt.t[:, kc * 128:kc * 128 + sz], [pt], [PET], eng="dve")
            for i in range(4):
                wbp, Wp = wslab(("ple", 0))
                wbs = [wslab(("pg", i * 2 + half)) for half in range(2)]
                for st, (off, sz) in enumerate(subt):
                    p1 = ps()
                    for half in range(2):
                        mm_tm(p1, half * 256, wbs[half][1], wbs[half][0], 0, 256, At(off, sz), [Abuf], 16, sz)
                    sg = t32()
                    op("act", "activation", [p1], [sg], out=sg.t[:sz, :], in_=p1.t[:sz, :], func=AF.Sigmoid)
                    p2 = ps()
                    mm_tm(p2, 0, Wp, wbp, i * 512, 512, (lambda o_, s_: (lambda kc: PET.t[:, kc, o_:o_ + s_]))(off, sz),
                          [PET], 2, sz)
                    op("dve", "tensor_tensor", [p2, sg], [sg], out=sg.t[:sz, :], in0=p2.t[:sz, :], in1=sg.t[:sz, :],
                       op=ALU.mult)
                    hs = H[st].t[:sz, i * 512:(i + 1) * 512]
                    op("dve", "tensor_tensor", [sg, H[st]], [H[st]], out=hs, in0=sg.t[:sz, :], in1=hs, op=ALU.add)
            if STOP == "F5":
                return
            for st, (off, sz) in enumerate(subt):
                ss = SMALL.t[:sz, st:st + 1]
                rs = SMALL.t[:sz, 8 + st:9 + st]
                op("act", "activation", [H[st]], [XN, SMALL], out=XN.t[:sz, :], in_=H[st].t[:sz, :], func=AF.Square,
                   accum_out=ss)
                rstd_from_ss(ss, rs, D)
                for i in range(4):
                    gq = t32()
                    dma("sp", gq.t[:, :], I["gfin_bc"][:, i * 512:(i + 1) * 512], gq.ld, wr=[gq])
                    hs = H[st].t[:sz, i * 512:(i + 1) * 512]
                    op("dve", "scalar_tensor_tensor", [H[st], SMALL, gq], [H[st]], out=hs, in0=hs, scalar=rs,
                       in1=gq.t[:sz, :], op0=ALU.mult, op1=ALU.mult)
                dma("sp", segs[st]["out"]["y"], H[st].t[:sz, :], H[st].st, rd=[H[st]])

        OK_ = ("y", "ckv", "kr", "fk", "fv", "lf")

        def prompt_segs(b):
            return [dict(off=o, sz=128, g=0, x=I["xp"][b * 512 + o:b * 512 + o + 128, :],
                         pe=I["pp"][b * 512 + o:b * 512 + o + 128, :],
                         out={k: O[k + "_p"][b * 512 + o:b * 512 + o + 128, :] for k in OK_},
                         kt=(b * 512 + o) // 128, first_cum=(b == 0 and o == 0)) for o in range(0, 512, 128)]

        if TP > 0 and STOP != "conv":
            sp_ = make_seq("p", CAPP)
            sp_.conv_src = None
            sp_.ktbase = 0
            sp_.PREF = PREFS[0]
            sp_.CONVST = CONVSTS[0]
            nb = TP // 512
            for b in range(nb):
                nxt = nxtb = None
                if b + 1 < nb:
                    nxt = (lambda b1: (lambda Adst: stage_a(Adst, prompt_segs(b1))))(b + 1)
                    nxtb = (lambda b1: (lambda Asrc: fk_fm(
                        Asrc, [dict(seq=sp_, pos0=b1 * 512, col0=0, n=512)], 512, range(6, 10))))(b + 1)
                groups = [dict(seq=sp_, pos0=b * 512, col0=0, n=512, first=(b == 0),
                               conv_out=(O["conv_p"] if b == nb - 1 else None))]
                block(b, groups, prompt_segs(b), 512, I["ropeC_p"][:, b * 512:(b + 1) * 512],
                      I["ropeS_p"][:, b * 512:(b + 1) * 512], a_done=(b > 0), next_a=nxt, skip_fk=(b > 0),
                      next_b=nxtb)

        NKTS = CAPS // 128
        sseqs = []
        for s in range(NSS if STOP not in ("conv", "prompt") else 0):
            sq_ = make_seq("s%d" % s, CAPS)
            sq_.conv_src = I["c_conv"][s]
            sq_.ktbase = s * NKTS
            sq_.PREF = PREFS[s % 2]
            sq_.CONVST = CONVSTS[s % 2]
            sseqs.append(sq_)
            vown = P.owner()
            for c in range(PAST // 512):
                t0 = c * 512
                pf_groups = [dict(seq=sq_, pos0=t0, col0=0, n=512)]
                pf_segs = [dict(off=o, sz=128, g=0) for o in range(0, 512, 128)]
                dma("pool", CB.t[:, 0:2048].rearrange("p (a b) -> p a b", a=4),
                    I["c_ckv"][s, t0:t0 + 512, :].rearrange("(a p) d -> p a d", p=128), CB.ld, wr=[CB])
                for st in range(4):
                    pt = ptb()
                    for kc in range(4):
                        op("pe", "transpose", [CB, identb], [pt], out=pt.t[:, kc * 128:(kc + 1) * 128],
                           in_=CB.t[:, st * 512 + kc * 128:st * 512 + (kc + 1) * 128], identity=identb.t[:, :])
                    for kc in range(4):
                        evac_copy(CKVT.t[:, kc, st * 128:(st + 1) * 128], pt.t[:, kc * 128:(kc + 1) * 128], [pt], [CKVT],
                                  eng=("act" if st % 2 else "dve"))
                kv_upproj(pf_groups, pf_segs, 512)
                dma("pool", CB.t[:, 0:256].rearrange("p (a b) -> p a b", a=4),
                    I["c_kr"][s, t0:t0 + 512, :].rearrange("(a p) d -> p a d", p=128), CB.ld, wr=[CB])
                pt = ptb()
                for st in range(4):
                    op("pe", "transpose", [CB, identb], [pt], out=pt.t[:64, st * 128:(st + 1) * 128],
                       in_=CB.t[:, st * 64:(st + 1) * 64], identity=identb.t[:, :])
                sg_ = tb16()
                evac_copy(sg_.t[:64, :], pt.t[:64, 0:512], [pt], [sg_])
                dma("sp", sq_.KrT[:, t0:t0 + 512], sg_.t[:64, :], sg_.st, rd=[sg_], wr=[sq_.kv])
                dma("pool", CB.t[:, :].rearrange("p (a b) -> p a b", a=4),
                    I["c_fv"][s, t0:t0 + 512, :].rearrange("(a p) d -> p a d", p=128), CB.ld, wr=[CB])
                dma("sp", sq_.Vf[t0:t0 + 512, :].rearrange("(a p) d -> p a d", p=128),
                    CB.t[:, :].rearrange("p (a b) -> p a b", a=4), vown, rd=[CB], wr=[sq_.kv])
                dma("pool", CB.t[:, :].rearrange("p (a b) -> p a b", a=4),
                    I["c_fk"][s, t0:t0 + 512, :].rearrange("(a p) d -> p a d", p=128), CB.ld, wr=[CB])
                for h in range(8):
                    pt = ptb()
                    for st in range(4):
                        op("pe", "transpose", [CB, identb], [pt], out=pt.t[:, st * 128:(st + 1) * 128],
                           in_=CB.t[:, st * 1024 + h * 128:st * 1024 + (h + 1) * 128], identity=identb.t[:, :])
                    sg_ = tb16()
                    evac_copy(sg_.t[:, :], pt.t[:, 0:512], [pt], [sg_])
                    dma("sp", sq_.KfT[h, :, t0:t0 + 512], sg_.t[:, :], sg_.st, rd=[sg_], wr=[sq_.kv])
            kb_ = sq_.ktbase
            dma("sp", LOGF.t[:, kb_:kb_ + PAST // 128, :], I["c_lf"][s].rearrange("(a p) h -> p a h", p=128), LOGF.ld,
                wr=[LOGF])
            for kt in range(PAST // 128):
                cum_tile(kb_ + kt, 128, kt == 0, sq_.PREF)
        for s0 in range(0, len(sseqs), 2):
            grp = sseqs[s0:s0 + 2]
            ng = len(grp)
            groups = [dict(seq=q_, pos0=PAST, col0=j * TS, n=TS, first=True, conv_out=O["conv_s"][s0 + j])
                      for j, q_ in enumerate(grp)]
            segs = [dict(off=j * TS, sz=TS, g=j, x=I["xs"][s0 + j], pe=I["ps"][s0 + j],
                         out={k: O[k + "_s"][s0 + j] for k in OK_},
                         kt=q_.ktbase + PAST // 128, first_cum=False) for j, q_ in enumerate(grp)]
            block(s0 // 2, groups, segs, ng * TS, I["ropeC_s"][:, s0 * TS:(s0 + ng) * TS],
                  I["ropeS_s"][:, s0 * TS:(s0 + ng) * TS])

        P.finish()
    return nc


def _rope_tables(pos):
    half = 32
    inv = (np.float32(10000.0) ** (-np.arange(half, dtype=np.float32) / np.float32(half))).astype(np.float32)
    ang = (pos.astype(np.float32)[:, None] * inv[None, :]).astype(np.float32)
    cos, sin = np.cos(ang).astype(np.float32), np.sin(ang).astype(np.float32)
    C = np.concatenate([cos, cos], axis=1).T
    S = np.concatenate([-sin, sin], axis=1).T
    return np.ascontiguousarray(C), np.ascontiguousarray(S)


_CACHE = {}


def run(inputs, n_cores, TP, PAST, NSS):
    f32 = lambda a: np.ascontiguousarray(np.asarray(a, dtype=np.float32))
    x_prompt, x_sample = f32(inputs["x_prompt"]), f32(inputs["x_sample"])
    key = (TP, PAST, NSS)
    if key not in _CACHE:
        _CACHE[key] = build(TP, PAST, NSS)
    nc = _CACHE[key]
    w_in = f32(inputs["w_in"])[0]
    com = {}
    com["w_in_p"] = np.ascontiguousarray(w_in[:, win_perm_cols()])
    w_uq = f32(inputs["w_uq"])[0].reshape(512, 8, 192)
    nope = w_uq[:, :, :128].reshape(512, 1024)
    rp = w_uq[:, :, 128:]
    rp_sw = np.concatenate([rp[:, :, 32:], rp[:, :, :32]], axis=2)
    com["w_uq_p"] = np.ascontiguousarray(np.concatenate([nope, rp.reshape(512, 512), rp_sw.reshape(512, 512)], axis=1))
    w_ukv = f32(inputs["w_ukv"])[0].reshape(512, 8, 256)
    com["w_ukv_p"] = np.ascontiguousarray(
        np.concatenate([w_ukv[:, :, :128].reshape(512, 1024), w_ukv[:, :, 128:].reshape(512, 1024)], axis=1))
    for k in ("w_oa", "w_ob", "w_o", "w_up", "w_down", "w_pg", "w_ple"):
        com[k] = f32(inputs[k])[0]
    for k, src in (("gT_mix", "g_mix"), ("gT_ffn", "g_ffn"), ("gT_ple", "g_ple")):
        com[k] = np.ascontiguousarray(f32(inputs[src])[0].reshape(16, 128).T)
    com["gqT"] = np.ascontiguousarray(f32(inputs["g_q"])[0].reshape(4, 128).T)
    com["gkv_bc"] = np.ascontiguousarray(np.broadcast_to(f32(inputs["g_kv"])[0][None, :], (128, 512)))
    com["gfin_bc"] = np.ascontiguousarray(np.broadcast_to(f32(inputs["g_final"])[None, :], (128, D)))
    com["bf_bc"] = np.ascontiguousarray(np.broadcast_to(f32(inputs["b_f"])[0][None, :], (128, 8)))
    wc = f32(inputs["w_conv"])[0]
    com["wcT"] = np.ascontiguousarray(wc.reshape(3, NFT, 128).transpose(2, 1, 0).reshape(128, NFT * 3))
    com["bcT"] = np.ascontiguousarray(f32(inputs["b_conv"])[0].reshape(NFT, 128).T)
    com["ident"] = np.eye(128, dtype=np.float32)
    com["tri"] = np.triu(np.ones((128, 128), dtype=np.float32))
    C, S = _rope_tables(np.arange(TP))
    com["ropeC_p"], com["ropeS_p"] = C, S
    C, S = _rope_tables(PAST + np.arange(TS))
    com["ropeC_s"] = np.ascontiguousarray(np.tile(C, (1, NSS)))
    com["ropeS_s"] = np.ascontiguousarray(np.tile(S, (1, NSS)))
    pp = f32(inputs["p_prompt"])[0]
    psm = f32(inputs["p_sample"])[0]
    c_ckv = f32(inputs["cache_mla_ckv"])[0]
    c_kr = f32(inputs["cache_mla_krope"])[0]
    c_fk = f32(inputs["cache_fox_k"])[0].reshape(-1, PAST, 1024)
    c_fv = f32(inputs["cache_fox_v"])[0].reshape(-1, PAST, 1024)
    c_lf = f32(inputs["cache_fox_logf"])[0]
    c_conv = f32(inputs["state_ffn_conv"])[0]
    in_maps = []
    for i in range(n_cores):
        m = dict(com)
        m["xp"] = x_prompt[i]
        m["pp"] = pp[i]
        sl = slice(i * NSS, (i + 1) * NSS)
        m["xs"] = x_sample[sl]
        m["ps"] = psm[sl]
        m["c_ckv"], m["c_kr"], m["c_fk"], m["c_fv"] = c_ckv[sl], c_kr[sl], c_fk[sl], c_fv[sl]
        m["c_lf"], m["c_conv"] = c_lf[sl], c_conv[sl]
        in_maps.append(m)
    res = run_bass_kernel_spmd(nc, in_maps, core_ids=list(range(n_cores)))
    R = res.results
    cat = lambda k: np.concatenate([np.asarray(r[k], dtype=np.float32)[None] for r in R], axis=0)
    cats = lambda k: np.concatenate([np.asarray(r[k], dtype=np.float32) for r in R], axis=0)
    B = n_cores
    return (cat("y_p"), cats("y_s"),
            cat("ckv_p")[None], cats("ckv_s")[None],
            cat("kr_p")[None], cats("kr_s")[None],
            cat("fk_p").reshape(1, B, TP, 8, 128), cats("fk_s").reshape(1, B * NSS, TS, 8, 128),
            cat("fv_p").reshape(1, B, TP, 8, 128), cats("fv_s").reshape(1, B * NSS, TS, 8, 128),
            cat("lf_p")[None], cats("lf_s")[None],
            cat("conv_p")[None], cats("conv_s")[None])


def kernel(**inputs):
    return run(inputs, 8, 4096, 2048, 2)
```
